# Optimizing a Trainium2 kernel written in Bass

```python
import math
import jax, jax.numpy as jnp
from jax import lax
import numpy as np

D_MODEL = 1024
BATCH = 4
SEQ = 8192
DEPTH = 2

HEAD_DIM = 64
DIFF_HEADS = D_MODEL // (4 * HEAD_DIM)
DIFF_WIDTH = DIFF_HEADS * 2 * HEAD_DIM
SB_WIDTH = D_MODEL - DIFF_WIDTH
SB_HEADS = SB_WIDTH // HEAD_DIM
D_FF = ((8 * D_MODEL // 3 + 255) // 256) * 256
CONV_WIDTH = 3
BLOCK_Q = 128
ROPE_THETA = 10000.0
LN_EPS = 1e-5
RMS_EPS = 1e-6
LAMBDA_STD = 0.1
DEEPNORM_ALPHA = (2 * DEPTH) ** 0.25
DEEPNORM_BETA = (8 * DEPTH) ** -0.25

kernel_name = "hymba_diff_stickbreak_deepnorm"


def layer_norm(x, g, b):
    x32 = x.astype(jnp.float32)
    mu = jnp.mean(x32, axis=-1, keepdims=True)
    xc = x32 - mu
    var = jnp.mean(xc * xc, axis=-1, keepdims=True)
    y = xc * lax.rsqrt(var + LN_EPS) * g.astype(jnp.float32) + b.astype(jnp.float32)
    return y.astype(x.dtype)


def rms_norm_f32(x32, g):
    return x32 * lax.rsqrt(jnp.mean(x32 * x32, axis=-1, keepdims=True) + RMS_EPS) * g.astype(jnp.float32)


def rope_tables(seq):
    inv = 1.0 / (ROPE_THETA ** (jnp.arange(0, HEAD_DIM, 2, dtype=jnp.float32) / HEAD_DIM))
    ang = jnp.arange(seq, dtype=jnp.float32)[:, None] * inv[None, :]
    return jnp.cos(ang), jnp.sin(ang)


def apply_rope(x, cos, sin):
    x32 = x.astype(jnp.float32)
    half = HEAD_DIM // 2
    x1, x2 = x32[..., :half], x32[..., half:]
    return jnp.concatenate([x1 * cos - x2 * sin, x2 * cos + x1 * sin], axis=-1).astype(x.dtype)


def diff_attention(q, k, v, lam, lam_init, norm_g):
    seq = q.shape[3]
    scale = HEAD_DIM ** -0.5
    outs = []
    for i in range(seq // BLOCK_Q):
        lo, hi = i * BLOCK_Q, (i + 1) * BLOCK_Q
        s = jnp.einsum('bhcqd,bhckd->bhcqk', q[:, :, :, lo:hi], k[:, :, :, :hi]).astype(jnp.float32) * scale
        causal = jnp.arange(hi)[None, :] <= jnp.arange(lo, hi)[:, None]
        p = jax.nn.softmax(jnp.where(causal, s, -jnp.inf), axis=-1)
        w = p[:, :, 0] - lam * p[:, :, 1]
        outs.append(jnp.einsum('bhqk,bhke->bhqe', w.astype(v.dtype), v[:, :, :hi]))
    o = jnp.concatenate(outs, axis=2).astype(jnp.float32)
    o = rms_norm_f32(o, norm_g) * (1.0 - lam_init)
    return o.astype(v.dtype)


def stick_breaking_attention(q, k, v, norm_g):
    seq = q.shape[2]
    scale = HEAD_DIM ** -0.5
    outs = []
    for i in range(seq // BLOCK_Q):
        lo, hi = i * BLOCK_Q, (i + 1) * BLOCK_Q
        z = jnp.einsum('bhqd,bhkd->bhqk', q[:, :, lo:hi], k[:, :, :hi]).astype(jnp.float32) * scale
        strict = jnp.arange(hi)[None, :] < jnp.arange(lo, hi)[:, None]
        log_1m_beta = jnp.where(strict, jax.nn.log_sigmoid(-z), 0.0)
        later = lax.cumsum(log_1m_beta, axis=log_1m_beta.ndim - 1, reverse=True) - log_1m_beta
        a = jnp.where(strict, jnp.exp(jax.nn.log_sigmoid(z) + later), 0.0)
        outs.append(jnp.einsum('bhqk,bhkd->bhqd', a.astype(v.dtype), v[:, :, :hi]))
    o = jnp.concatenate(outs, axis=2).astype(jnp.float32)
    return rms_norm_f32(o, norm_g).astype(v.dtype)


def causal_depthwise_conv(h, w, b):
    seq = h.shape[1]
    hp = jnp.pad(h, ((0, 0), (CONV_WIDTH - 1, 0), (0, 0)))
    out = b + hp[:, 0:seq] * w[0]
    for j in range(1, CONV_WIDTH):
        out = out + hp[:, j:j + seq] * w[j]
    return out


def setup_inputs(seed: int = 0) -> dict:
    key = jax.random.key(seed)
    ks = jax.random.split(key, 20)
    f32 = jnp.float32
    d, f = D_MODEL, D_FF
    nrm = lambda k, shape, s: jax.random.normal(k, shape, f32) * s
    return {
        "x": nrm(ks[0], (BATCH, SEQ, d), 1.0),
        "w_in": nrm(ks[1], (DEPTH, d, 3 * d), d ** -0.5),
        "w_out": nrm(ks[2], (DEPTH, d, d), d ** -0.5 * DEEPNORM_BETA),
        "lam_q1": nrm(ks[3], (DEPTH, HEAD_DIM), LAMBDA_STD),
        "lam_k1": nrm(ks[4], (DEPTH, HEAD_DIM), LAMBDA_STD),
        "lam_q2": nrm(ks[5], (DEPTH, HEAD_DIM), LAMBDA_STD),
        "lam_k2": nrm(ks[6], (DEPTH, HEAD_DIM), LAMBDA_STD),
        "diff_norm_g": 1.0 + nrm(ks[7], (DEPTH, 2 * HEAD_DIM), 0.02),
        "sb_norm_g": 1.0 + nrm(ks[8], (DEPTH, HEAD_DIM), 0.02),
        "ln1_g": 1.0 + nrm(ks[9], (DEPTH, d), 0.02),
        "ln1_b": nrm(ks[10], (DEPTH, d), 0.02),
        "w_up": nrm(ks[11], (DEPTH, d, 2 * f), d ** -0.5 * DEEPNORM_BETA),
        "conv_w": nrm(ks[12], (DEPTH, CONV_WIDTH, 2 * f), CONV_WIDTH ** -0.5),
        "conv_b": nrm(ks[13], (DEPTH, 2 * f), 0.01),
        "w_down": nrm(ks[14], (DEPTH, f, d), f ** -0.5 * DEEPNORM_BETA),
        "ln2_g": 1.0 + nrm(ks[15], (DEPTH, d), 0.02),
        "ln2_b": nrm(ks[16], (DEPTH, d), 0.02),
    }


def reference(x, w_in, w_out, lam_q1, lam_k1, lam_q2, lam_k2, diff_norm_g, sb_norm_g,
              ln1_g, ln1_b, w_up, conv_w, conv_b, w_down, ln2_g, ln2_b):
    bsz, seq, _ = x.shape
    cos, sin = rope_tables(seq)
    o1 = DIFF_WIDTH
    o2 = 2 * DIFF_WIDTH
    o3 = 3 * DIFF_WIDTH
    o4 = o3 + SB_WIDTH
    o5 = o4 + SB_WIDTH
    for l in range(DEPTH):
        lam_init = 0.8 - 0.6 * math.exp(-0.3 * l)
        lam = (jnp.exp(jnp.sum(lam_q1[l].astype(jnp.float32) * lam_k1[l].astype(jnp.float32)))
               - jnp.exp(jnp.sum(lam_q2[l].astype(jnp.float32) * lam_k2[l].astype(jnp.float32)))
               + lam_init)
        h = x @ w_in[l]
        dq, dk, dv = h[..., :o1], h[..., o1:o2], h[..., o2:o3]
        sq, sk, sv = h[..., o3:o4], h[..., o4:o5], h[..., o5:]
        dq = apply_rope(dq.reshape(bsz, seq, DIFF_HEADS, 2, HEAD_DIM).transpose(0, 2, 3, 1, 4), cos, sin)
        dk = apply_rope(dk.reshape(bsz, seq, DIFF_HEADS, 2, HEAD_DIM).transpose(0, 2, 3, 1, 4), cos, sin)
        dv = dv.reshape(bsz, seq, DIFF_HEADS, 2 * HEAD_DIM).transpose(0, 2, 1, 3)
        sq = sq.reshape(bsz, seq, SB_HEADS, HEAD_DIM).transpose(0, 2, 1, 3)
        sk = sk.reshape(bsz, seq, SB_HEADS, HEAD_DIM).transpose(0, 2, 1, 3)
        sv = sv.reshape(bsz, seq, SB_HEADS, HEAD_DIM).transpose(0, 2, 1, 3)
        d_out = diff_attention(dq, dk, dv, lam, lam_init, diff_norm_g[l])
        s_out = stick_breaking_attention(sq, sk, sv, sb_norm_g[l])
        mixed = jnp.concatenate([
            d_out.transpose(0, 2, 1, 3).reshape(bsz, seq, DIFF_WIDTH),
            s_out.transpose(0, 2, 1, 3).reshape(bsz, seq, SB_WIDTH)], axis=-1)
        x = layer_norm(DEEPNORM_ALPHA * x + mixed @ w_out[l], ln1_g[l], ln1_b[l])
        u = causal_depthwise_conv(x @ w_up[l], conv_w[l], conv_b[l])
        g = jax.nn.silu(u[..., :D_FF]) * u[..., D_FF:]
        x = layer_norm(DEEPNORM_ALPHA * x + g @ w_down[l], ln2_g[l], ln2_b[l])
    return x
```

```python
import math
from contextlib import ExitStack

import numpy as np
import concourse.bass as bass
import concourse.mybir as mybir
from concourse.bass_utils import run_bass_kernel_spmd

F32 = mybir.dt.float32
BF16 = mybir.dt.bfloat16
F16 = mybir.dt.float16
AF = mybir.ActivationFunctionType
ALU = mybir.AluOpType
AX = mybir.AxisListType

D = 1024
HD = 64
DFF = 2816
NCH = D // 128
NFF = DFF // 128
DEPTH = 2
ALPHA = (2 * DEPTH) ** 0.25
LN_EPS = 1e-5
RMS_EPS = 1e-6
NEG = -30000.0
SAFE_SAME_ENGINE = True
ENG = ['pe', 'act', 'dve', 'pool', 'sp']
NDS = 8


class Buf:
    __slots__ = ("t", "wr", "rd")

    def __init__(self, t):
        self.t = t
        self.wr = None
        self.rd = {}


class Prog:
    def __init__(self, nc, stack):
        self.nc = nc
        self.stack = stack
        self.ops = {e: [] for e in ENG}
        self.cur = {}
        self.cnt = {}
        self.seen = {e: {} for e in ENG}
        self.pending = {e: [] for e in ENG}
        self.nsem = 0
        self.dsem = {}
        self.dcnt = {}
        self.dn = {}
        for e in ENG:
            self._newsem(e)
        for e in ('sp', 'pool', 'act'):
            self.dsem[e] = [self._mk(f"d_{e}{i}") for i in range(NDS)]
            self.dcnt[e] = [0] * NDS
            self.dn[e] = 0
        self.nops = 0

    def _mk(self, name):
        self.nsem += 1
        return self.stack.enter_context(self.nc.semaphore(name))

    def _newsem(self, e):
        self.cur[e] = self._mk(f"s_{e}_{self.nsem}")
        self.cnt[e] = 0

    def op(self, e, fn, rd=(), wr=(), waits=(), dma=False):
        evs = [w for w in waits if w is not None]
        for b in rd:
            if b.wr is not None:
                evs.append(b.wr)
        for b in wr:
            if b.wr is not None:
                evs.append(b.wr)
            evs.extend(b.rd.values())
        if self.pending[e]:
            evs.extend(self.pending[e])
            self.pending[e] = []
        if dma:
            r = self.dn[e] % NDS
            self.dn[e] += 1
            sem = self.dsem[e][r]
            if self.dcnt[e][r] > 0:
                evs.append((sem, self.dcnt[e][r], e, True))
            self.dcnt[e][r] += 16
            val = self.dcnt[e][r]
            inc = 16
        else:
            if self.cnt[e] > 30000:
                self._newsem(e)
            sem = self.cur[e]
            self.cnt[e] += 1
            val = self.cnt[e]
            inc = 1
        wl = []
        seen = self.seen[e]
        for (ws, wv, we, wd) in evs:
            if we == e and not wd and (e == 'pe' or not SAFE_SAME_ENGINE):
                continue
            k = id(ws)
            if seen.get(k, -1) >= wv:
                continue
            seen[k] = wv
            wl.append((ws, wv))

        def run(eng, fn=fn, wl=wl, sem=sem, inc=inc):
            for ws, wv in wl:
                eng.wait_ge(ws, wv)
            fn(eng).then_inc(sem, inc)
        self.ops[e].append(run)
        self.nops += 1
        ev = (sem, val, e, dma)
        for b in rd:
            k = id(sem)
            o = b.rd.get(k)
            if o is None or o[1] < val:
                b.rd[k] = ev
        for b in wr:
            b.wr = ev
            b.rd = {}
        return ev

    def last_events(self):
        evs = []
        for e in ENG:
            if self.cnt[e] > 0:
                evs.append((self.cur[e], self.cnt[e], e, False))
        for e in self.dsem:
            for r in range(NDS):
                if self.dcnt[e][r] > 0:
                    evs.append((self.dsem[e][r], self.dcnt[e][r], e, True))
        return evs

    def barrier(self):
        evs = self.last_events()
        for e in ENG:
            self.pending[e] = list(evs)

    def emit(self):
        nc = self.nc
        final = self.last_events()
        with nc.Block() as block:
            @block.tensor
            def _(eng):
                for f in self.ops['pe']:
                    f(eng)

            @block.scalar
            def _(eng):
                for f in self.ops['act']:
                    f(eng)

            @block.vector
            def _(eng):
                for f in self.ops['dve']:
                    f(eng)

            @block.gpsimd
            def _(eng):
                for f in self.ops['pool']:
                    f(eng)

            @block.sync
            def _(eng):
                for f in self.ops['sp']:
                    f(eng)
                for (ws, wv, we, wd) in final:
                    eng.wait_ge(ws, wv)


def build(S, depth=DEPTH, nlayers_run=None, debug=False, skip=()):
    NT = S // 512
    NKB = S // 128
    VCH = min(16, NKB)
    nc = bass.Bass("TRN2", target_bir_lowering=False)
    dt_in = lambda name, shape, dt=F32: nc.dram_tensor(name, shape, dt, kind="ExternalInput").ap()
    xT = dt_in("xT", [D, S])
    w_in = dt_in("w_in", [depth, D, 4096])
    w_out = dt_in("w_out", [depth, D, D])
    w_up = dt_in("w_up", [depth, D, 2 * DFF])
    w_down = dt_in("w_down", [depth, DFF, D])
    lamv = dt_in("lamv", [depth, 128, 4 * HD])
    dgn = dt_in("dgn", [128, depth])
    sgn = dt_in("sgn", [128, depth])
    lnp = dt_in("lnp", [128, depth * 4 * NCH])
    cvw = dt_in("cvw", [128, depth * 3 * 2 * NFF])
    cvb = dt_in("cvb", [128, depth * 2 * NFF])
    cos2 = dt_in("cos2", [128, S])
    sin2 = dt_in("sin2", [128, S])
    outT = nc.dram_tensor("outT", [D, S], F32, kind="ExternalOutput").ap()
    sk = "ExternalOutput" if debug else "Internal"
    qk_s = nc.dram_tensor("qk_s", [2048, S], BF16, kind=sk).ap()
    v_s = nc.dram_tensor("v_s", [S, 1024], BF16, kind=sk).ap()
    mix_s = nc.dram_tensor("mix_s", [D, S], BF16, kind=sk).ap()
    x1_s = nc.dram_tensor("x1_s", [D, S], F32, kind=sk).ap()
    x_s = nc.dram_tensor("x_s", [D, S], F32, kind=sk).ap()

    with ExitStack() as st:
        P = Prog(nc, st)

        uid = [0]

        def sbt(stack, name, shape, dt):
            uid[0] += 1
            return Buf(stack.enter_context(nc.sbuf_tensor(f"{name}_{uid[0]}", shape, dt)))

        banks = [Buf(st.enter_context(nc.psum_tensor(f"bank{i}", [128, 512], F32))) for i in range(8)]
        ident = sbt(st, "ident", [128, 128], BF16)
        maskD = sbt(st, "maskD", [128, 128], BF16)
        maskS = sbt(st, "maskS", [128, 128], BF16)
        uinc = sbt(st, "uinc", [128, 128], F16)
        lstr = sbt(st, "lstr", [128, 128], F16)
        ones_b = sbt(st, "ones_b", [128, 128], BF16)
        om128 = sbt(st, "om128", [128, 128], F32)
        om64 = sbt(st, "om64", [128, 128], F32)
        om1024 = sbt(st, "om1024", [128, 128], F32)
        cst = sbt(st, "cst", [128, 128], F32)
        lam_t = sbt(st, "lam_t", [128, depth * 4 * HD], F32)
        lam_w = sbt(st, "lam_w", [128, 8], F32)
        nlam = sbt(st, "nlam", [128, depth], F32)
        dgn_t = sbt(st, "dgn_t", [128, depth], F32)
        sgn_t = sbt(st, "sgn_t", [128, depth], F32)
        lnp_t = sbt(st, "lnp_t", [128, depth * 4 * NCH], F32)
        cvw_t = sbt(st, "cvw_t", [128, depth * 6 * NFF], F32)
        cvb_t = sbt(st, "cvb_t", [128, depth * 2 * NFF], F32)

        def aff(buf, val, pattern, cmp, fill, base, cm):
            P.op('pool', lambda e: e.memset(buf.t[:], val), wr=[buf])
            P.op('pool', lambda e: e.affine_select(out=buf.t[:], in_=buf.t[:], pattern=pattern, compare_op=cmp,
                                                    fill=fill, base=base, channel_multiplier=cm), wr=[buf])

        def cvt(dst, src):
            P.op('dve', lambda e: e.tensor_copy(out=dst.t[:], in_=src.t[:]), rd=[src], wr=[dst])

        aff(cst, 1.0, [[-1, 128]], ALU.is_equal, 0.0, 0, 1)
        cvt(ident, cst)
        aff(cst, 0.0, [[1, 128]], ALU.is_ge, NEG, 0, -1)
        cvt(maskD, cst)
        aff(cst, 0.0, [[1, 128]], ALU.is_ge, NEG, -1, -1)
        cvt(maskS, cst)
        aff(cst, 1.0, [[-1, 128]], ALU.is_ge, 0.0, 0, 1)
        cvt(uinc, cst)
        aff(cst, 1.0, [[1, 128]], ALU.is_ge, 0.0, -1, -1)
        cvt(lstr, cst)
        P.op('pool', lambda e: e.memset(cst.t[:], 1.0), wr=[cst])
        cvt(ones_b, cst)
        P.op('pool', lambda e: e.memset(om128.t[:], 1.0 / 128), wr=[om128])
        P.op('pool', lambda e: e.memset(om1024.t[:], 1.0 / 1024), wr=[om1024])
        P.op('pool', lambda e: e.memset(om64.t[:], 0.0), wr=[om64])
        P.op('pool', lambda e: e.memset(om64.t[0:64, 0:64], 1.0 / 64), wr=[om64])
        P.op('pool', lambda e: e.memset(om64.t[64:128, 64:128], 1.0 / 64), wr=[om64])

        for l in range(depth):
            P.op('sp', lambda e, l=l: e.dma_start(out=lam_t.t[:, l * 256:(l + 1) * 256], in_=lamv[l]), wr=[lam_t], dma=True)
        P.op('sp', lambda e: e.dma_start(out=dgn_t.t[:], in_=dgn), wr=[dgn_t], dma=True)
        P.op('sp', lambda e: e.dma_start(out=sgn_t.t[:], in_=sgn), wr=[sgn_t], dma=True)
        P.op('sp', lambda e: e.dma_start(out=lnp_t.t[:], in_=lnp), wr=[lnp_t], dma=True)
        P.op('sp', lambda e: e.dma_start(out=cvw_t.t[:], in_=cvw), wr=[cvw_t], dma=True)
        P.op('sp', lambda e: e.dma_start(out=cvb_t.t[:], in_=cvb), wr=[cvb_t], dma=True)
        for l in range(depth):
            lam_init = 0.8 - 0.6 * math.exp(-0.3 * l)
            b0 = l * 256
            for i in range(2):
                q = lam_t.t[:, b0 + i * 128: b0 + i * 128 + 64]
                k = lam_t.t[:, b0 + i * 128 + 64: b0 + i * 128 + 128]
                P.op('dve', lambda e, q=q, k=k: e.tensor_tensor(out=q, in0=q, in1=k, op=ALU.mult), wr=[lam_t])
                P.op('dve', lambda e, q=q, i=i: e.reduce_sum(out=lam_w.t[:, i:i + 1], in_=q, axis=AX.X), rd=[lam_t], wr=[lam_w])
            P.op('act', lambda e: e.activation(out=lam_w.t[:, 2:4], in_=lam_w.t[:, 0:2], func=AF.Exp), wr=[lam_w])
            P.op('dve', lambda e: e.tensor_tensor(out=lam_w.t[:, 4:5], in0=lam_w.t[:, 3:4], in1=lam_w.t[:, 2:3], op=ALU.subtract), wr=[lam_w])
            P.op('dve', lambda e, l=l, li=lam_init: e.tensor_scalar(out=nlam.t[:, l:l + 1], in0=lam_w.t[:, 4:5], scalar1=-li, scalar2=None, op0=ALU.add),
                 rd=[lam_w], wr=[nlam])
            P.op('dve', lambda e, l=l, li=lam_init: e.tensor_scalar(out=dgn_t.t[:, l:l + 1], in0=dgn_t.t[:, l:l + 1], scalar1=1.0 - li, scalar2=None, op0=ALU.mult),
                 wr=[dgn_t])

        bank_rr = [0]

        def next_bank():
            b = banks[bank_rr[0] % 8]
            bank_rr[0] += 1
            return b

        def load_weight(ph, dst, src_view, nk, ncols, stg):
            CH = 2048
            i = 0
            for kc in range(nk):
                for c0 in range(0, ncols, CH):
                    cw = min(CH, ncols - c0)
                    sg = stg[i % len(stg)]
                    P.op('sp', lambda e, sg=sg, kc=kc, c0=c0, cw=cw: e.dma_start(out=sg.t[:, 0:cw], in_=src_view[:, kc, c0:c0 + cw]),
                         wr=[sg], dma=True)
                    eng = ('pool', 'dve')[i % 2]
                    P.op(eng, lambda e, sg=sg, kc=kc, c0=c0, cw=cw: e.tensor_copy(out=dst.t[:, kc, c0:c0 + cw], in_=sg.t[:, 0:cw]),
                         rd=[sg], wr=[dst])
                    i += 1

        def layer_norm(ph, y, N, gcol, bcol, sqr, msb, rsb, tmp, off=0):
            b1 = next_bank()
            b2 = next_bank()
            for oc in range(NCH):
                sq = sqr[oc % len(sqr)]
                P.op('act', lambda e, sq=sq, oc=oc: e.activation(out=sq.t[:, 0:N], in_=y.t[:, oc, off:off + N], func=AF.Square), rd=[y], wr=[sq])
                P.op('pe', lambda e, oc=oc: e.matmul(b1.t[:, 0:N], lhsT=om1024.t[:], rhs=y.t[:, oc, off:off + N], start=(oc == 0), stop=(oc == NCH - 1)),
                     rd=[y, om1024], wr=[b1])
                P.op('pe', lambda e, sq=sq, oc=oc: e.matmul(b2.t[:, 0:N], lhsT=om1024.t[:], rhs=sq.t[:, 0:N], start=(oc == 0), stop=(oc == NCH - 1)),
                     rd=[sq, om1024], wr=[b2])
            P.op('act', lambda e: e.activation(out=msb.t[:, 0:N], in_=b1.t[:, 0:N], func=AF.Copy), rd=[b1], wr=[msb])
            P.op('act', lambda e: e.activation(out=rsb.t[:, 0:N], in_=b1.t[:, 0:N], func=AF.Square), rd=[b1], wr=[rsb])
            P.op('dve', lambda e: e.tensor_tensor(out=rsb.t[:, 0:N], in0=b2.t[:, 0:N], in1=rsb.t[:, 0:N], op=ALU.subtract), rd=[b2], wr=[rsb])
            P.op('act', lambda e: e.activation(out=rsb.t[:, 0:N], in_=rsb.t[:, 0:N], func=AF.Ln, bias=LN_EPS, scale=1.0), wr=[rsb])
            P.op('act', lambda e: e.activation(out=rsb.t[:, 0:N], in_=rsb.t[:, 0:N], func=AF.Exp, scale=-0.5), wr=[rsb])
            for oc in range(NCH):
                tm = tmp[oc % len(tmp)]
                P.op('pool', lambda e, tm=tm, oc=oc: e.tensor_tensor(out=tm.t[:, 0:N], in0=y.t[:, oc, off:off + N], in1=msb.t[:, 0:N], op=ALU.subtract),
                     rd=[y, msb], wr=[tm])
                P.op('pool', lambda e, tm=tm: e.tensor_tensor(out=tm.t[:, 0:N], in0=tm.t[:, 0:N], in1=rsb.t[:, 0:N], op=ALU.mult), rd=[rsb], wr=[tm])
                P.op('dve', lambda e, tm=tm, oc=oc: e.tensor_scalar(out=y.t[:, oc, off:off + N], in0=tm.t[:, 0:N], scalar1=gcol(oc), scalar2=bcol(oc),
                                                                 op0=ALU.mult, op1=ALU.add), rd=[tm, lnp_t], wr=[y])

        xview = lambda ap: ap.rearrange("(c p) s -> p c s", p=128)

        nl = depth if nlayers_run is None else nlayers_run
        def run_layer(l, x_in, x_out):

            P.barrier()
            with ExitStack() as ph:
              if 'A' not in skip:
                wA = sbt(ph, "wA", [128, NCH, 4096], BF16)
                stg = [sbt(ph, f"stgA{i}", [128, 2048], F32) for i in range(2)]
                xf = [sbt(ph, f"xfA{i}", [128, NCH, 512], F32) for i in range(1)]
                xb = [sbt(ph, f"xbA{i}", [128, NCH, 512], BF16) for i in range(2)]
                cs = [sbt(ph, f"csA{i}", [128, 512], F32) for i in range(2)]
                sn = [sbt(ph, f"snA{i}", [128, 512], F32) for i in range(2)]
                t1 = [sbt(ph, f"t1A{i}", [128, 512], F32) for i in range(3)]
                t2 = [sbt(ph, f"t2A{i}", [128, 512], F32) for i in range(3)]
                ob = [sbt(ph, f"obA{i}", [128, 512], BF16) for i in range(4)]
                load_weight(ph, wA, xview(w_in[l]), NCH, 4096, stg)
                oi = [0]
                for j in range(NT):
                    t0 = j * 512
                    xfj = xf[0]
                    xbj = xb[j % 2]
                    csj = cs[j % 2]
                    snj = sn[j % 2]
                    P.op('sp', lambda e, xfj=xfj, t0=t0: e.dma_start(out=xfj.t[:, :, :], in_=xview(x_in)[:, :, t0:t0 + 512]), wr=[xfj], dma=True)
                    P.op('sp', lambda e, csj=csj, t0=t0: e.dma_start(out=csj.t[:], in_=cos2[:, t0:t0 + 512]), wr=[csj], dma=True)
                    P.op('sp', lambda e, snj=snj, t0=t0: e.dma_start(out=snj.t[:], in_=sin2[:, t0:t0 + 512]), wr=[snj], dma=True)
                    for kc in range(NCH):
                        eng = ('pool', 'dve')[kc % 2]
                        P.op(eng, lambda e, kc=kc, xfj=xfj, xbj=xbj: e.tensor_copy(out=xbj.t[:, kc, :], in_=xfj.t[:, kc, :]), rd=[xfj], wr=[xbj])
                    for c in range(8):
                        b1 = next_bank()
                        b2 = next_bank()
                        for kc in range(NCH):
                            P.op('pe', lambda e, kc=kc, c=c, b1=b1, xbj=xbj: e.matmul(b1.t[:, :], lhsT=wA.t[:, kc, 128 * c:128 * c + 128], rhs=xbj.t[:, kc, :],
                                                                                start=(kc == 0), stop=(kc == NCH - 1)), rd=[wA, xbj], wr=[b1])
                        for kc in range(NCH):
                            P.op('pe', lambda e, kc=kc, c=c, b2=b2, xbj=xbj: e.matmul(b2.t[:, :], lhsT=wA.t[:, kc, 3072 + 128 * c:3072 + 128 * c + 128], rhs=xbj.t[:, kc, :],
                                                                                start=(kc == 0), stop=(kc == NCH - 1)), rd=[wA, xbj], wr=[b2])
                        a1 = t1[oi[0] % 3]
                        a2 = t2[oi[0] % 3]
                        o = ob[oi[0] % 4]
                        oi[0] += 1
                        P.op('dve', lambda e, a1=a1, b1=b1, csj=csj: e.tensor_tensor(out=a1.t[:], in0=b1.t[:, :], in1=csj.t[:], op=ALU.mult), rd=[b1, csj], wr=[a1])
                        P.op('dve', lambda e, a2=a2, b2=b2, snj=snj: e.tensor_tensor(out=a2.t[:], in0=b2.t[:, :], in1=snj.t[:], op=ALU.mult), rd=[b2, snj], wr=[a2])
                        P.op('pool', lambda e, a1=a1, a2=a2, o=o: e.tensor_tensor(out=o.t[:], in0=a1.t[:], in1=a2.t[:], op=ALU.add), rd=[a1, a2], wr=[o])
                        P.op('pool', lambda e, o=o, c=c, t0=t0: e.dma_start(out=qk_s[128 * c:128 * c + 128, t0:t0 + 512], in_=o.t[:]), rd=[o], dma=True)
                    for c in range(8):
                        b1 = next_bank()
                        for kc in range(NCH):
                            P.op('pe', lambda e, kc=kc, c=c, b1=b1, xbj=xbj: e.matmul(b1.t[:, :], lhsT=wA.t[:, kc, 1536 + 128 * c:1536 + 128 * c + 128], rhs=xbj.t[:, kc, :],
                                                                                start=(kc == 0), stop=(kc == NCH - 1)), rd=[wA, xbj], wr=[b1])
                        o = ob[oi[0] % 4]
                        oi[0] += 1
                        P.op('act', lambda e, o=o, b1=b1: e.activation(out=o.t[:], in_=b1.t[:, :], func=AF.Copy), rd=[b1], wr=[o])
                        P.op('pool', lambda e, o=o, c=c, t0=t0: e.dma_start(out=qk_s[1024 + 128 * c:1024 + 128 * c + 128, t0:t0 + 512], in_=o.t[:]), rd=[o], dma=True)
                    for tb in range(4):
                        for half, wc0 in ((0, 1024), (1, 2560)):
                            b1 = next_bank()
                            for kc in range(NCH):
                                P.op('pe', lambda e, kc=kc, b1=b1, xbj=xbj, tb=tb, wc0=wc0: e.matmul(b1.t[:, :], lhsT=xbj.t[:, kc, 128 * tb:128 * tb + 128], rhs=wA.t[:, kc, wc0:wc0 + 512],
                                                                                            start=(kc == 0), stop=(kc == NCH - 1)), rd=[wA, xbj], wr=[b1])
                            o = ob[oi[0] % 4]
                            oi[0] += 1
                            P.op('act', lambda e, o=o, b1=b1: e.activation(out=o.t[:], in_=b1.t[:, :], func=AF.Copy), rd=[b1], wr=[o])
                            P.op('pool', lambda e, o=o, tb=tb, half=half, t0=t0: e.dma_start(out=v_s[t0 + 128 * tb:t0 + 128 * tb + 128, 512 * half:512 * half + 512], in_=o.t[:]),
                                 rd=[o], dma=True)

            P.barrier()
            with ExitStack() as ph:
                KTp = [sbt(ph, "KTa", [128, S], BF16), sbt(ph, "KTb", [128, S], BF16)]
                P.op('pool', lambda e: e.memset(KTp[0].t[64:128, :], 0.0), wr=[KTp[0]])
                P.op('pool', lambda e: e.memset(KTp[1].t[0:64, :], 0.0), wr=[KTp[1]])
                QT = sbt(ph, "QT", [128, S], BF16)
                VV = sbt(ph, "VV", [128, NKB, 128], BF16)
                Er = [sbt(ph, f"Er{i}", [128, 512], F32) for i in range(4)]
                SPr = [sbt(ph, f"SPr{i}", [128, 512], F16) for i in range(4)]
                Fr = [sbt(ph, f"Fr{i}", [128, 512], F32) for i in range(3)]
                Ar = [sbt(ph, f"Ar{i}", [128, 512], BF16) for i in range(4)]
                fa = [sbt(ph, f"fa{i}", [128, 512], F32) for i in range(4)]
                fo = [sbt(ph, f"fo{i}", [128, 512], BF16) for i in range(2)]
                vview = v_s.rearrange("(kb p) f -> p kb f", p=128)
                fcnt = [0]

                def rms_finish(o_sb, omat, gcolap, eps, row0, t0):
                    sq = fa[3]
                    rb = banks[7]
                    ofin = fo[fcnt[0] % 2]
                    fcnt[0] += 1
                    P.op('act', lambda e: e.activation(out=sq.t[:], in_=o_sb.t[:], func=AF.Square), rd=[o_sb], wr=[sq])
                    P.op('pe', lambda e: e.matmul(rb.t[:, :], lhsT=omat.t[:], rhs=sq.t[:], start=True, stop=True), rd=[sq, omat], wr=[rb])
                    P.op('act', lambda e: e.activation(out=sq.t[:], in_=rb.t[:, :], func=AF.Ln, bias=eps, scale=1.0), rd=[rb], wr=[sq])
                    P.op('act', lambda e: e.activation(out=sq.t[:], in_=sq.t[:], func=AF.Exp, scale=-0.5), wr=[sq])
                    P.op('dve', lambda e: e.tensor_tensor(out=o_sb.t[:], in0=o_sb.t[:], in1=sq.t[:], op=ALU.mult), rd=[sq], wr=[o_sb])
                    P.op('dve', lambda e: e.tensor_scalar(out=ofin.t[:], in0=o_sb.t[:], scalar1=gcolap, scalar2=None, op0=ALU.mult), rd=[o_sb, dgn_t, sgn_t], wr=[ofin])
                    P.op('pool', lambda e: e.dma_start(out=mix_s[row0:row0 + 128, t0:t0 + 512], in_=ofin.t[:]), rd=[ofin], dma=True)

                for h in range(0 if 'Bd' in skip else 4):
                    P.op('sp', lambda e, h=h: e.dma_start(out=KTp[0].t[0:64, :], in_=qk_s[512 + 128 * h:512 + 128 * h + 64, :]), wr=[KTp[0]], dma=True)
                    P.op('sp', lambda e, h=h: e.dma_start(out=KTp[1].t[64:128, :], in_=qk_s[512 + 128 * h + 64:512 + 128 * h + 128, :]), wr=[KTp[1]], dma=True)
                    P.op('sp', lambda e, h=h: e.dma_start(out=QT.t[:], in_=qk_s[128 * h:128 * h + 128, :]), wr=[QT], dma=True)
                    for q4 in range(0, NKB, VCH):
                        P.op('sp', lambda e, h=h, q4=q4: e.dma_start(out=VV.t[:, q4:q4 + VCH, :], in_=vview[:, q4:q4 + VCH, 128 * h:128 * h + 128]), wr=[VV], dma=True)
                    steps = [(j, c, kb) for j in range(NT) for c in range(2) for kb in range(4 * j + 4)]
                    pz_of = {}

                    def d_qk(i):
                        j, c, kb = steps[i]
                        pz = banks[6 + i % 2]
                        pz_of[i] = pz
                        KT = KTp[c]
                        kT = KT.t[:, kb * 128:(kb + 1) * 128]
                        q0 = j * 512
                        if kb >= 4 * j:
                            c0 = 128 * (kb - 4 * j)
                            P.op('pe', lambda e: e.matmul(pz.t[:, c0:512], lhsT=kT, rhs=QT.t[:, q0 + c0:q0 + 512], start=True, stop=False), rd=[KT, QT], wr=[pz])
                            P.op('pe', lambda e: e.matmul(pz.t[:, c0:c0 + 128], lhsT=ident.t[:], rhs=maskD.t[:], start=False, stop=True), rd=[ident, maskD], wr=[pz])
                        else:
                            P.op('pe', lambda e: e.matmul(pz.t[:, :], lhsT=kT, rhs=QT.t[:, q0:q0 + 512], start=True, stop=True), rd=[KT, QT], wr=[pz])

                    def d_exp(i):
                        j, c, kb = steps[i]
                        pz = pz_of[i]
                        c0 = max(0, 128 * (kb - 4 * j))
                        A = Ar[i % 4]
                        P.op('act', lambda e: e.activation(out=A.t[:, c0:512], in_=pz.t[:, c0:512], func=AF.Exp, scale=0.125), rd=[pz], wr=[A])

                    def d_pv(i):
                        j, c, kb = steps[i]
                        c0 = max(0, 128 * (kb - 4 * j))
                        A = Ar[i % 4]
                        pr = (2 * j + c) % 3
                        po = banks[2 * pr]
                        pl = banks[2 * pr + 1]
                        stt = (kb == 0)
                        P.op('pe', lambda e: e.matmul(po.t[:, c0:512], lhsT=VV.t[:, kb, :], rhs=A.t[:, c0:512], start=stt, stop=True, skip_group_check=True), rd=[VV, A], wr=[po])
                        P.op('pe', lambda e: e.matmul(pl.t[:, c0:512], lhsT=ones_b.t[:], rhs=A.t[:, c0:512], start=stt, stop=True, skip_group_check=True), rd=[ones_b, A], wr=[pl])
                        if c == 1 and kb == 4 * j + 3:
                            p0 = (2 * j) % 3
                            po0, pl0 = banks[2 * p0], banks[2 * p0 + 1]
                            r0, o0, r1 = fa[0], fa[1], fa[2]
                            P.op('dve', lambda e: e.reciprocal(out=r0.t[:], in_=pl0.t[:, :]), rd=[pl0], wr=[r0])
                            P.op('dve', lambda e: e.tensor_tensor(out=o0.t[:], in0=po0.t[:, :], in1=r0.t[:], op=ALU.mult), rd=[po0, r0], wr=[o0])
                            P.op('dve', lambda e: e.reciprocal(out=r1.t[:], in_=pl.t[:, :]), rd=[pl], wr=[r1])
                            P.op('dve', lambda e: e.tensor_tensor(out=r1.t[:], in0=po.t[:, :], in1=r1.t[:], op=ALU.mult), rd=[po], wr=[r1])
                            P.op('dve', lambda e: e.scalar_tensor_tensor(out=o0.t[:], in0=r1.t[:], scalar=nlam.t[:, l:l + 1], in1=o0.t[:], op0=ALU.mult, op1=ALU.add),
                                 rd=[r1, nlam], wr=[o0])
                            rms_finish(o0, om128, dgn_t.t[:, l:l + 1], RMS_EPS, 128 * h, j * 512)

                    n = len(steps)
                    for t in range(n + 1):
                        if t < n:
                            d_qk(t)
                            d_exp(t)
                        if t >= 1:
                            d_pv(t - 1)

                for pp in range(0 if 'Bs' in skip else 4):
                    P.op('sp', lambda e, pp=pp: e.dma_start(out=KTp[0].t[0:64, :], in_=qk_s[1536 + 128 * pp:1536 + 128 * pp + 64, :]), wr=[KTp[0]], dma=True)
                    P.op('sp', lambda e, pp=pp: e.dma_start(out=KTp[1].t[64:128, :], in_=qk_s[1536 + 128 * pp + 64:1536 + 128 * pp + 128, :]), wr=[KTp[1]], dma=True)
                    P.op('sp', lambda e, pp=pp: e.dma_start(out=QT.t[:], in_=qk_s[1024 + 128 * pp:1024 + 128 * pp + 128, :]), wr=[QT], dma=True)
                    for q4 in range(0, NKB, VCH):
                        P.op('sp', lambda e, pp=pp, q4=q4: e.dma_start(out=VV.t[:, q4:q4 + VCH, :], in_=vview[:, q4:q4 + VCH, 512 + 128 * pp:512 + 128 * pp + 128]), wr=[VV], dma=True)
                    steps = [(j, kb, hh) for j in range(NT) for kb in range(4 * j + 3, -1, -1) for hh in range(2)]
                    pz_of = {}

                    def geom(i):
                        j, kb, hh = steps[i]
                        diag = kb >= 4 * j
                        c0 = 128 * (kb - 4 * j) if diag else 0
                        return j, kb, hh, diag, c0

                    def s_qk(i):
                        j, kb, hh, diag, c0 = geom(i)
                        pz = banks[4 + i % 3]
                        pz_of[i] = pz
                        KT = KTp[hh]
                        kT = KT.t[:, kb * 128:(kb + 1) * 128]
                        q0 = j * 512
                        if diag:
                            P.op('pe', lambda e: e.matmul(pz.t[:, c0:512], lhsT=kT, rhs=QT.t[:, q0 + c0:q0 + 512], start=True, stop=False), rd=[KT, QT], wr=[pz])
                            P.op('pe', lambda e: e.matmul(pz.t[:, c0:c0 + 128], lhsT=ident.t[:], rhs=maskS.t[:], start=False, stop=True), rd=[ident, maskS], wr=[pz])
                        else:
                            P.op('pe', lambda e: e.matmul(pz.t[:, :], lhsT=kT, rhs=QT.t[:, q0:q0 + 512], start=True, stop=True), rd=[KT, QT], wr=[pz])

                    def s_act1(i):
                        j, kb, hh, diag, c0 = geom(i)
                        pz = pz_of[i]
                        E = Er[i % 4]
                        SP = SPr[i % 4]
                        P.op('act', lambda e: e.activation(out=E.t[:, c0:512], in_=pz.t[:, c0:512], func=AF.Exp, scale=0.125), rd=[pz], wr=[E])
                        P.op('act', lambda e: e.activation(out=SP.t[:, c0:512], in_=E.t[:, c0:512], func=AF.Ln, bias=1.0, scale=1.0), rd=[E], wr=[SP])

                    def s_u(i):
                        j, kb, hh, diag, c0 = geom(i)
                        SP = SPr[i % 4]
                        X = banks[hh]
                        if kb == 4 * j + 3:
                            P.op('dve', lambda e: e.memset(X.t[:, :], 0.0), wr=[X])
                        P.op('pe', lambda e: e.matmul(X.t[:, c0:512], lhsT=uinc.t[:], rhs=SP.t[:, c0:512], start=False, stop=True, skip_group_check=True), rd=[uinc, SP], wr=[X])

                    def s_act2(i):
                        j, kb, hh, diag, c0 = geom(i)
                        X = banks[hh]
                        Fb = Fr[i % 3]
                        P.op('act', lambda e: e.activation(out=Fb.t[:, c0:512], in_=X.t[:, c0:512], func=AF.Exp, scale=-1.0), rd=[X], wr=[Fb])

                    def s_mul(i):
                        j, kb, hh, diag, c0 = geom(i)
                        E = Er[i % 4]
                        Fb = Fr[i % 3]
                        A = Ar[i % 4]
                        P.op('dve', lambda e: e.tensor_tensor(out=A.t[:, c0:512], in0=E.t[:, c0:512], in1=Fb.t[:, c0:512], op=ALU.mult), rd=[E, Fb], wr=[A])

                    def s_lpv(i):
                        j, kb, hh, diag, c0 = geom(i)
                        SP = SPr[i % 4]
                        A = Ar[i % 4]
                        X = banks[hh]
                        po = banks[2 + hh]
                        if kb > 0:
                            P.op('pe', lambda e: e.matmul(X.t[:, c0:512], lhsT=lstr.t[:], rhs=SP.t[:, c0:512], start=False, stop=True, skip_group_check=True), rd=[lstr, SP], wr=[X])
                        if kb == 4 * j + 3:
                            P.op('dve', lambda e: e.memset(po.t[:, :], 0.0), wr=[po])
                        P.op('pe', lambda e: e.matmul(po.t[:, c0:512], lhsT=VV.t[:, kb, :], rhs=A.t[:, c0:512], start=False, stop=True, skip_group_check=True), rd=[VV, A], wr=[po])
                        if kb == 0 and hh == 1:
                            o0 = fa[0]
                            pa, pb = banks[2], banks[3]
                            P.op('act', lambda e: e.activation(out=o0.t[0:64, :], in_=pa.t[0:64, :], func=AF.Copy), rd=[pa], wr=[o0])
                            P.op('dve', lambda e: e.tensor_copy(out=o0.t[64:128, :], in_=pb.t[64:128, :]), rd=[pb], wr=[o0])
                            rms_finish(o0, om64, sgn_t.t[:, l:l + 1], RMS_EPS, 512 + 128 * pp, j * 512)

                    n = len(steps)
                    for t in range(n + 2):
                        if t < n:
                            s_qk(t)
                            s_act1(t)
                        if 1 <= t <= n:
                            s_u(t - 1)
                            s_act2(t - 1)
                            s_mul(t - 1)
                        if t >= 2:
                            s_lpv(t - 2)

            P.barrier()
            with ExitStack() as ph:
                wO = sbt(ph, "wO", [128, NCH, D], BF16)
                stg = [sbt(ph, f"stgC{i}", [128, 2048], F32) for i in range(2)]
                mx = [sbt(ph, f"mxC{i}", [128, NCH, 512], BF16) for i in range(2)]
                yy = [sbt(ph, f"yyC{i}", [128, NCH, 512], F32) for i in range(2)]
                sqr = [sbt(ph, f"sqC{i}", [128, 512], F32) for i in range(2)]
                tmp = [sbt(ph, f"tmC{i}", [128, 512], F32) for i in range(2)]
                msb = sbt(ph, "msbC", [128, 512], F32)
                rsb = sbt(ph, "rsbC", [128, 512], F32)
                load_weight(ph, wO, xview(w_out[l]), NCH, D, stg)
                g0 = l * 4 * NCH
                for j in range(0 if 'C1' in skip else NT):
                    t0 = j * 512
                    m = mx[j % 2]
                    y = yy[j % 2]
                    P.op('sp', lambda e, m=m, t0=t0: e.dma_start(out=m.t[:, :, :], in_=xview(mix_s)[:, :, t0:t0 + 512]), wr=[m], dma=True)
                    P.op('sp', lambda e, y=y, t0=t0: e.dma_start(out=y.t[:, :, :], in_=xview(x_in)[:, :, t0:t0 + 512]), wr=[y], dma=True)
                    for oc in range(NCH):
                        b1 = next_bank()
                        for kc in range(NCH):
                            P.op('pe', lambda e, kc=kc, oc=oc, b1=b1, m=m: e.matmul(b1.t[:, :], lhsT=wO.t[:, kc, 128 * oc:128 * oc + 128], rhs=m.t[:, kc, :],
                                                                               start=(kc == 0), stop=(kc == NCH - 1)), rd=[wO, m], wr=[b1])
                        P.op('dve', lambda e, oc=oc, b1=b1, y=y: e.scalar_tensor_tensor(out=y.t[:, oc, :], in0=y.t[:, oc, :], scalar=ALPHA, in1=b1.t[:, :],
                                                                                   op0=ALU.mult, op1=ALU.add), rd=[b1], wr=[y])
                    layer_norm(ph, y, 512, lambda oc, g0=g0: lnp_t.t[:, g0 + oc:g0 + oc + 1], lambda oc, g0=g0: lnp_t.t[:, g0 + NCH + oc:g0 + NCH + oc + 1],
                               sqr, msb, rsb, tmp)
                    P.op('pool', lambda e, y=y, t0=t0: e.dma_start(out=xview(x1_s)[:, :, t0:t0 + 512], in_=y.t[:, :, :]), rd=[y], dma=True)

            P.barrier()
            with ExitStack() as ph:
                NF = 256
                wU = sbt(ph, "wU", [128, NCH, 2 * DFF], BF16)
                wD = sbt(ph, "wD", [128, NFF, D], BF16)
                stg = [sbt(ph, f"stgF{i}", [128, 2048], F32) for i in range(2)]
                xh = [sbt(ph, f"xhF{i}", [128, NCH, NF + 2], F32) for i in range(2)]
                xbh = [sbt(ph, f"xbF{i}", [128, NCH, NF + 2], BF16) for i in range(2)]
                gT = sbt(ph, "gT", [128, NFF, NF], BF16)
                ua = [sbt(ph, f"uaF{i}", [128, NF], F32) for i in range(2)]
                ub = [sbt(ph, f"ubF{i}", [128, NF], F32) for i in range(2)]
                sa = [sbt(ph, f"saF{i}", [128, NF], F32) for i in range(2)]
                sqr = [sbt(ph, f"sqF{i}", [128, NF], F32) for i in range(2)]
                tmp = [sbt(ph, f"tmF{i}", [128, NF], F32) for i in range(2)]
                msb = sbt(ph, "msbF", [128, NF], F32)
                rsb = sbt(ph, "rsbF", [128, NF], F32)
                load_weight(ph, wU, xview(w_up[l]), NCH, 2 * DFF, stg)
                load_weight(ph, wD, xview(w_down[l]), NFF, D, stg)
                g0 = l * 4 * NCH + 2 * NCH
                cw0 = l * 6 * NFF
                cb0 = l * 2 * NFF
                ui = [0]
                for j in range(0 if 'C2' in skip else S // NF):
                    t0 = j * NF
                    x1 = xh[j % 2]
                    x1b = xbh[j % 2]
                    if j == 0:
                        P.op('pool', lambda e, x1=x1: e.memset(x1.t[:, :, 0:2], 0.0), wr=[x1])
                        P.op('sp', lambda e, x1=x1: e.dma_start(out=x1.t[:, :, 2:NF + 2], in_=xview(x1_s)[:, :, 0:NF]), wr=[x1], dma=True)
                    else:
                        P.op('sp', lambda e, x1=x1, t0=t0: e.dma_start(out=x1.t[:, :, :], in_=xview(x1_s)[:, :, t0 - 2:t0 + NF]), wr=[x1], dma=True)
                    for kc in range(NCH):
                        eng = ('pool', 'dve')[kc % 2]
                        P.op(eng, lambda e, kc=kc, x1=x1, x1b=x1b: e.tensor_copy(out=x1b.t[:, kc, :], in_=x1.t[:, kc, :]), rd=[x1], wr=[x1b])
                    for i in range(NFF):
                        res = []
                        for half in range(2):
                            ch = half * NFF + i
                            b1 = next_bank()
                            for kc in range(NCH):
                                P.op('pe', lambda e, kc=kc, ch=ch, b1=b1, x1b=x1b: e.matmul(b1.t[:, 0:NF + 2], lhsT=wU.t[:, kc, 128 * ch:128 * ch + 128], rhs=x1b.t[:, kc, :],
                                                                                       start=(kc == 0), stop=(kc == NCH - 1)), rd=[wU, x1b], wr=[b1])
                            u = (ua, ub)[half][ui[0] % 2]
                            w = lambda tap, ch=ch: cvw_t.t[:, cw0 + tap * 2 * NFF + ch:cw0 + tap * 2 * NFF + ch + 1]
                            bcol = cvb_t.t[:, cb0 + ch:cb0 + ch + 1]
                            P.op('act', lambda e, u=u, b1=b1, w=w, bcol=bcol: e.activation(out=u.t[:], in_=b1.t[:, 2:NF + 2], func=AF.Identity, bias=bcol, scale=w(2)),
                                 rd=[b1, cvw_t, cvb_t], wr=[u])
                            P.op('dve', lambda e, u=u, b1=b1, w=w: e.scalar_tensor_tensor(out=u.t[:], in0=b1.t[:, 1:NF + 1], scalar=w(1), in1=u.t[:], op0=ALU.mult, op1=ALU.add),
                                 rd=[b1, cvw_t], wr=[u])
                            P.op('dve', lambda e, u=u, b1=b1, w=w: e.scalar_tensor_tensor(out=u.t[:], in0=b1.t[:, 0:NF], scalar=w(0), in1=u.t[:], op0=ALU.mult, op1=ALU.add),
                                 rd=[b1, cvw_t], wr=[u])
                            res.append(u)
                        s = sa[ui[0] % 2]
                        ui[0] += 1
                        P.op('act', lambda e, s=s, u=res[0]: e.activation(out=s.t[:], in_=u.t[:], func=AF.Silu), rd=[res[0]], wr=[s])
                        P.op('pool', lambda e, s=s, u=res[1], i=i: e.tensor_tensor(out=gT.t[:, i, :], in0=s.t[:], in1=u.t[:], op=ALU.mult), rd=[s, res[1]], wr=[gT])
                    for oc in range(NCH):
                        b1 = next_bank()
                        for i in range(NFF):
                            P.op('pe', lambda e, i=i, oc=oc, b1=b1: e.matmul(b1.t[:, 0:NF], lhsT=wD.t[:, i, 128 * oc:128 * oc + 128], rhs=gT.t[:, i, :],
                                                                        start=(i == 0), stop=(i == NFF - 1)), rd=[wD, gT], wr=[b1])
                        P.op('dve', lambda e, oc=oc, b1=b1, x1=x1: e.scalar_tensor_tensor(out=x1.t[:, oc, 2:NF + 2], in0=x1.t[:, oc, 2:NF + 2], scalar=ALPHA, in1=b1.t[:, 0:NF],
                                                                                     op0=ALU.mult, op1=ALU.add), rd=[b1], wr=[x1])
                    layer_norm(ph, x1, NF, lambda oc, g0=g0: lnp_t.t[:, g0 + oc:g0 + oc + 1], lambda oc, g0=g0: lnp_t.t[:, g0 + NCH + oc:g0 + NCH + oc + 1],
                               sqr, msb, rsb, tmp, off=2)
                    P.op('pool', lambda e, x1=x1, t0=t0: e.dma_start(out=xview(x_out)[:, :, t0:t0 + NF], in_=x1.t[:, :, 2:NF + 2]), rd=[x1], dma=True)

        for l in range(nl):
            run_layer(l, xT if l == 0 else x_s, outT if l == nl - 1 else x_s)
        P.emit()
        nops = P.nops
    return nc, nops


def _host_prep(inp, S):
    depth = DEPTH
    f = np.float32
    w_in = np.asarray(inp["w_in"], f)
    idx = np.arange(1024)
    base = (idx // 64) * 64
    off = idx % 64
    perm = base + (off + 32) % 64
    w_ext = np.concatenate([w_in, w_in[:, :, perm]], axis=2)
    lamv = np.concatenate([inp["lam_q1"], inp["lam_k1"], inp["lam_q2"], inp["lam_k2"]], axis=1).astype(f)
    lamv = np.ascontiguousarray(np.broadcast_to(lamv[:, None, :], (depth, 128, 256)))
    dgn = np.ascontiguousarray(np.asarray(inp["diff_norm_g"], f).T)
    sgn = np.ascontiguousarray(np.tile(np.asarray(inp["sb_norm_g"], f), (1, 2)).T)
    cols = []
    for l in range(depth):
        for nm in ("ln1_g", "ln1_b", "ln2_g", "ln2_b"):
            cols.append(np.asarray(inp[nm][l], f).reshape(NCH, 128).T)
    lnp = np.ascontiguousarray(np.concatenate(cols, axis=1))
    cw = []
    cb = []
    for l in range(depth):
        for tap in range(3):
            cw.append(np.asarray(inp["conv_w"][l, tap], f).reshape(2 * NFF, 128).T)
        cb.append(np.asarray(inp["conv_b"][l], f).reshape(2 * NFF, 128).T)
    cvw = np.ascontiguousarray(np.concatenate(cw, axis=1))
    cvb = np.ascontiguousarray(np.concatenate(cb, axis=1))
    inv = (1.0 / (np.float32(10000.0) ** (np.arange(0, HD, 2, dtype=np.float32) / np.float32(HD)))).astype(f)
    ang = (np.arange(S, dtype=f)[None, :] * inv[:, None]).astype(f)
    c = np.cos(ang).astype(f)
    s = np.sin(ang).astype(f)
    cos2 = np.ascontiguousarray(np.tile(c, (4, 1)))
    sin2 = np.ascontiguousarray(np.concatenate([-s, s, -s, s], axis=0))
    shared = dict(w_in=np.ascontiguousarray(w_ext), w_out=np.asarray(inp["w_out"], f), w_up=np.asarray(inp["w_up"], f),
                  w_down=np.asarray(inp["w_down"], f), lamv=lamv, dgn=dgn, sgn=sgn, lnp=lnp, cvw=cvw, cvb=cvb,
                  cos2=cos2, sin2=sin2)
    return shared


def kernel(**inputs):
    x = np.asarray(inputs["x"], np.float32)
    B, S, _ = x.shape
    shared = _host_prep(inputs, S)
    nc, _ = build(S)
    in_maps = []
    for c in range(8):
        m = dict(shared)
        m["xT"] = np.ascontiguousarray(x[c % B].T)
        in_maps.append(m)
    res = run_bass_kernel_spmd(nc, in_maps, core_ids=list(range(8)))
    out = np.stack([np.ascontiguousarray(res.results[b]["outT"].T) for b in range(B)], axis=0)
    return out.astype(np.float32)
```

```python
import math
from contextlib import ExitStack

import numpy as np
import concourse.bass as bass
import concourse.mybir as mybir
from concourse.bass_utils import run_bass_kernel_spmd

F32 = mybir.dt.float32
BF16 = mybir.dt.bfloat16
F16 = mybir.dt.float16
AF = mybir.ActivationFunctionType
ALU = mybir.AluOpType
AX = mybir.AxisListType

D = 1024
HD = 64
DFF = 2816
NCH = D // 128
NFF = DFF // 128
DEPTH = 2
ALPHA = (2 * DEPTH) ** 0.25
LN_EPS = 1e-5
RMS_EPS = 1e-6
NEG = -30000.0
SAFE_SAME_ENGINE = True
ENG = ['pe', 'act', 'dve', 'pool', 'sp']
NDS = 8


class Buf:
    __slots__ = ("t", "wr", "rd")

    def __init__(self, t):
        self.t = t
        self.wr = None
        self.rd = {}


class Prog:
    def __init__(self, nc, stack):
        self.nc = nc
        self.stack = stack
        self.ops = {e: [] for e in ENG}
        self.cur = {}
        self.cnt = {}
        self.seen = {e: {} for e in ENG}
        self.pending = {e: [] for e in ENG}
        self.nsem = 0
        self.dsem = {}
        self.dcnt = {}
        self.dn = {}
        for e in ENG:
            self._newsem(e)
        for e in ('sp', 'pool', 'act'):
            self.dsem[e] = [self._mk(f"d_{e}{i}") for i in range(NDS)]
            self.dcnt[e] = [0] * NDS
            self.dn[e] = 0
        self.nops = 0

    def _mk(self, name):
        self.nsem += 1
        return self.stack.enter_context(self.nc.semaphore(name))

    def _newsem(self, e):
        self.cur[e] = self._mk(f"s_{e}_{self.nsem}")
        self.cnt[e] = 0

    def op(self, e, fn, rd=(), wr=(), waits=(), dma=False):
        evs = [w for w in waits if w is not None]
        for b in rd:
            if b.wr is not None:
                evs.append(b.wr)
        for b in wr:
            if b.wr is not None:
                evs.append(b.wr)
            evs.extend(b.rd.values())
        if self.pending[e]:
            evs.extend(self.pending[e])
            self.pending[e] = []
        if dma:
            r = self.dn[e] % NDS
            self.dn[e] += 1
            sem = self.dsem[e][r]
            if self.dcnt[e][r] > 0:
                evs.append((sem, self.dcnt[e][r], e, True))
            self.dcnt[e][r] += 16
            val = self.dcnt[e][r]
            inc = 16
        else:
            if self.cnt[e] > 30000:
                self._newsem(e)
            sem = self.cur[e]
            self.cnt[e] += 1
            val = self.cnt[e]
            inc = 1
        wl = []
        seen = self.seen[e]
        for (ws, wv, we, wd) in evs:
            if we == e and not wd and (e == 'pe' or not SAFE_SAME_ENGINE):
                continue
            k = id(ws)
            if seen.get(k, -1) >= wv:
                continue
            seen[k] = wv
            wl.append((ws, wv))

        def run(eng, fn=fn, wl=wl, sem=sem, inc=inc):
            for ws, wv in wl:
                eng.wait_ge(ws, wv)
            fn(eng).then_inc(sem, inc)
        self.ops[e].append(run)
        self.nops += 1
        ev = (sem, val, e, dma)
        for b in rd:
            k = id(sem)
            o = b.rd.get(k)
            if o is None or o[1] < val:
                b.rd[k] = ev
        for b in wr:
            b.wr = ev
            b.rd = {}
        return ev

    def last_events(self):
        evs = []
        for e in ENG:
            if self.cnt[e] > 0:
                evs.append((self.cur[e], self.cnt[e], e, False))
        for e in self.dsem:
            for r in range(NDS):
                if self.dcnt[e][r] > 0:
                    evs.append((self.dsem[e][r], self.dcnt[e][r], e, True))
        return evs

    def barrier(self):
        evs = self.last_events()
        for e in ENG:
            self.pending[e] = list(evs)

    def emit(self):
        nc = self.nc
        final = self.last_events()
        with nc.Block() as block:
            @block.tensor
            def _(eng):
                for f in self.ops['pe']:
                    f(eng)

            @block.scalar
            def _(eng):
                for f in self.ops['act']:
                    f(eng)

            @block.vector
            def _(eng):
                for f in self.ops['dve']:
                    f(eng)

            @block.gpsimd
            def _(eng):
                for f in self.ops['pool']:
                    f(eng)

            @block.sync
            def _(eng):
                for f in self.ops['sp']:
                    f(eng)
                for (ws, wv, we, wd) in final:
                    eng.wait_ge(ws, wv)


def build(S, depth=DEPTH, nlayers_run=None, debug=False, skip=()):
    NT = S // 512
    NKB = S // 128
    VCH = min(16, NKB)
    nc = bass.Bass("TRN2", target_bir_lowering=False)
    dt_in = lambda name, shape, dt=F32: nc.dram_tensor(name, shape, dt, kind="ExternalInput").ap()
    xT = dt_in("xT", [D, S])
    w_in = dt_in("w_in", [depth, D, 4096])
    w_out = dt_in("w_out", [depth, D, D])
    w_up = dt_in("w_up", [depth, D, 2 * DFF])
    w_down = dt_in("w_down", [depth, DFF, D])
    lamv = dt_in("lamv", [depth, 128, 4 * HD])
    dgn = dt_in("dgn", [128, depth])
    sgn = dt_in("sgn", [128, depth])
    lnp = dt_in("lnp", [128, depth * 4 * NCH])
    cvw = dt_in("cvw", [128, depth * 3 * 2 * NFF])
    cvb = dt_in("cvb", [128, depth * 2 * NFF])
    cos2 = dt_in("cos2", [128, S])
    sin2 = dt_in("sin2", [128, S])
    outT = nc.dram_tensor("outT", [D, S], F32, kind="ExternalOutput").ap()
    sk = "ExternalOutput" if debug else "Internal"
    qk_s = nc.dram_tensor("qk_s", [2048, S], BF16, kind=sk).ap()
    v_s = nc.dram_tensor("v_s", [S, 1024], BF16, kind=sk).ap()
    mix_s = nc.dram_tensor("mix_s", [D, S], BF16, kind=sk).ap()
    x1_s = nc.dram_tensor("x1_s", [D, S], F32, kind=sk).ap()
    x_s = nc.dram_tensor("x_s", [D, S], F32, kind=sk).ap()

    with ExitStack() as st:
        P = Prog(nc, st)

        uid = [0]

        def sbt(stack, name, shape, dt):
            uid[0] += 1
            return Buf(stack.enter_context(nc.sbuf_tensor(f"{name}_{uid[0]}", shape, dt)))

        banks = [Buf(st.enter_context(nc.psum_tensor(f"bank{i}", [128, 512], F32))) for i in range(8)]
        ident = sbt(st, "ident", [128, 128], BF16)
        maskD = sbt(st, "maskD", [128, 128], BF16)
        maskS = sbt(st, "maskS", [128, 128], BF16)
        uinc = sbt(st, "uinc", [128, 128], F16)
        lstr = sbt(st, "lstr", [128, 128], F16)
        ones_b = sbt(st, "ones_b", [128, 128], BF16)
        om128 = sbt(st, "om128", [128, 128], F32)
        om64 = sbt(st, "om64", [128, 128], F32)
        om1024 = sbt(st, "om1024", [128, 128], F32)
        cst = sbt(st, "cst", [128, 128], F32)
        lam_t = sbt(st, "lam_t", [128, depth * 4 * HD], F32)
        lam_w = sbt(st, "lam_w", [128, 8], F32)
        nlam = sbt(st, "nlam", [128, depth], F32)
        dgn_t = sbt(st, "dgn_t", [128, depth], F32)
        sgn_t = sbt(st, "sgn_t", [128, depth], F32)
        lnp_t = sbt(st, "lnp_t", [128, depth * 4 * NCH], F32)
        cvw_t = sbt(st, "cvw_t", [128, depth * 6 * NFF], F32)
        cvb_t = sbt(st, "cvb_t", [128, depth * 2 * NFF], F32)

        def aff(buf, val, pattern, cmp, fill, base, cm):
            P.op('pool', lambda e: e.memset(buf.t[:], val), wr=[buf])
            P.op('pool', lambda e: e.affine_select(out=buf.t[:], in_=buf.t[:], pattern=pattern, compare_op=cmp,
                                                    fill=fill, base=base, channel_multiplier=cm), wr=[buf])

        def cvt(dst, src):
            P.op('dve', lambda e: e.tensor_copy(out=dst.t[:], in_=src.t[:]), rd=[src], wr=[dst])

        aff(cst, 1.0, [[-1, 128]], ALU.is_equal, 0.0, 0, 1)
        cvt(ident, cst)
        aff(cst, 0.0, [[1, 128]], ALU.is_ge, NEG, 0, -1)
        cvt(maskD, cst)
        aff(cst, 0.0, [[1, 128]], ALU.is_ge, NEG, -1, -1)
        cvt(maskS, cst)
        aff(cst, 1.0, [[-1, 128]], ALU.is_ge, 0.0, 0, 1)
        cvt(uinc, cst)
        aff(cst, 1.0, [[1, 128]], ALU.is_ge, 0.0, -1, -1)
        cvt(lstr, cst)
        P.op('pool', lambda e: e.memset(cst.t[:], 1.0), wr=[cst])
        cvt(ones_b, cst)
        P.op('pool', lambda e: e.memset(om128.t[:], 1.0 / 128), wr=[om128])
        P.op('pool', lambda e: e.memset(om1024.t[:], 1.0 / 1024), wr=[om1024])
        P.op('pool', lambda e: e.memset(om64.t[:], 0.0), wr=[om64])
        P.op('pool', lambda e: e.memset(om64.t[0:64, 0:64], 1.0 / 64), wr=[om64])
        P.op('pool', lambda e: e.memset(om64.t[64:128, 64:128], 1.0 / 64), wr=[om64])

        for l in range(depth):
            P.op('sp', lambda e, l=l: e.dma_start(out=lam_t.t[:, l * 256:(l + 1) * 256], in_=lamv[l]), wr=[lam_t], dma=True)
        P.op('sp', lambda e: e.dma_start(out=dgn_t.t[:], in_=dgn), wr=[dgn_t], dma=True)
        P.op('sp', lambda e: e.dma_start(out=sgn_t.t[:], in_=sgn), wr=[sgn_t], dma=True)
        P.op('sp', lambda e: e.dma_start(out=lnp_t.t[:], in_=lnp), wr=[lnp_t], dma=True)
        P.op('sp', lambda e: e.dma_start(out=cvw_t.t[:], in_=cvw), wr=[cvw_t], dma=True)
        P.op('sp', lambda e: e.dma_start(out=cvb_t.t[:], in_=cvb), wr=[cvb_t], dma=True)
        for l in range(depth):
            lam_init = 0.8 - 0.6 * math.exp(-0.3 * l)
            b0 = l * 256
            for i in range(2):
                q = lam_t.t[:, b0 + i * 128: b0 + i * 128 + 64]
                k = lam_t.t[:, b0 + i * 128 + 64: b0 + i * 128 + 128]
                P.op('dve', lambda e, q=q, k=k: e.tensor_tensor(out=q, in0=q, in1=k, op=ALU.mult), wr=[lam_t])
                P.op('dve', lambda e, q=q, i=i: e.reduce_sum(out=lam_w.t[:, i:i + 1], in_=q, axis=AX.X), rd=[lam_t], wr=[lam_w])
            P.op('act', lambda e: e.activation(out=lam_w.t[:, 2:4], in_=lam_w.t[:, 0:2], func=AF.Exp), wr=[lam_w])
            P.op('dve', lambda e: e.tensor_tensor(out=lam_w.t[:, 4:5], in0=lam_w.t[:, 3:4], in1=lam_w.t[:, 2:3], op=ALU.subtract), wr=[lam_w])
            P.op('dve', lambda e, l=l, li=lam_init: e.tensor_scalar(out=nlam.t[:, l:l + 1], in0=lam_w.t[:, 4:5], scalar1=-li, scalar2=None, op0=ALU.add),
                 rd=[lam_w], wr=[nlam])
            P.op('dve', lambda e, l=l, li=lam_init: e.tensor_scalar(out=dgn_t.t[:, l:l + 1], in0=dgn_t.t[:, l:l + 1], scalar1=1.0 - li, scalar2=None, op0=ALU.mult),
                 wr=[dgn_t])

        bank_rr = [0]

        def next_bank():
            b = banks[bank_rr[0] % 8]
            bank_rr[0] += 1
            return b

        def load_weight(ph, dst, src_view, nk, ncols, stg):
            CH = 2048
            i = 0
            for kc in range(nk):
                for c0 in range(0, ncols, CH):
                    cw = min(CH, ncols - c0)
                    sg = stg[i % len(stg)]
                    P.op('sp', lambda e, sg=sg, kc=kc, c0=c0, cw=cw: e.dma_start(out=sg.t[:, 0:cw], in_=src_view[:, kc, c0:c0 + cw]),
                         wr=[sg], dma=True)
                    eng = ('pool', 'dve')[i % 2]
                    P.op(eng, lambda e, sg=sg, kc=kc, c0=c0, cw=cw: e.tensor_copy(out=dst.t[:, kc, c0:c0 + cw], in_=sg.t[:, 0:cw]),
                         rd=[sg], wr=[dst])
                    i += 1

        def layer_norm(ph, y, N, gcol, bcol, sqr, msb, rsb, tmp, off=0):
            b1 = next_bank()
            b2 = next_bank()
            for oc in range(NCH):
                sq = sqr[oc % len(sqr)]
                P.op('act', lambda e, sq=sq, oc=oc: e.activation(out=sq.t[:, 0:N], in_=y.t[:, oc, off:off + N], func=AF.Square), rd=[y], wr=[sq])
                P.op('pe', lambda e, oc=oc: e.matmul(b1.t[:, 0:N], lhsT=om1024.t[:], rhs=y.t[:, oc, off:off + N], start=(oc == 0), stop=(oc == NCH - 1)),
                     rd=[y, om1024], wr=[b1])
                P.op('pe', lambda e, sq=sq, oc=oc: e.matmul(b2.t[:, 0:N], lhsT=om1024.t[:], rhs=sq.t[:, 0:N], start=(oc == 0), stop=(oc == NCH - 1)),
                     rd=[sq, om1024], wr=[b2])
            P.op('act', lambda e: e.activation(out=msb.t[:, 0:N], in_=b1.t[:, 0:N], func=AF.Copy), rd=[b1], wr=[msb])
            P.op('act', lambda e: e.activation(out=rsb.t[:, 0:N], in_=b1.t[:, 0:N], func=AF.Square), rd=[b1], wr=[rsb])
            P.op('dve', lambda e: e.tensor_tensor(out=rsb.t[:, 0:N], in0=b2.t[:, 0:N], in1=rsb.t[:, 0:N], op=ALU.subtract), rd=[b2], wr=[rsb])
            P.op('act', lambda e: e.activation(out=rsb.t[:, 0:N], in_=rsb.t[:, 0:N], func=AF.Ln, bias=LN_EPS, scale=1.0), wr=[rsb])
            P.op('act', lambda e: e.activation(out=rsb.t[:, 0:N], in_=rsb.t[:, 0:N], func=AF.Exp, scale=-0.5), wr=[rsb])
            for oc in range(NCH):
                tm = tmp[oc % len(tmp)]
                P.op('pool', lambda e, tm=tm, oc=oc: e.tensor_tensor(out=tm.t[:, 0:N], in0=y.t[:, oc, off:off + N], in1=msb.t[:, 0:N], op=ALU.subtract),
                     rd=[y, msb], wr=[tm])
                P.op('pool', lambda e, tm=tm: e.tensor_tensor(out=tm.t[:, 0:N], in0=tm.t[:, 0:N], in1=rsb.t[:, 0:N], op=ALU.mult), rd=[rsb], wr=[tm])
                P.op('dve', lambda e, tm=tm, oc=oc: e.tensor_scalar(out=y.t[:, oc, off:off + N], in0=tm.t[:, 0:N], scalar1=gcol(oc), scalar2=bcol(oc),
                                                                 op0=ALU.mult, op1=ALU.add), rd=[tm, lnp_t], wr=[y])

        xview = lambda ap: ap.rearrange("(c p) s -> p c s", p=128)

        nl = depth if nlayers_run is None else nlayers_run
        def run_layer(l, x_in, x_out):

            P.barrier()
            with ExitStack() as ph:
              if 'A' not in skip:
                wA = sbt(ph, "wA", [128, NCH, 4096], BF16)
                stg = [sbt(ph, f"stgA{i}", [128, 2048], F32) for i in range(2)]
                xf = [sbt(ph, f"xfA{i}", [128, NCH, 512], F32) for i in range(1)]
                xb = [sbt(ph, f"xbA{i}", [128, NCH, 512], BF16) for i in range(2)]
                cs = [sbt(ph, f"csA{i}", [128, 512], F32) for i in range(2)]
                sn = [sbt(ph, f"snA{i}", [128, 512], F32) for i in range(2)]
                t1 = [sbt(ph, f"t1A{i}", [128, 512], F32) for i in range(3)]
                t2 = [sbt(ph, f"t2A{i}", [128, 512], F32) for i in range(3)]
                ob = [sbt(ph, f"obA{i}", [128, 512], BF16) for i in range(4)]
                load_weight(ph, wA, xview(w_in[l]), NCH, 4096, stg)
                oi = [0]
                for j in range(NT):
                    t0 = j * 512
                    xfj = xf[0]
                    xbj = xb[j % 2]
                    csj = cs[j % 2]
                    snj = sn[j % 2]
                    P.op('sp', lambda e, xfj=xfj, t0=t0: e.dma_start(out=xfj.t[:, :, :], in_=xview(x_in)[:, :, t0:t0 + 512]), wr=[xfj], dma=True)
                    P.op('sp', lambda e, csj=csj, t0=t0: e.dma_start(out=csj.t[:], in_=cos2[:, t0:t0 + 512]), wr=[csj], dma=True)
                    P.op('sp', lambda e, snj=snj, t0=t0: e.dma_start(out=snj.t[:], in_=sin2[:, t0:t0 + 512]), wr=[snj], dma=True)
                    for kc in range(NCH):
                        eng = ('pool', 'dve')[kc % 2]
                        P.op(eng, lambda e, kc=kc, xfj=xfj, xbj=xbj: e.tensor_copy(out=xbj.t[:, kc, :], in_=xfj.t[:, kc, :]), rd=[xfj], wr=[xbj])
                    for c in range(8):
                        b1 = next_bank()
                        b2 = next_bank()
                        for kc in range(NCH):
                            P.op('pe', lambda e, kc=kc, c=c, b1=b1, xbj=xbj: e.matmul(b1.t[:, :], lhsT=wA.t[:, kc, 128 * c:128 * c + 128], rhs=xbj.t[:, kc, :],
                                                                                start=(kc == 0), stop=(kc == NCH - 1)), rd=[wA, xbj], wr=[b1])
                        for kc in range(NCH):
                            P.op('pe', lambda e, kc=kc, c=c, b2=b2, xbj=xbj: e.matmul(b2.t[:, :], lhsT=wA.t[:, kc, 3072 + 128 * c:3072 + 128 * c + 128], rhs=xbj.t[:, kc, :],
                                                                                start=(kc == 0), stop=(kc == NCH - 1)), rd=[wA, xbj], wr=[b2])
                        a1 = t1[oi[0] % 3]
                        a2 = t2[oi[0] % 3]
                        o = ob[oi[0] % 4]
                        oi[0] += 1
                        P.op('dve', lambda e, a1=a1, b1=b1, csj=csj: e.tensor_tensor(out=a1.t[:], in0=b1.t[:, :], in1=csj.t[:], op=ALU.mult), rd=[b1, csj], wr=[a1])
                        P.op('dve', lambda e, a2=a2, b2=b2, snj=snj: e.tensor_tensor(out=a2.t[:], in0=b2.t[:, :], in1=snj.t[:], op=ALU.mult), rd=[b2, snj], wr=[a2])
                        P.op('pool', lambda e, a1=a1, a2=a2, o=o: e.tensor_tensor(out=o.t[:], in0=a1.t[:], in1=a2.t[:], op=ALU.add), rd=[a1, a2], wr=[o])
                        P.op('pool', lambda e, o=o, c=c, t0=t0: e.dma_start(out=qk_s[128 * c:128 * c + 128, t0:t0 + 512], in_=o.t[:]), rd=[o], dma=True)
                    for c in range(8):
                        b1 = next_bank()
                        for kc in range(NCH):
                            P.op('pe', lambda e, kc=kc, c=c, b1=b1, xbj=xbj: e.matmul(b1.t[:, :], lhsT=wA.t[:, kc, 1536 + 128 * c:1536 + 128 * c + 128], rhs=xbj.t[:, kc, :],
                                                                                start=(kc == 0), stop=(kc == NCH - 1)), rd=[wA, xbj], wr=[b1])
                        o = ob[oi[0] % 4]
                        oi[0] += 1
                        P.op('act', lambda e, o=o, b1=b1: e.activation(out=o.t[:], in_=b1.t[:, :], func=AF.Copy), rd=[b1], wr=[o])
                        P.op('pool', lambda e, o=o, c=c, t0=t0: e.dma_start(out=qk_s[1024 + 128 * c:1024 + 128 * c + 128, t0:t0 + 512], in_=o.t[:]), rd=[o], dma=True)
                    for tb in range(4):
                        for half, wc0 in ((0, 1024), (1, 2560)):
                            b1 = next_bank()
                            for kc in range(NCH):
                                P.op('pe', lambda e, kc=kc, b1=b1, xbj=xbj, tb=tb, wc0=wc0: e.matmul(b1.t[:, :], lhsT=xbj.t[:, kc, 128 * tb:128 * tb + 128], rhs=wA.t[:, kc, wc0:wc0 + 512],
                                                                                            start=(kc == 0), stop=(kc == NCH - 1)), rd=[wA, xbj], wr=[b1])
                            o = ob[oi[0] % 4]
                            oi[0] += 1
                            P.op('act', lambda e, o=o, b1=b1: e.activation(out=o.t[:], in_=b1.t[:, :], func=AF.Copy), rd=[b1], wr=[o])
                            P.op('pool', lambda e, o=o, tb=tb, half=half, t0=t0: e.dma_start(out=v_s[t0 + 128 * tb:t0 + 128 * tb + 128, 512 * half:512 * half + 512], in_=o.t[:]),
                                 rd=[o], dma=True)

            P.barrier()
            with ExitStack() as ph:
                KTp = [sbt(ph, "KTa", [128, S], BF16), sbt(ph, "KTb", [128, S], BF16)]
                P.op('pool', lambda e: e.memset(KTp[0].t[64:128, :], 0.0), wr=[KTp[0]])
                P.op('pool', lambda e: e.memset(KTp[1].t[0:64, :], 0.0), wr=[KTp[1]])
                QT = sbt(ph, "QT", [128, S], BF16)
                VV = sbt(ph, "VV", [128, NKB, 128], BF16)
                Er = [sbt(ph, f"Er{i}", [128, 512], F32) for i in range(4)]
                SPr = [sbt(ph, f"SPr{i}", [128, 512], F16) for i in range(4)]
                Fr = [sbt(ph, f"Fr{i}", [128, 512], F32) for i in range(3)]
                Ar = [sbt(ph, f"Ar{i}", [128, 512], BF16) for i in range(4)]
                fa = [sbt(ph, f"fa{i}", [128, 512], F32) for i in range(4)]
                fo = [sbt(ph, f"fo{i}", [128, 512], BF16) for i in range(2)]
                vview = v_s.rearrange("(kb p) f -> p kb f", p=128)
                fcnt = [0]

                def rms_finish(o_sb, omat, gcolap, eps, row0, t0):
                    sq = fa[3]
                    rb = banks[7]
                    ofin = fo[fcnt[0] % 2]
                    fcnt[0] += 1
                    P.op('act', lambda e: e.activation(out=sq.t[:], in_=o_sb.t[:], func=AF.Square), rd=[o_sb], wr=[sq])
                    P.op('pe', lambda e: e.matmul(rb.t[:, :], lhsT=omat.t[:], rhs=sq.t[:], start=True, stop=True), rd=[sq, omat], wr=[rb])
                    P.op('act', lambda e: e.activation(out=sq.t[:], in_=rb.t[:, :], func=AF.Ln, bias=eps, scale=1.0), rd=[rb], wr=[sq])
                    P.op('act', lambda e: e.activation(out=sq.t[:], in_=sq.t[:], func=AF.Exp, scale=-0.5), wr=[sq])
                    P.op('dve', lambda e: e.tensor_tensor(out=o_sb.t[:], in0=o_sb.t[:], in1=sq.t[:], op=ALU.mult), rd=[sq], wr=[o_sb])
                    P.op('dve', lambda e: e.tensor_scalar(out=ofin.t[:], in0=o_sb.t[:], scalar1=gcolap, scalar2=None, op0=ALU.mult), rd=[o_sb, dgn_t, sgn_t], wr=[ofin])
                    P.op('pool', lambda e: e.dma_start(out=mix_s[row0:row0 + 128, t0:t0 + 512], in_=ofin.t[:]), rd=[ofin], dma=True)

                for h in range(0 if 'Bd' in skip else 4):
                    P.op('sp', lambda e, h=h: e.dma_start(out=KTp[0].t[0:64, :], in_=qk_s[512 + 128 * h:512 + 128 * h + 64, :]), wr=[KTp[0]], dma=True)
                    P.op('sp', lambda e, h=h: e.dma_start(out=KTp[1].t[64:128, :], in_=qk_s[512 + 128 * h + 64:512 + 128 * h + 128, :]), wr=[KTp[1]], dma=True)
                    P.op('sp', lambda e, h=h: e.dma_start(out=QT.t[:], in_=qk_s[128 * h:128 * h + 128, :]), wr=[QT], dma=True)
                    for q4 in range(0, NKB, VCH):
                        P.op('sp', lambda e, h=h, q4=q4: e.dma_start(out=VV.t[:, q4:q4 + VCH, :], in_=vview[:, q4:q4 + VCH, 128 * h:128 * h + 128]), wr=[VV], dma=True)
                    steps = [(j, c, kb) for j in range(NT) for c in range(2) for kb in range(4 * j + 4)]
                    pz_of = {}

                    def d_qk(i):
                        j, c, kb = steps[i]
                        pz = banks[6 + i % 2]
                        pz_of[i] = pz
                        KT = KTp[c]
                        kT = KT.t[:, kb * 128:(kb + 1) * 128]
                        q0 = j * 512
                        if kb >= 4 * j:
                            c0 = 128 * (kb - 4 * j)
                            P.op('pe', lambda e: e.matmul(pz.t[:, c0:512], lhsT=kT, rhs=QT.t[:, q0 + c0:q0 + 512], start=True, stop=False), rd=[KT, QT], wr=[pz])
                            P.op('pe', lambda e: e.matmul(pz.t[:, c0:c0 + 128], lhsT=ident.t[:], rhs=maskD.t[:], start=False, stop=True), rd=[ident, maskD], wr=[pz])
                        else:
                            P.op('pe', lambda e: e.matmul(pz.t[:, :], lhsT=kT, rhs=QT.t[:, q0:q0 + 512], start=True, stop=True), rd=[KT, QT], wr=[pz])

                    def d_exp(i):
                        j, c, kb = steps[i]
                        pz = pz_of[i]
                        c0 = max(0, 128 * (kb - 4 * j))
                        A = Ar[i % 4]
                        P.op('act', lambda e: e.activation(out=A.t[:, c0:512], in_=pz.t[:, c0:512], func=AF.Exp, scale=0.125), rd=[pz], wr=[A])

                    def d_pv(i):
                        j, c, kb = steps[i]
                        c0 = max(0, 128 * (kb - 4 * j))
                        A = Ar[i % 4]
                        pr = (2 * j + c) % 3
                        po = banks[2 * pr]
                        pl = banks[2 * pr + 1]
                        stt = (kb == 0)
                        P.op('pe', lambda e: e.matmul(po.t[:, c0:512], lhsT=VV.t[:, kb, :], rhs=A.t[:, c0:512], start=stt, stop=True, skip_group_check=True), rd=[VV, A], wr=[po])
                        P.op('pe', lambda e: e.matmul(pl.t[:, c0:512], lhsT=ones_b.t[:], rhs=A.t[:, c0:512], start=stt, stop=True, skip_group_check=True), rd=[ones_b, A], wr=[pl])
                        if c == 1 and kb == 4 * j + 3:
                            p0 = (2 * j) % 3
                            po0, pl0 = banks[2 * p0], banks[2 * p0 + 1]
                            r0, o0, r1 = fa[0], fa[1], fa[2]
                            P.op('dve', lambda e: e.reciprocal(out=r0.t[:], in_=pl0.t[:, :]), rd=[pl0], wr=[r0])
                            P.op('dve', lambda e: e.tensor_tensor(out=o0.t[:], in0=po0.t[:, :], in1=r0.t[:], op=ALU.mult), rd=[po0, r0], wr=[o0])
                            P.op('dve', lambda e: e.reciprocal(out=r1.t[:], in_=pl.t[:, :]), rd=[pl], wr=[r1])
                            P.op('dve', lambda e: e.tensor_tensor(out=r1.t[:], in0=po.t[:, :], in1=r1.t[:], op=ALU.mult), rd=[po], wr=[r1])
                            P.op('dve', lambda e: e.scalar_tensor_tensor(out=o0.t[:], in0=r1.t[:], scalar=nlam.t[:, l:l + 1], in1=o0.t[:], op0=ALU.mult, op1=ALU.add),
                                 rd=[r1, nlam], wr=[o0])
                            rms_finish(o0, om128, dgn_t.t[:, l:l + 1], RMS_EPS, 128 * h, j * 512)

                    n = len(steps)
                    for t in range(n + 2):
                        if t < n:
                            d_qk(t)
                            d_exp(t)
                        if t >= 2:
                            d_pv(t - 2)

                for pp in range(0 if 'Bs' in skip else 4):
                    P.op('sp', lambda e, pp=pp: e.dma_start(out=KTp[0].t[0:64, :], in_=qk_s[1536 + 128 * pp:1536 + 128 * pp + 64, :]), wr=[KTp[0]], dma=True)
                    P.op('sp', lambda e, pp=pp: e.dma_start(out=KTp[1].t[64:128, :], in_=qk_s[1536 + 128 * pp + 64:1536 + 128 * pp + 128, :]), wr=[KTp[1]], dma=True)
                    P.op('sp', lambda e, pp=pp: e.dma_start(out=QT.t[:], in_=qk_s[1024 + 128 * pp:1024 + 128 * pp + 128, :]), wr=[QT], dma=True)
                    for q4 in range(0, NKB, VCH):
                        P.op('sp', lambda e, pp=pp, q4=q4: e.dma_start(out=VV.t[:, q4:q4 + VCH, :], in_=vview[:, q4:q4 + VCH, 512 + 128 * pp:512 + 128 * pp + 128]), wr=[VV], dma=True)
                    steps = [(j, kb, hh) for j in range(NT) for kb in range(4 * j + 3, -1, -1) for hh in range(2)]
                    pz_of = {}

                    def geom(i):
                        j, kb, hh = steps[i]
                        diag = kb >= 4 * j
                        c0 = 128 * (kb - 4 * j) if diag else 0
                        return j, kb, hh, diag, c0

                    def s_qk(i):
                        j, kb, hh, diag, c0 = geom(i)
                        pz = banks[4 + i % 3]
                        pz_of[i] = pz
                        KT = KTp[hh]
                        kT = KT.t[:, kb * 128:(kb + 1) * 128]
                        q0 = j * 512
                        if diag:
                            P.op('pe', lambda e: e.matmul(pz.t[:, c0:512], lhsT=kT, rhs=QT.t[:, q0 + c0:q0 + 512], start=True, stop=False), rd=[KT, QT], wr=[pz])
                            P.op('pe', lambda e: e.matmul(pz.t[:, c0:c0 + 128], lhsT=ident.t[:], rhs=maskS.t[:], start=False, stop=True), rd=[ident, maskS], wr=[pz])
                        else:
                            P.op('pe', lambda e: e.matmul(pz.t[:, :], lhsT=kT, rhs=QT.t[:, q0:q0 + 512], start=True, stop=True), rd=[KT, QT], wr=[pz])

                    def s_act1(i):
                        j, kb, hh, diag, c0 = geom(i)
                        pz = pz_of[i]
                        E = Er[i % 4]
                        SP = SPr[i % 4]
                        P.op('act', lambda e: e.activation(out=E.t[:, c0:512], in_=pz.t[:, c0:512], func=AF.Exp, scale=0.125), rd=[pz], wr=[E])

                    def s_act1b(i):
                        j, kb, hh, diag, c0 = geom(i)
                        E = Er[i % 4]
                        SP = SPr[i % 4]
                        P.op('act', lambda e: e.activation(out=SP.t[:, c0:512], in_=E.t[:, c0:512], func=AF.Ln, bias=1.0, scale=1.0), rd=[E], wr=[SP])

                    def s_u(i):
                        j, kb, hh, diag, c0 = geom(i)
                        SP = SPr[i % 4]
                        X = banks[hh]
                        if kb == 4 * j + 3:
                            P.op('dve', lambda e: e.memset(X.t[:, :], 0.0), wr=[X])
                        P.op('pe', lambda e: e.matmul(X.t[:, c0:512], lhsT=uinc.t[:], rhs=SP.t[:, c0:512], start=False, stop=True, skip_group_check=True), rd=[uinc, SP], wr=[X])

                    def s_act2(i):
                        j, kb, hh, diag, c0 = geom(i)
                        X = banks[hh]
                        Fb = Fr[i % 3]
                        P.op('act', lambda e: e.activation(out=Fb.t[:, c0:512], in_=X.t[:, c0:512], func=AF.Exp, scale=-1.0), rd=[X], wr=[Fb])

                    def s_mul(i):
                        j, kb, hh, diag, c0 = geom(i)
                        E = Er[i % 4]
                        Fb = Fr[i % 3]
                        A = Ar[i % 4]
                        P.op('dve', lambda e: e.tensor_tensor(out=A.t[:, c0:512], in0=E.t[:, c0:512], in1=Fb.t[:, c0:512], op=ALU.mult), rd=[E, Fb], wr=[A])

                    def s_lpv(i):
                        j, kb, hh, diag, c0 = geom(i)
                        SP = SPr[i % 4]
                        A = Ar[i % 4]
                        X = banks[hh]
                        po = banks[2 + hh]
                        if kb > 0:
                            P.op('pe', lambda e: e.matmul(X.t[:, c0:512], lhsT=lstr.t[:], rhs=SP.t[:, c0:512], start=False, stop=True, skip_group_check=True), rd=[lstr, SP], wr=[X])
                        if kb == 4 * j + 3:
                            P.op('dve', lambda e: e.memset(po.t[:, :], 0.0), wr=[po])
                        P.op('pe', lambda e: e.matmul(po.t[:, c0:512], lhsT=VV.t[:, kb, :], rhs=A.t[:, c0:512], start=False, stop=True, skip_group_check=True), rd=[VV, A], wr=[po])
                        if kb == 0 and hh == 1:
                            o0 = fa[0]
                            pa, pb = banks[2], banks[3]
                            P.op('act', lambda e: e.activation(out=o0.t[0:64, :], in_=pa.t[0:64, :], func=AF.Copy), rd=[pa], wr=[o0])
                            P.op('dve', lambda e: e.tensor_copy(out=o0.t[64:128, :], in_=pb.t[64:128, :]), rd=[pb], wr=[o0])
                            rms_finish(o0, om64, sgn_t.t[:, l:l + 1], RMS_EPS, 512 + 128 * pp, j * 512)

                    n = len(steps)
                    for t in range(n + 2):
                        if t < n:
                            s_qk(t)
                            s_act1(t)
                        if 1 <= t <= n:
                            s_u(t - 1)
                            s_act2(t - 1)
                        if t < n:
                            s_act1b(t)
                        if 1 <= t <= n:
                            s_mul(t - 1)
                        if t >= 2:
                            s_lpv(t - 2)

            P.barrier()
            with ExitStack() as ph:
                wO = sbt(ph, "wO", [128, NCH, D], BF16)
                stg = [sbt(ph, f"stgC{i}", [128, 2048], F32) for i in range(2)]
                mx = [sbt(ph, f"mxC{i}", [128, NCH, 512], BF16) for i in range(2)]
                yy = [sbt(ph, f"yyC{i}", [128, NCH, 512], F32) for i in range(2)]
                sqr = [sbt(ph, f"sqC{i}", [128, 512], F32) for i in range(2)]
                tmp = [sbt(ph, f"tmC{i}", [128, 512], F32) for i in range(2)]
                msb = sbt(ph, "msbC", [128, 512], F32)
                rsb = sbt(ph, "rsbC", [128, 512], F32)
                load_weight(ph, wO, xview(w_out[l]), NCH, D, stg)
                g0 = l * 4 * NCH
                for j in range(0 if 'C1' in skip else NT):
                    t0 = j * 512
                    m = mx[j % 2]
                    y = yy[j % 2]
                    P.op('sp', lambda e, m=m, t0=t0: e.dma_start(out=m.t[:, :, :], in_=xview(mix_s)[:, :, t0:t0 + 512]), wr=[m], dma=True)
                    P.op('sp', lambda e, y=y, t0=t0: e.dma_start(out=y.t[:, :, :], in_=xview(x_in)[:, :, t0:t0 + 512]), wr=[y], dma=True)
                    for oc in range(NCH):
                        b1 = next_bank()
                        for kc in range(NCH):
                            P.op('pe', lambda e, kc=kc, oc=oc, b1=b1, m=m: e.matmul(b1.t[:, :], lhsT=wO.t[:, kc, 128 * oc:128 * oc + 128], rhs=m.t[:, kc, :],
                                                                               start=(kc == 0), stop=(kc == NCH - 1)), rd=[wO, m], wr=[b1])
                        P.op('dve', lambda e, oc=oc, b1=b1, y=y: e.scalar_tensor_tensor(out=y.t[:, oc, :], in0=y.t[:, oc, :], scalar=ALPHA, in1=b1.t[:, :],
                                                                                   op0=ALU.mult, op1=ALU.add), rd=[b1], wr=[y])
                    layer_norm(ph, y, 512, lambda oc, g0=g0: lnp_t.t[:, g0 + oc:g0 + oc + 1], lambda oc, g0=g0: lnp_t.t[:, g0 + NCH + oc:g0 + NCH + oc + 1],
                               sqr, msb, rsb, tmp)
                    P.op('pool', lambda e, y=y, t0=t0: e.dma_start(out=xview(x1_s)[:, :, t0:t0 + 512], in_=y.t[:, :, :]), rd=[y], dma=True)

            P.barrier()
            with ExitStack() as ph:
                NF = 256
                wU = sbt(ph, "wU", [128, NCH, 2 * DFF], BF16)
                wD = sbt(ph, "wD", [128, NFF, D], BF16)
                stg = [sbt(ph, f"stgF{i}", [128, 2048], F32) for i in range(2)]
                xh = [sbt(ph, f"xhF{i}", [128, NCH, NF + 2], F32) for i in range(2)]
                xbh = [sbt(ph, f"xbF{i}", [128, NCH, NF + 2], BF16) for i in range(2)]
                gT = sbt(ph, "gT", [128, NFF, NF], BF16)
                ua = [sbt(ph, f"uaF{i}", [128, NF], F32) for i in range(2)]
                ub = [sbt(ph, f"ubF{i}", [128, NF], F32) for i in range(2)]
                sa = [sbt(ph, f"saF{i}", [128, NF], F32) for i in range(2)]
                sqr = [sbt(ph, f"sqF{i}", [128, NF], F32) for i in range(2)]
                tmp = [sbt(ph, f"tmF{i}", [128, NF], F32) for i in range(2)]
                msb = sbt(ph, "msbF", [128, NF], F32)
                rsb = sbt(ph, "rsbF", [128, NF], F32)
                load_weight(ph, wU, xview(w_up[l]), NCH, 2 * DFF, stg)
                load_weight(ph, wD, xview(w_down[l]), NFF, D, stg)
                g0 = l * 4 * NCH + 2 * NCH
                cw0 = l * 6 * NFF
                cb0 = l * 2 * NFF
                ui = [0]
                for j in range(0 if 'C2' in skip else S // NF):
                    t0 = j * NF
                    x1 = xh[j % 2]
                    x1b = xbh[j % 2]
                    if j == 0:
                        P.op('pool', lambda e, x1=x1: e.memset(x1.t[:, :, 0:2], 0.0), wr=[x1])
                        P.op('sp', lambda e, x1=x1: e.dma_start(out=x1.t[:, :, 2:NF + 2], in_=xview(x1_s)[:, :, 0:NF]), wr=[x1], dma=True)
                    else:
                        P.op('sp', lambda e, x1=x1, t0=t0: e.dma_start(out=x1.t[:, :, :], in_=xview(x1_s)[:, :, t0 - 2:t0 + NF]), wr=[x1], dma=True)
                    for kc in range(NCH):
                        eng = ('pool', 'dve')[kc % 2]
                        P.op(eng, lambda e, kc=kc, x1=x1, x1b=x1b: e.tensor_copy(out=x1b.t[:, kc, :], in_=x1.t[:, kc, :]), rd=[x1], wr=[x1b])
                    for i in range(NFF):
                        res = []
                        for half in range(2):
                            ch = half * NFF + i
                            b1 = next_bank()
                            for kc in range(NCH):
                                P.op('pe', lambda e, kc=kc, ch=ch, b1=b1, x1b=x1b: e.matmul(b1.t[:, 0:NF + 2], lhsT=wU.t[:, kc, 128 * ch:128 * ch + 128], rhs=x1b.t[:, kc, :],
                                                                                       start=(kc == 0), stop=(kc == NCH - 1)), rd=[wU, x1b], wr=[b1])
                            u = (ua, ub)[half][ui[0] % 2]
                            w = lambda tap, ch=ch: cvw_t.t[:, cw0 + tap * 2 * NFF + ch:cw0 + tap * 2 * NFF + ch + 1]
                            bcol = cvb_t.t[:, cb0 + ch:cb0 + ch + 1]
                            P.op('act', lambda e, u=u, b1=b1, w=w, bcol=bcol: e.activation(out=u.t[:], in_=b1.t[:, 2:NF + 2], func=AF.Identity, bias=bcol, scale=w(2)),
                                 rd=[b1, cvw_t, cvb_t], wr=[u])
                            P.op('dve', lambda e, u=u, b1=b1, w=w: e.scalar_tensor_tensor(out=u.t[:], in0=b1.t[:, 1:NF + 1], scalar=w(1), in1=u.t[:], op0=ALU.mult, op1=ALU.add),
                                 rd=[b1, cvw_t], wr=[u])
                            P.op('dve', lambda e, u=u, b1=b1, w=w: e.scalar_tensor_tensor(out=u.t[:], in0=b1.t[:, 0:NF], scalar=w(0), in1=u.t[:], op0=ALU.mult, op1=ALU.add),
                                 rd=[b1, cvw_t], wr=[u])
                            res.append(u)
                        s = sa[ui[0] % 2]
                        ui[0] += 1
                        P.op('act', lambda e, s=s, u=res[0]: e.activation(out=s.t[:], in_=u.t[:], func=AF.Silu), rd=[res[0]], wr=[s])
                        P.op('pool', lambda e, s=s, u=res[1], i=i: e.tensor_tensor(out=gT.t[:, i, :], in0=s.t[:], in1=u.t[:], op=ALU.mult), rd=[s, res[1]], wr=[gT])
                    for oc in range(NCH):
                        b1 = next_bank()
                        for i in range(NFF):
                            P.op('pe', lambda e, i=i, oc=oc, b1=b1: e.matmul(b1.t[:, 0:NF], lhsT=wD.t[:, i, 128 * oc:128 * oc + 128], rhs=gT.t[:, i, :],
                                                                        start=(i == 0), stop=(i == NFF - 1)), rd=[wD, gT], wr=[b1])
                        P.op('dve', lambda e, oc=oc, b1=b1, x1=x1: e.scalar_tensor_tensor(out=x1.t[:, oc, 2:NF + 2], in0=x1.t[:, oc, 2:NF + 2], scalar=ALPHA, in1=b1.t[:, 0:NF],
                                                                                     op0=ALU.mult, op1=ALU.add), rd=[b1], wr=[x1])
                    layer_norm(ph, x1, NF, lambda oc, g0=g0: lnp_t.t[:, g0 + oc:g0 + oc + 1], lambda oc, g0=g0: lnp_t.t[:, g0 + NCH + oc:g0 + NCH + oc + 1],
                               sqr, msb, rsb, tmp, off=2)
                    P.op('pool', lambda e, x1=x1, t0=t0: e.dma_start(out=xview(x_out)[:, :, t0:t0 + NF], in_=x1.t[:, :, 2:NF + 2]), rd=[x1], dma=True)

        for l in range(nl):
            run_layer(l, xT if l == 0 else x_s, outT if l == nl - 1 else x_s)
        P.emit()
        nops = P.nops
    return nc, nops


def _host_prep(inp, S):
    depth = DEPTH
    f = np.float32
    w_in = np.asarray(inp["w_in"], f)
    idx = np.arange(1024)
    base = (idx // 64) * 64
    off = idx % 64
    perm = base + (off + 32) % 64
    w_ext = np.concatenate([w_in, w_in[:, :, perm]], axis=2)
    lamv = np.concatenate([inp["lam_q1"], inp["lam_k1"], inp["lam_q2"], inp["lam_k2"]], axis=1).astype(f)
    lamv = np.ascontiguousarray(np.broadcast_to(lamv[:, None, :], (depth, 128, 256)))
    dgn = np.ascontiguousarray(np.asarray(inp["diff_norm_g"], f).T)
    sgn = np.ascontiguousarray(np.tile(np.asarray(inp["sb_norm_g"], f), (1, 2)).T)
    cols = []
    for l in range(depth):
        for nm in ("ln1_g", "ln1_b", "ln2_g", "ln2_b"):
            cols.append(np.asarray(inp[nm][l], f).reshape(NCH, 128).T)
    lnp = np.ascontiguousarray(np.concatenate(cols, axis=1))
    cw = []
    cb = []
    for l in range(depth):
        for tap in range(3):
            cw.append(np.asarray(inp["conv_w"][l, tap], f).reshape(2 * NFF, 128).T)
        cb.append(np.asarray(inp["conv_b"][l], f).reshape(2 * NFF, 128).T)
    cvw = np.ascontiguousarray(np.concatenate(cw, axis=1))
    cvb = np.ascontiguousarray(np.concatenate(cb, axis=1))
    inv = (1.0 / (np.float32(10000.0) ** (np.arange(0, HD, 2, dtype=np.float32) / np.float32(HD)))).astype(f)
    ang = (np.arange(S, dtype=f)[None, :] * inv[:, None]).astype(f)
    c = np.cos(ang).astype(f)
    s = np.sin(ang).astype(f)
    cos2 = np.ascontiguousarray(np.tile(c, (4, 1)))
    sin2 = np.ascontiguousarray(np.concatenate([-s, s, -s, s], axis=0))
    shared = dict(w_in=np.ascontiguousarray(w_ext), w_out=np.asarray(inp["w_out"], f), w_up=np.asarray(inp["w_up"], f),
                  w_down=np.asarray(inp["w_down"], f), lamv=lamv, dgn=dgn, sgn=sgn, lnp=lnp, cvw=cvw, cvb=cvb,
                  cos2=cos2, sin2=sin2)
    return shared


def kernel(**inputs):
    x = np.asarray(inputs["x"], np.float32)
    B, S, _ = x.shape
    shared = _host_prep(inputs, S)
    nc, _ = build(S)
    in_maps = []
    for c in range(8):
        m = dict(shared)
        m["xT"] = np.ascontiguousarray(x[c % B].T)
        in_maps.append(m)
    res = run_bass_kernel_spmd(nc, in_maps, core_ids=list(range(8)))
    out = np.stack([np.ascontiguousarray(res.results[b]["outT"].T) for b in range(B)], axis=0)
    return out.astype(np.float32)
```

```python
import math
from contextlib import ExitStack

import numpy as np
import concourse.bass as bass
import concourse.mybir as mybir
from concourse.bass_utils import run_bass_kernel_spmd

F32 = mybir.dt.float32
BF16 = mybir.dt.bfloat16
F16 = mybir.dt.float16
AF = mybir.ActivationFunctionType
ALU = mybir.AluOpType
AX = mybir.AxisListType

D = 1024
HD = 64
DFF = 2816
NCH = D // 128
NFF = DFF // 128
DEPTH = 2
ALPHA = (2 * DEPTH) ** 0.25
LN_EPS = 1e-5
RMS_EPS = 1e-6
NEG = -30000.0
SAFE_SAME_ENGINE = True
ENG = ['pe', 'act', 'dve', 'pool', 'sp']
NDS = 8


class Buf:
    __slots__ = ("t", "wr", "rd")

    def __init__(self, t):
        self.t = t
        self.wr = None
        self.rd = {}


class Prog:
    def __init__(self, nc, stack):
        self.nc = nc
        self.stack = stack
        self.ops = {e: [] for e in ENG}
        self.cur = {}
        self.cnt = {}
        self.seen = {e: {} for e in ENG}
        self.pending = {e: [] for e in ENG}
        self.nsem = 0
        self.dsem = {}
        self.dcnt = {}
        self.dn = {}
        for e in ENG:
            self._newsem(e)
        for e in ('sp', 'pool', 'act'):
            self.dsem[e] = [self._mk(f"d_{e}{i}") for i in range(NDS)]
            self.dcnt[e] = [0] * NDS
            self.dn[e] = 0
        self.nops = 0

    def _mk(self, name):
        self.nsem += 1
        return self.stack.enter_context(self.nc.semaphore(name))

    def _newsem(self, e):
        self.cur[e] = self._mk(f"s_{e}_{self.nsem}")
        self.cnt[e] = 0

    def op(self, e, fn, rd=(), wr=(), waits=(), dma=False):
        evs = [w for w in waits if w is not None]
        for b in rd:
            if b.wr is not None:
                evs.append(b.wr)
        for b in wr:
            if b.wr is not None:
                evs.append(b.wr)
            evs.extend(b.rd.values())
        if self.pending[e]:
            evs.extend(self.pending[e])
            self.pending[e] = []
        if dma:
            r = self.dn[e] % NDS
            self.dn[e] += 1
            sem = self.dsem[e][r]
            if self.dcnt[e][r] > 0:
                evs.append((sem, self.dcnt[e][r], e, True))
            self.dcnt[e][r] += 16
            val = self.dcnt[e][r]
            inc = 16
        else:
            if self.cnt[e] > 30000:
                self._newsem(e)
            sem = self.cur[e]
            self.cnt[e] += 1
            val = self.cnt[e]
            inc = 1
        wl = []
        seen = self.seen[e]
        for (ws, wv, we, wd) in evs:
            if we == e and not wd and (e == 'pe' or not SAFE_SAME_ENGINE):
                continue
            k = id(ws)
            if seen.get(k, -1) >= wv:
                continue
            seen[k] = wv
            wl.append((ws, wv))

        def run(eng, fn=fn, wl=wl, sem=sem, inc=inc):
            for ws, wv in wl:
                eng.wait_ge(ws, wv)
            fn(eng).then_inc(sem, inc)
        self.ops[e].append(run)
        self.nops += 1
        ev = (sem, val, e, dma)
        for b in rd:
            k = id(sem)
            o = b.rd.get(k)
            if o is None or o[1] < val:
                b.rd[k] = ev
        for b in wr:
            b.wr = ev
            b.rd = {}
        return ev

    def last_events(self):
        evs = []
        for e in ENG:
            if self.cnt[e] > 0:
                evs.append((self.cur[e], self.cnt[e], e, False))
        for e in self.dsem:
            for r in range(NDS):
                if self.dcnt[e][r] > 0:
                    evs.append((self.dsem[e][r], self.dcnt[e][r], e, True))
        return evs

    def barrier(self):
        evs = self.last_events()
        for e in ENG:
            self.pending[e] = list(evs)

    def emit(self):
        nc = self.nc
        final = self.last_events()
        with nc.Block() as block:
            @block.tensor
            def _(eng):
                for f in self.ops['pe']:
                    f(eng)

            @block.scalar
            def _(eng):
                for f in self.ops['act']:
                    f(eng)

            @block.vector
            def _(eng):
                for f in self.ops['dve']:
                    f(eng)

            @block.gpsimd
            def _(eng):
                for f in self.ops['pool']:
                    f(eng)

            @block.sync
            def _(eng):
                for f in self.ops['sp']:
                    f(eng)
                for (ws, wv, we, wd) in final:
                    eng.wait_ge(ws, wv)


def build(S, depth=DEPTH, nlayers_run=None, debug=False, skip=()):
    NT = S // 512
    NKB = S // 128
    VCH = min(16, NKB)
    nc = bass.Bass("TRN2", target_bir_lowering=False)
    dt_in = lambda name, shape, dt=F32: nc.dram_tensor(name, shape, dt, kind="ExternalInput").ap()
    xT = dt_in("xT", [D, S])
    w_in = dt_in("w_in", [depth, D, 4096])
    w_out = dt_in("w_out", [depth, D, D])
    w_up = dt_in("w_up", [depth, D, 2 * DFF])
    w_down = dt_in("w_down", [depth, DFF, D])
    lamv = dt_in("lamv", [depth, 128, 4 * HD])
    dgn = dt_in("dgn", [128, depth])
    sgn = dt_in("sgn", [128, depth])
    lnp = dt_in("lnp", [128, depth * 4 * NCH])
    cvw = dt_in("cvw", [128, depth * 3 * 2 * NFF])
    cvb = dt_in("cvb", [128, depth * 2 * NFF])
    cos2 = dt_in("cos2", [128, S])
    sin2 = dt_in("sin2", [128, S])
    outT = nc.dram_tensor("outT", [D, S], F32, kind="ExternalOutput").ap()
    sk = "ExternalOutput" if debug else "Internal"
    qk_s = nc.dram_tensor("qk_s", [2048, S], BF16, kind=sk).ap()
    v_s = nc.dram_tensor("v_s", [S, 1024], BF16, kind=sk).ap()
    mix_s = nc.dram_tensor("mix_s", [D, S], BF16, kind=sk).ap()
    x1_s = nc.dram_tensor("x1_s", [D, S], F32, kind=sk).ap()
    x_s = nc.dram_tensor("x_s", [D, S], F32, kind=sk).ap()

    with ExitStack() as st:
        P = Prog(nc, st)

        uid = [0]

        def sbt(stack, name, shape, dt):
            uid[0] += 1
            return Buf(stack.enter_context(nc.sbuf_tensor(f"{name}_{uid[0]}", shape, dt)))

        banks = [Buf(st.enter_context(nc.psum_tensor(f"bank{i}", [128, 512], F32))) for i in range(8)]
        ident = sbt(st, "ident", [128, 128], BF16)
        maskD = sbt(st, "maskD", [128, 128], BF16)
        maskS = sbt(st, "maskS", [128, 128], BF16)
        uinc = sbt(st, "uinc", [128, 128], F16)
        lstr = sbt(st, "lstr", [128, 128], F16)
        ones_b = sbt(st, "ones_b", [128, 128], BF16)
        om128 = sbt(st, "om128", [128, 128], F32)
        om64 = sbt(st, "om64", [128, 128], F32)
        om1024 = sbt(st, "om1024", [128, 128], F32)
        cst = sbt(st, "cst", [128, 128], F32)
        lam_t = sbt(st, "lam_t", [128, depth * 4 * HD], F32)
        lam_w = sbt(st, "lam_w", [128, 8], F32)
        nlam = sbt(st, "nlam", [128, depth], F32)
        dgn_t = sbt(st, "dgn_t", [128, depth], F32)
        sgn_t = sbt(st, "sgn_t", [128, depth], F32)
        lnp_t = sbt(st, "lnp_t", [128, depth * 4 * NCH], F32)
        cvw_t = sbt(st, "cvw_t", [128, depth * 6 * NFF], F32)
        cvb_t = sbt(st, "cvb_t", [128, depth * 2 * NFF], F32)

        def aff(buf, val, pattern, cmp, fill, base, cm):
            P.op('pool', lambda e: e.memset(buf.t[:], val), wr=[buf])
            P.op('pool', lambda e: e.affine_select(out=buf.t[:], in_=buf.t[:], pattern=pattern, compare_op=cmp,
                                                    fill=fill, base=base, channel_multiplier=cm), wr=[buf])

        def cvt(dst, src):
            P.op('dve', lambda e: e.tensor_copy(out=dst.t[:], in_=src.t[:]), rd=[src], wr=[dst])

        aff(cst, 1.0, [[-1, 128]], ALU.is_equal, 0.0, 0, 1)
        cvt(ident, cst)
        aff(cst, 0.0, [[1, 128]], ALU.is_ge, NEG, 0, -1)
        cvt(maskD, cst)
        aff(cst, 0.0, [[1, 128]], ALU.is_ge, NEG, -1, -1)
        cvt(maskS, cst)
        aff(cst, 1.0, [[-1, 128]], ALU.is_ge, 0.0, 0, 1)
        cvt(uinc, cst)
        aff(cst, 1.0, [[1, 128]], ALU.is_ge, 0.0, -1, -1)
        cvt(lstr, cst)
        P.op('pool', lambda e: e.memset(cst.t[:], 1.0), wr=[cst])
        cvt(ones_b, cst)
        ones_f = sbt(st, "ones_f", [128, 128], F32)
        P.op('pool', lambda e: e.memset(ones_f.t[:], 1.0), wr=[ones_f])
        P.op('pool', lambda e: e.memset(om128.t[:], 1.0 / 128), wr=[om128])
        P.op('pool', lambda e: e.memset(om1024.t[:], 1.0 / 1024), wr=[om1024])
        P.op('pool', lambda e: e.memset(om64.t[:], 0.0), wr=[om64])
        P.op('pool', lambda e: e.memset(om64.t[0:64, 0:64], 1.0 / 64), wr=[om64])
        P.op('pool', lambda e: e.memset(om64.t[64:128, 64:128], 1.0 / 64), wr=[om64])

        for l in range(depth):
            P.op('sp', lambda e, l=l: e.dma_start(out=lam_t.t[:, l * 256:(l + 1) * 256], in_=lamv[l]), wr=[lam_t], dma=True)
        P.op('sp', lambda e: e.dma_start(out=dgn_t.t[:], in_=dgn), wr=[dgn_t], dma=True)
        P.op('sp', lambda e: e.dma_start(out=sgn_t.t[:], in_=sgn), wr=[sgn_t], dma=True)
        P.op('sp', lambda e: e.dma_start(out=lnp_t.t[:], in_=lnp), wr=[lnp_t], dma=True)
        P.op('sp', lambda e: e.dma_start(out=cvw_t.t[:], in_=cvw), wr=[cvw_t], dma=True)
        P.op('sp', lambda e: e.dma_start(out=cvb_t.t[:], in_=cvb), wr=[cvb_t], dma=True)
        for l in range(depth):
            lam_init = 0.8 - 0.6 * math.exp(-0.3 * l)
            b0 = l * 256
            for i in range(2):
                q = lam_t.t[:, b0 + i * 128: b0 + i * 128 + 64]
                k = lam_t.t[:, b0 + i * 128 + 64: b0 + i * 128 + 128]
                P.op('dve', lambda e, q=q, k=k: e.tensor_tensor(out=q, in0=q, in1=k, op=ALU.mult), wr=[lam_t])
                P.op('dve', lambda e, q=q, i=i: e.reduce_sum(out=lam_w.t[:, i:i + 1], in_=q, axis=AX.X), rd=[lam_t], wr=[lam_w])
            P.op('act', lambda e: e.activation(out=lam_w.t[:, 2:4], in_=lam_w.t[:, 0:2], func=AF.Exp), wr=[lam_w])
            P.op('dve', lambda e: e.tensor_tensor(out=lam_w.t[:, 4:5], in0=lam_w.t[:, 3:4], in1=lam_w.t[:, 2:3], op=ALU.subtract), wr=[lam_w])
            P.op('dve', lambda e, l=l, li=lam_init: e.tensor_scalar(out=nlam.t[:, l:l + 1], in0=lam_w.t[:, 4:5], scalar1=-li, scalar2=None, op0=ALU.add),
                 rd=[lam_w], wr=[nlam])
            P.op('dve', lambda e, l=l, li=lam_init: e.tensor_scalar(out=dgn_t.t[:, l:l + 1], in0=dgn_t.t[:, l:l + 1], scalar1=1.0 - li, scalar2=None, op0=ALU.mult),
                 wr=[dgn_t])

        bank_rr = [0]

        def next_bank():
            b = banks[bank_rr[0] % 8]
            bank_rr[0] += 1
            return b

        def load_weight(ph, dst, src_view, nk, ncols, stg):
            CH = 2048
            i = 0
            for kc in range(nk):
                for c0 in range(0, ncols, CH):
                    cw = min(CH, ncols - c0)
                    sg = stg[i % len(stg)]
                    P.op('sp', lambda e, sg=sg, kc=kc, c0=c0, cw=cw: e.dma_start(out=sg.t[:, 0:cw], in_=src_view[:, kc, c0:c0 + cw]),
                         wr=[sg], dma=True)
                    eng = ('pool', 'dve')[i % 2]
                    P.op(eng, lambda e, sg=sg, kc=kc, c0=c0, cw=cw: e.tensor_copy(out=dst.t[:, kc, c0:c0 + cw], in_=sg.t[:, 0:cw]),
                         rd=[sg], wr=[dst])
                    i += 1

        def layer_norm(ph, y, N, gcol, bcol, sqr, msb, rsb, tmp, off=0):
            b1 = next_bank()
            b2 = next_bank()
            for oc in range(NCH):
                sq = sqr[oc % len(sqr)]
                P.op('act', lambda e, sq=sq, oc=oc: e.activation(out=sq.t[:, 0:N], in_=y.t[:, oc, off:off + N], func=AF.Square), rd=[y], wr=[sq])
                P.op('pe', lambda e, oc=oc: e.matmul(b1.t[:, 0:N], lhsT=om1024.t[:], rhs=y.t[:, oc, off:off + N], start=(oc == 0), stop=(oc == NCH - 1)),
                     rd=[y, om1024], wr=[b1])
                P.op('pe', lambda e, sq=sq, oc=oc: e.matmul(b2.t[:, 0:N], lhsT=om1024.t[:], rhs=sq.t[:, 0:N], start=(oc == 0), stop=(oc == NCH - 1)),
                     rd=[sq, om1024], wr=[b2])
            P.op('act', lambda e: e.activation(out=msb.t[:, 0:N], in_=b1.t[:, 0:N], func=AF.Copy), rd=[b1], wr=[msb])
            P.op('act', lambda e: e.activation(out=rsb.t[:, 0:N], in_=b1.t[:, 0:N], func=AF.Square), rd=[b1], wr=[rsb])
            P.op('dve', lambda e: e.tensor_tensor(out=rsb.t[:, 0:N], in0=b2.t[:, 0:N], in1=rsb.t[:, 0:N], op=ALU.subtract), rd=[b2], wr=[rsb])
            P.op('act', lambda e: e.activation(out=rsb.t[:, 0:N], in_=rsb.t[:, 0:N], func=AF.Ln, bias=LN_EPS, scale=1.0), wr=[rsb])
            P.op('act', lambda e: e.activation(out=rsb.t[:, 0:N], in_=rsb.t[:, 0:N], func=AF.Exp, scale=-0.5), wr=[rsb])
            for oc in range(NCH):
                tm = tmp[oc % len(tmp)]
                P.op('pool', lambda e, tm=tm, oc=oc: e.tensor_tensor(out=tm.t[:, 0:N], in0=y.t[:, oc, off:off + N], in1=msb.t[:, 0:N], op=ALU.subtract),
                     rd=[y, msb], wr=[tm])
                P.op('pool', lambda e, tm=tm: e.tensor_tensor(out=tm.t[:, 0:N], in0=tm.t[:, 0:N], in1=rsb.t[:, 0:N], op=ALU.mult), rd=[rsb], wr=[tm])
                P.op('dve', lambda e, tm=tm, oc=oc: e.tensor_scalar(out=y.t[:, oc, off:off + N], in0=tm.t[:, 0:N], scalar1=gcol(oc), scalar2=bcol(oc),
                                                                 op0=ALU.mult, op1=ALU.add), rd=[tm, lnp_t], wr=[y])

        xview = lambda ap: ap.rearrange("(c p) s -> p c s", p=128)

        nl = depth if nlayers_run is None else nlayers_run
        def run_layer(l, x_in, x_out):

            P.barrier()
            with ExitStack() as ph:
              if 'A' not in skip:
                wA = sbt(ph, "wA", [128, NCH, 4096], BF16)
                stg = [sbt(ph, f"stgA{i}", [128, 2048], F32) for i in range(2)]
                xf = [sbt(ph, f"xfA{i}", [128, NCH, 512], F32) for i in range(1)]
                xb = [sbt(ph, f"xbA{i}", [128, NCH, 512], BF16) for i in range(2)]
                cs = [sbt(ph, f"csA{i}", [128, 512], F32) for i in range(2)]
                sn = [sbt(ph, f"snA{i}", [128, 512], F32) for i in range(2)]
                t1 = [sbt(ph, f"t1A{i}", [128, 512], F32) for i in range(3)]
                t2 = [sbt(ph, f"t2A{i}", [128, 512], F32) for i in range(3)]
                ob = [sbt(ph, f"obA{i}", [128, 512], BF16) for i in range(4)]
                load_weight(ph, wA, xview(w_in[l]), NCH, 4096, stg)
                oi = [0]
                for j in range(NT):
                    t0 = j * 512
                    xfj = xf[0]
                    xbj = xb[j % 2]
                    csj = cs[j % 2]
                    snj = sn[j % 2]
                    P.op('sp', lambda e, xfj=xfj, t0=t0: e.dma_start(out=xfj.t[:, :, :], in_=xview(x_in)[:, :, t0:t0 + 512]), wr=[xfj], dma=True)
                    P.op('sp', lambda e, csj=csj, t0=t0: e.dma_start(out=csj.t[:], in_=cos2[:, t0:t0 + 512]), wr=[csj], dma=True)
                    P.op('sp', lambda e, snj=snj, t0=t0: e.dma_start(out=snj.t[:], in_=sin2[:, t0:t0 + 512]), wr=[snj], dma=True)
                    for kc in range(NCH):
                        eng = ('pool', 'dve')[kc % 2]
                        P.op(eng, lambda e, kc=kc, xfj=xfj, xbj=xbj: e.tensor_copy(out=xbj.t[:, kc, :], in_=xfj.t[:, kc, :]), rd=[xfj], wr=[xbj])
                    for c in range(8):
                        b1 = next_bank()
                        b2 = next_bank()
                        for kc in range(NCH):
                            P.op('pe', lambda e, kc=kc, c=c, b1=b1, xbj=xbj: e.matmul(b1.t[:, :], lhsT=wA.t[:, kc, 128 * c:128 * c + 128], rhs=xbj.t[:, kc, :],
                                                                                start=(kc == 0), stop=(kc == NCH - 1)), rd=[wA, xbj], wr=[b1])
                        for kc in range(NCH):
                            P.op('pe', lambda e, kc=kc, c=c, b2=b2, xbj=xbj: e.matmul(b2.t[:, :], lhsT=wA.t[:, kc, 3072 + 128 * c:3072 + 128 * c + 128], rhs=xbj.t[:, kc, :],
                                                                                start=(kc == 0), stop=(kc == NCH - 1)), rd=[wA, xbj], wr=[b2])
                        a1 = t1[oi[0] % 3]
                        a2 = t2[oi[0] % 3]
                        o = ob[oi[0] % 4]
                        oi[0] += 1
                        P.op('dve', lambda e, a1=a1, b1=b1, csj=csj: e.tensor_tensor(out=a1.t[:], in0=b1.t[:, :], in1=csj.t[:], op=ALU.mult), rd=[b1, csj], wr=[a1])
                        P.op('dve', lambda e, a2=a2, b2=b2, snj=snj: e.tensor_tensor(out=a2.t[:], in0=b2.t[:, :], in1=snj.t[:], op=ALU.mult), rd=[b2, snj], wr=[a2])
                        P.op('pool', lambda e, a1=a1, a2=a2, o=o: e.tensor_tensor(out=o.t[:], in0=a1.t[:], in1=a2.t[:], op=ALU.add), rd=[a1, a2], wr=[o])
                        P.op('pool', lambda e, o=o, c=c, t0=t0: e.dma_start(out=qk_s[128 * c:128 * c + 128, t0:t0 + 512], in_=o.t[:]), rd=[o], dma=True)
                    for c in range(8):
                        b1 = next_bank()
                        for kc in range(NCH):
                            P.op('pe', lambda e, kc=kc, c=c, b1=b1, xbj=xbj: e.matmul(b1.t[:, :], lhsT=wA.t[:, kc, 1536 + 128 * c:1536 + 128 * c + 128], rhs=xbj.t[:, kc, :],
                                                                                start=(kc == 0), stop=(kc == NCH - 1)), rd=[wA, xbj], wr=[b1])
                        o = ob[oi[0] % 4]
                        oi[0] += 1
                        P.op('act', lambda e, o=o, b1=b1: e.activation(out=o.t[:], in_=b1.t[:, :], func=AF.Copy), rd=[b1], wr=[o])
                        P.op('pool', lambda e, o=o, c=c, t0=t0: e.dma_start(out=qk_s[1024 + 128 * c:1024 + 128 * c + 128, t0:t0 + 512], in_=o.t[:]), rd=[o], dma=True)
                    for tb in range(4):
                        for half, wc0 in ((0, 1024), (1, 2560)):
                            b1 = next_bank()
                            for kc in range(NCH):
                                P.op('pe', lambda e, kc=kc, b1=b1, xbj=xbj, tb=tb, wc0=wc0: e.matmul(b1.t[:, :], lhsT=xbj.t[:, kc, 128 * tb:128 * tb + 128], rhs=wA.t[:, kc, wc0:wc0 + 512],
                                                                                            start=(kc == 0), stop=(kc == NCH - 1)), rd=[wA, xbj], wr=[b1])
                            o = ob[oi[0] % 4]
                            oi[0] += 1
                            P.op('act', lambda e, o=o, b1=b1: e.activation(out=o.t[:], in_=b1.t[:, :], func=AF.Copy), rd=[b1], wr=[o])
                            P.op('pool', lambda e, o=o, tb=tb, half=half, t0=t0: e.dma_start(out=v_s[t0 + 128 * tb:t0 + 128 * tb + 128, 512 * half:512 * half + 512], in_=o.t[:]),
                                 rd=[o], dma=True)

            P.barrier()
            with ExitStack() as ph:
                vview = v_s.rearrange("(kb p) f -> p kb f", p=128)
                sets = []
                for si in range(2):
                    kp = [sbt(ph, "KTa", [128, S], BF16), sbt(ph, "KTb", [128, S], BF16)]
                    P.op('pool', lambda e, kp=kp: e.memset(kp[0].t[64:128, :], 0.0), wr=[kp[0]])
                    P.op('pool', lambda e, kp=kp: e.memset(kp[1].t[0:64, :], 0.0), wr=[kp[1]])
                    sets.append((kp, sbt(ph, "QT", [128, S], BF16), sbt(ph, "VV", [128, NKB, 128], BF16)))
                units = ([('d', h) for h in range(0 if 'Bd' in skip else 4)] + [('s', pp) for pp in range(0 if 'Bs' in skip else 4)])

                def load_unit(ui):
                    kind, idx = units[ui]
                    kp, qt, vv = sets[ui % 2]
                    if kind == 'd':
                        kr, qr, vc = 512 + 128 * idx, 128 * idx, 128 * idx
                    else:
                        kr, qr, vc = 1536 + 128 * idx, 1024 + 128 * idx, 512 + 128 * idx
                    P.op('sp', lambda e: e.dma_start(out=kp[0].t[0:64, :], in_=qk_s[kr:kr + 64, :]), wr=[kp[0]], dma=True)
                    P.op('sp', lambda e: e.dma_start(out=kp[1].t[64:128, :], in_=qk_s[kr + 64:kr + 128, :]), wr=[kp[1]], dma=True)
                    P.op('sp', lambda e: e.dma_start(out=qt.t[:], in_=qk_s[qr:qr + 128, :]), wr=[qt], dma=True)
                    for q4 in range(0, NKB, VCH):
                        P.op('sp', lambda e, q4=q4: e.dma_start(out=vv.t[:, q4:q4 + VCH, :], in_=vview[:, q4:q4 + VCH, vc:vc + 128]), wr=[vv], dma=True)
                Er = [sbt(ph, f"Er{i}", [128, 512], F32) for i in range(4)]
                SPr = [sbt(ph, f"SPr{i}", [128, 512], F16) for i in range(4)]
                Fr = [sbt(ph, f"Fr{i}", [128, 512], F32) for i in range(3)]
                Ar = [sbt(ph, f"Ar{i}", [128, 512], BF16) for i in range(4)]
                fa = [sbt(ph, f"fa{i}", [128, 512], F32) for i in range(4)]
                fo = [sbt(ph, f"fo{i}", [128, 512], BF16) for i in range(2)]
                fcnt = [0]

                def rms_finish(o_sb, omat, gcolap, eps, row0, t0):
                    sq = fa[3]
                    rb = banks[7]
                    ofin = fo[fcnt[0] % 2]
                    fcnt[0] += 1
                    P.op('act', lambda e: e.activation(out=sq.t[:], in_=o_sb.t[:], func=AF.Square), rd=[o_sb], wr=[sq])
                    P.op('pe', lambda e: e.matmul(rb.t[:, :], lhsT=omat.t[:], rhs=sq.t[:], start=True, stop=True), rd=[sq, omat], wr=[rb])
                    P.op('act', lambda e: e.activation(out=sq.t[:], in_=rb.t[:, :], func=AF.Ln, bias=eps, scale=1.0), rd=[rb], wr=[sq])
                    P.op('act', lambda e: e.activation(out=sq.t[:], in_=sq.t[:], func=AF.Exp, scale=-0.5), wr=[sq])
                    P.op('dve', lambda e: e.tensor_tensor(out=o_sb.t[:], in0=o_sb.t[:], in1=sq.t[:], op=ALU.mult), rd=[sq], wr=[o_sb])
                    P.op('dve', lambda e: e.tensor_scalar(out=ofin.t[:], in0=o_sb.t[:], scalar1=gcolap, scalar2=None, op0=ALU.mult), rd=[o_sb, dgn_t, sgn_t], wr=[ofin])
                    P.op('pool', lambda e: e.dma_start(out=mix_s[row0:row0 + 128, t0:t0 + 512], in_=ofin.t[:]), rd=[ofin], dma=True)

                if units:
                    load_unit(0)
                for ui, (kind, h) in enumerate(units):
                    if kind != 'd':
                        continue
                    if ui + 1 < len(units):
                        load_unit(ui + 1)
                    KTp, QT, VV = sets[ui % 2]
                    steps = [(j, c, kb) for j in range(NT) for c in range(2) for kb in range(4 * j + 4)]
                    pz_of = {}

                    def d_qk(i):
                        j, c, kb = steps[i]
                        pz = banks[6 + i % 2]
                        pz_of[i] = pz
                        KT = KTp[c]
                        QTl = QT
                        kT = KT.t[:, kb * 128:(kb + 1) * 128]
                        q0 = j * 512
                        if kb >= 4 * j:
                            c0 = 128 * (kb - 4 * j)
                            P.op('pe', lambda e: e.matmul(pz.t[:, c0:512], lhsT=kT, rhs=QTl.t[:, q0 + c0:q0 + 512], start=True, stop=False), rd=[KT, QT], wr=[pz])
                            P.op('pe', lambda e: e.matmul(pz.t[:, c0:c0 + 128], lhsT=ident.t[:], rhs=maskD.t[:], start=False, stop=True), rd=[ident, maskD], wr=[pz])
                        else:
                            P.op('pe', lambda e: e.matmul(pz.t[:, :], lhsT=kT, rhs=QTl.t[:, q0:q0 + 512], start=True, stop=True), rd=[KT, QT], wr=[pz])

                    def d_exp(i):
                        j, c, kb = steps[i]
                        pz = pz_of[i]
                        c0 = max(0, 128 * (kb - 4 * j))
                        A = Ar[i % 4]
                        P.op('act', lambda e: e.activation(out=A.t[:, c0:512], in_=pz.t[:, c0:512], func=AF.Exp, scale=0.125), rd=[pz], wr=[A])

                    def d_pv(i):
                        j, c, kb = steps[i]
                        c0 = max(0, 128 * (kb - 4 * j))
                        A = Ar[i % 4]
                        pr = (2 * j + c) % 3
                        po = banks[2 * pr]
                        pl = banks[2 * pr + 1]
                        stt = (kb == 0)
                        VVl = VV
                        P.op('pe', lambda e: e.matmul(po.t[:, c0:512], lhsT=VVl.t[:, kb, :], rhs=A.t[:, c0:512], start=stt, stop=True, skip_group_check=True), rd=[VV, A], wr=[po])
                        P.op('pe', lambda e: e.matmul(pl.t[:, c0:512], lhsT=ones_b.t[:], rhs=A.t[:, c0:512], start=stt, stop=True, skip_group_check=True), rd=[ones_b, A], wr=[pl])
                        if c == 1 and kb == 4 * j + 3:
                            p0 = (2 * j) % 3
                            po0, pl0 = banks[2 * p0], banks[2 * p0 + 1]
                            r0, o0, r1 = fa[0], fa[1], fa[2]
                            P.op('dve', lambda e: e.reciprocal(out=r0.t[:], in_=pl0.t[:, :]), rd=[pl0], wr=[r0])
                            P.op('dve', lambda e: e.tensor_tensor(out=o0.t[:], in0=po0.t[:, :], in1=r0.t[:], op=ALU.mult), rd=[po0, r0], wr=[o0])
                            P.op('dve', lambda e: e.reciprocal(out=r1.t[:], in_=pl.t[:, :]), rd=[pl], wr=[r1])
                            P.op('dve', lambda e: e.tensor_tensor(out=r1.t[:], in0=po.t[:, :], in1=r1.t[:], op=ALU.mult), rd=[po], wr=[r1])
                            P.op('dve', lambda e: e.scalar_tensor_tensor(out=o0.t[:], in0=r1.t[:], scalar=nlam.t[:, l:l + 1], in1=o0.t[:], op0=ALU.mult, op1=ALU.add),
                                 rd=[r1, nlam], wr=[o0])
                            rms_finish(o0, om128, dgn_t.t[:, l:l + 1], RMS_EPS, 128 * h, j * 512)

                    n = len(steps)
                    for t in range(n + 2):
                        if t < n:
                            d_qk(t)
                            d_exp(t)
                        if t >= 2:
                            d_pv(t - 2)

                for ui, (kind, pp) in enumerate(units):
                    if kind != 's':
                        continue
                    if ui + 1 < len(units):
                        load_unit(ui + 1)
                    KTp, QT, VV = sets[ui % 2]
                    steps = [(j, kb, hh) for j in range(NT) for kb in range(4 * j + 3, -1, -1) for hh in range(2)]
                    pz_of = {}

                    def geom(i):
                        j, kb, hh = steps[i]
                        diag = kb >= 4 * j
                        c0 = 128 * (kb - 4 * j) if diag else 0
                        return j, kb, hh, diag, c0

                    def s_qk(i):
                        j, kb, hh, diag, c0 = geom(i)
                        pz = banks[4 + i % 3]
                        pz_of[i] = pz
                        KT = KTp[hh]
                        QTl = QT
                        kT = KT.t[:, kb * 128:(kb + 1) * 128]
                        q0 = j * 512
                        if diag:
                            P.op('pe', lambda e: e.matmul(pz.t[:, c0:512], lhsT=kT, rhs=QTl.t[:, q0 + c0:q0 + 512], start=True, stop=False), rd=[KT, QT], wr=[pz])
                            P.op('pe', lambda e: e.matmul(pz.t[:, c0:c0 + 128], lhsT=ident.t[:], rhs=maskS.t[:], start=False, stop=True), rd=[ident, maskS], wr=[pz])
                        else:
                            P.op('pe', lambda e: e.matmul(pz.t[:, :], lhsT=kT, rhs=QTl.t[:, q0:q0 + 512], start=True, stop=True), rd=[KT, QT], wr=[pz])

                    def s_act1(i):
                        j, kb, hh, diag, c0 = geom(i)
                        pz = pz_of[i]
                        E = Er[i % 4]
                        SP = SPr[i % 4]
                        P.op('act', lambda e: e.activation(out=E.t[:, c0:512], in_=pz.t[:, c0:512], func=AF.Exp, scale=0.125), rd=[pz], wr=[E])

                    def s_act1b(i):
                        j, kb, hh, diag, c0 = geom(i)
                        E = Er[i % 4]
                        SP = SPr[i % 4]
                        P.op('act', lambda e: e.activation(out=SP.t[:, c0:512], in_=E.t[:, c0:512], func=AF.Ln, bias=1.0, scale=1.0), rd=[E], wr=[SP])

                    def s_u(i):
                        j, kb, hh, diag, c0 = geom(i)
                        SP = SPr[i % 4]
                        X = banks[hh]
                        if kb == 4 * j + 3:
                            P.op('dve', lambda e: e.memset(X.t[:, :], 0.0), wr=[X])
                        P.op('pe', lambda e: e.matmul(X.t[:, c0:512], lhsT=uinc.t[:], rhs=SP.t[:, c0:512], start=False, stop=True, skip_group_check=True), rd=[uinc, SP], wr=[X])

                    def s_act2(i):
                        j, kb, hh, diag, c0 = geom(i)
                        X = banks[hh]
                        Fb = Fr[i % 3]
                        P.op('act', lambda e: e.activation(out=Fb.t[:, c0:512], in_=X.t[:, c0:512], func=AF.Exp, scale=-1.0), rd=[X], wr=[Fb])

                    def s_mul(i):
                        j, kb, hh, diag, c0 = geom(i)
                        E = Er[i % 4]
                        Fb = Fr[i % 3]
                        A = Ar[i % 4]
                        P.op('dve', lambda e: e.tensor_tensor(out=A.t[:, c0:512], in0=E.t[:, c0:512], in1=Fb.t[:, c0:512], op=ALU.mult), rd=[E, Fb], wr=[A])

                    def s_lpv(i):
                        j, kb, hh, diag, c0 = geom(i)
                        SP = SPr[i % 4]
                        A = Ar[i % 4]
                        X = banks[hh]
                        po = banks[2 + hh]
                        VVl = VV
                        if kb > 0:
                            P.op('pe', lambda e: e.matmul(X.t[:, c0:512], lhsT=lstr.t[:], rhs=SP.t[:, c0:512], start=False, stop=True, skip_group_check=True), rd=[lstr, SP], wr=[X])
                        if kb == 4 * j + 3:
                            P.op('dve', lambda e: e.memset(po.t[:, :], 0.0), wr=[po])
                        P.op('pe', lambda e: e.matmul(po.t[:, c0:512], lhsT=VVl.t[:, kb, :], rhs=A.t[:, c0:512], start=False, stop=True, skip_group_check=True), rd=[VV, A], wr=[po])
                        if kb == 0 and hh == 1:
                            o0 = fa[0]
                            pa, pb = banks[2], banks[3]
                            P.op('act', lambda e: e.activation(out=o0.t[0:64, :], in_=pa.t[0:64, :], func=AF.Copy), rd=[pa], wr=[o0])
                            P.op('dve', lambda e: e.tensor_copy(out=o0.t[64:128, :], in_=pb.t[64:128, :]), rd=[pb], wr=[o0])
                            rms_finish(o0, om64, sgn_t.t[:, l:l + 1], RMS_EPS, 512 + 128 * pp, j * 512)

                    n = len(steps)
                    for t in range(n + 2):
                        if t < n:
                            s_qk(t)
                            s_act1(t)
                        if 1 <= t <= n:
                            s_u(t - 1)
                            s_act2(t - 1)
                        if t < n:
                            s_act1b(t)
                        if 1 <= t <= n:
                            s_mul(t - 1)
                        if t >= 2:
                            s_lpv(t - 2)

            P.barrier()
            with ExitStack() as ph:
                wO = sbt(ph, "wO", [128, NCH, D], BF16)
                stg = [sbt(ph, f"stgC{i}", [128, 2048], F32) for i in range(2)]
                mx = [sbt(ph, f"mxC{i}", [128, NCH, 512], BF16) for i in range(2)]
                yy = [sbt(ph, f"yyC{i}", [128, NCH, 512], F32) for i in range(2)]
                sqr = [sbt(ph, f"sqC{i}", [128, 512], F32) for i in range(2)]
                tmp = [sbt(ph, f"tmC{i}", [128, 512], F32) for i in range(2)]
                msb = sbt(ph, "msbC", [128, 512], F32)
                rsb = sbt(ph, "rsbC", [128, 512], F32)
                load_weight(ph, wO, xview(w_out[l]), NCH, D, stg)
                g0 = l * 4 * NCH
                for j in range(0 if 'C1' in skip else NT):
                    t0 = j * 512
                    m = mx[j % 2]
                    y = yy[j % 2]
                    P.op('sp', lambda e, m=m, t0=t0: e.dma_start(out=m.t[:, :, :], in_=xview(mix_s)[:, :, t0:t0 + 512]), wr=[m], dma=True)
                    P.op('sp', lambda e, y=y, t0=t0: e.dma_start(out=y.t[:, :, :], in_=xview(x_in)[:, :, t0:t0 + 512]), wr=[y], dma=True)
                    for oc in range(NCH):
                        b1 = next_bank()
                        for kc in range(NCH):
                            P.op('pe', lambda e, kc=kc, oc=oc, b1=b1, m=m: e.matmul(b1.t[:, :], lhsT=wO.t[:, kc, 128 * oc:128 * oc + 128], rhs=m.t[:, kc, :],
                                                                               start=(kc == 0), stop=(kc == NCH - 1)), rd=[wO, m], wr=[b1])
                        P.op('dve', lambda e, oc=oc, b1=b1, y=y: e.scalar_tensor_tensor(out=y.t[:, oc, :], in0=y.t[:, oc, :], scalar=ALPHA, in1=b1.t[:, :],
                                                                                   op0=ALU.mult, op1=ALU.add), rd=[b1], wr=[y])
                    layer_norm(ph, y, 512, lambda oc, g0=g0: lnp_t.t[:, g0 + oc:g0 + oc + 1], lambda oc, g0=g0: lnp_t.t[:, g0 + NCH + oc:g0 + NCH + oc + 1],
                               sqr, msb, rsb, tmp)
                    P.op('pool', lambda e, y=y, t0=t0: e.dma_start(out=xview(x1_s)[:, :, t0:t0 + 512], in_=y.t[:, :, :]), rd=[y], dma=True)

            P.barrier()
            with ExitStack() as ph:
                NF = 256
                wU = sbt(ph, "wU", [128, NCH, 2 * DFF], BF16)
                wD = sbt(ph, "wD", [128, NFF, D], BF16)
                stg = [sbt(ph, f"stgF{i}", [128, 2048], F32) for i in range(2)]
                xh = [sbt(ph, f"xhF{i}", [128, NCH, NF + 2], F32) for i in range(2)]
                xbh = [sbt(ph, f"xbF{i}", [128, NCH, NF + 2], BF16) for i in range(2)]
                gT = sbt(ph, "gT", [128, NFF, NF], BF16)
                ua = [sbt(ph, f"uaF{i}", [128, NF], F32) for i in range(2)]
                ub = [sbt(ph, f"ubF{i}", [128, NF], F32) for i in range(2)]
                sa = [sbt(ph, f"saF{i}", [128, NF], F32) for i in range(2)]
                sqr = [sbt(ph, f"sqF{i}", [128, NF], F32) for i in range(2)]
                tmp = [sbt(ph, f"tmF{i}", [128, NF], F32) for i in range(2)]
                msb = sbt(ph, "msbF", [128, NF], F32)
                rsb = sbt(ph, "rsbF", [128, NF], F32)
                load_weight(ph, wU, xview(w_up[l]), NCH, 2 * DFF, stg)
                load_weight(ph, wD, xview(w_down[l]), NFF, D, stg)
                g0 = l * 4 * NCH + 2 * NCH
                cw0 = l * 6 * NFF
                cb0 = l * 2 * NFF
                ui = [0]
                for j in range(0 if 'C2' in skip else S // NF):
                    t0 = j * NF
                    x1 = xh[j % 2]
                    x1b = xbh[j % 2]
                    if j == 0:
                        P.op('pool', lambda e, x1=x1: e.memset(x1.t[:, :, 0:2], 0.0), wr=[x1])
                        P.op('sp', lambda e, x1=x1: e.dma_start(out=x1.t[:, :, 2:NF + 2], in_=xview(x1_s)[:, :, 0:NF]), wr=[x1], dma=True)
                    else:
                        P.op('sp', lambda e, x1=x1, t0=t0: e.dma_start(out=x1.t[:, :, :], in_=xview(x1_s)[:, :, t0 - 2:t0 + NF]), wr=[x1], dma=True)
                    for kc in range(NCH):
                        eng = ('pool', 'dve')[kc % 2]
                        P.op(eng, lambda e, kc=kc, x1=x1, x1b=x1b: e.tensor_copy(out=x1b.t[:, kc, :], in_=x1.t[:, kc, :]), rd=[x1], wr=[x1b])
                    for i in range(NFF):
                        res = []
                        for half in range(2):
                            ch = half * NFF + i
                            b1 = next_bank()
                            for kc in range(NCH):
                                P.op('pe', lambda e, kc=kc, ch=ch, b1=b1, x1b=x1b: e.matmul(b1.t[:, 0:NF + 2], lhsT=wU.t[:, kc, 128 * ch:128 * ch + 128], rhs=x1b.t[:, kc, :],
                                                                                       start=(kc == 0), stop=(kc == NCH - 1)), rd=[wU, x1b], wr=[b1])
                            u = (ua, ub)[half][ui[0] % 2]
                            w = lambda tap, ch=ch: cvw_t.t[:, cw0 + tap * 2 * NFF + ch:cw0 + tap * 2 * NFF + ch + 1]
                            bcol = cvb_t.t[:, cb0 + ch:cb0 + ch + 1]
                            P.op('act', lambda e, u=u, b1=b1, w=w, bcol=bcol: e.activation(out=u.t[:], in_=b1.t[:, 2:NF + 2], func=AF.Identity, bias=bcol, scale=w(2)),
                                 rd=[b1, cvw_t, cvb_t], wr=[u])
                            P.op('dve', lambda e, u=u, b1=b1, w=w: e.scalar_tensor_tensor(out=u.t[:], in0=b1.t[:, 1:NF + 1], scalar=w(1), in1=u.t[:], op0=ALU.mult, op1=ALU.add),
                                 rd=[b1, cvw_t], wr=[u])
                            P.op('dve', lambda e, u=u, b1=b1, w=w: e.scalar_tensor_tensor(out=u.t[:], in0=b1.t[:, 0:NF], scalar=w(0), in1=u.t[:], op0=ALU.mult, op1=ALU.add),
                                 rd=[b1, cvw_t], wr=[u])
                            res.append(u)
                        s = sa[ui[0] % 2]
                        ui[0] += 1
                        P.op('act', lambda e, s=s, u=res[0]: e.activation(out=s.t[:], in_=u.t[:], func=AF.Silu), rd=[res[0]], wr=[s])
                        P.op('pool', lambda e, s=s, u=res[1], i=i: e.tensor_tensor(out=gT.t[:, i, :], in0=s.t[:], in1=u.t[:], op=ALU.mult), rd=[s, res[1]], wr=[gT])
                    for oc in range(NCH):
                        b1 = next_bank()
                        for i in range(NFF):
                            P.op('pe', lambda e, i=i, oc=oc, b1=b1: e.matmul(b1.t[:, 0:NF], lhsT=wD.t[:, i, 128 * oc:128 * oc + 128], rhs=gT.t[:, i, :],
                                                                        start=(i == 0), stop=(i == NFF - 1)), rd=[wD, gT], wr=[b1])
                        P.op('dve', lambda e, oc=oc, b1=b1, x1=x1: e.scalar_tensor_tensor(out=x1.t[:, oc, 2:NF + 2], in0=x1.t[:, oc, 2:NF + 2], scalar=ALPHA, in1=b1.t[:, 0:NF],
                                                                                     op0=ALU.mult, op1=ALU.add), rd=[b1], wr=[x1])
                    layer_norm(ph, x1, NF, lambda oc, g0=g0: lnp_t.t[:, g0 + oc:g0 + oc + 1], lambda oc, g0=g0: lnp_t.t[:, g0 + NCH + oc:g0 + NCH + oc + 1],
                               sqr, msb, rsb, tmp, off=2)
                    P.op('pool', lambda e, x1=x1, t0=t0: e.dma_start(out=xview(x_out)[:, :, t0:t0 + NF], in_=x1.t[:, :, 2:NF + 2]), rd=[x1], dma=True)

        for l in range(nl):
            run_layer(l, xT if l == 0 else x_s, outT if l == nl - 1 else x_s)
        P.emit()
        nops = P.nops
    return nc, nops


def _host_prep(inp, S):
    depth = DEPTH
    f = np.float32
    w_in = np.asarray(inp["w_in"], f)
    idx = np.arange(1024)
    base = (idx // 64) * 64
    off = idx % 64
    perm = base + (off + 32) % 64
    w_ext = np.concatenate([w_in, w_in[:, :, perm]], axis=2)
    lamv = np.concatenate([inp["lam_q1"], inp["lam_k1"], inp["lam_q2"], inp["lam_k2"]], axis=1).astype(f)
    lamv = np.ascontiguousarray(np.broadcast_to(lamv[:, None, :], (depth, 128, 256)))
    dgn = np.ascontiguousarray(np.asarray(inp["diff_norm_g"], f).T)
    sgn = np.ascontiguousarray(np.tile(np.asarray(inp["sb_norm_g"], f), (1, 2)).T)
    cols = []
    for l in range(depth):
        for nm in ("ln1_g", "ln1_b", "ln2_g", "ln2_b"):
            cols.append(np.asarray(inp[nm][l], f).reshape(NCH, 128).T)
    lnp = np.ascontiguousarray(np.concatenate(cols, axis=1))
    cw = []
    cb = []
    for l in range(depth):
        for tap in range(3):
            cw.append(np.asarray(inp["conv_w"][l, tap], f).reshape(2 * NFF, 128).T)
        cb.append(np.asarray(inp["conv_b"][l], f).reshape(2 * NFF, 128).T)
    cvw = np.ascontiguousarray(np.concatenate(cw, axis=1))
    cvb = np.ascontiguousarray(np.concatenate(cb, axis=1))
    inv = (1.0 / (np.float32(10000.0) ** (np.arange(0, HD, 2, dtype=np.float32) / np.float32(HD)))).astype(f)
    ang = (np.arange(S, dtype=f)[None, :] * inv[:, None]).astype(f)
    c = np.cos(ang).astype(f)
    s = np.sin(ang).astype(f)
    cos2 = np.ascontiguousarray(np.tile(c, (4, 1)))
    sin2 = np.ascontiguousarray(np.concatenate([-s, s, -s, s], axis=0))
    shared = dict(w_in=np.ascontiguousarray(w_ext), w_out=np.asarray(inp["w_out"], f), w_up=np.asarray(inp["w_up"], f),
                  w_down=np.asarray(inp["w_down"], f), lamv=lamv, dgn=dgn, sgn=sgn, lnp=lnp, cvw=cvw, cvb=cvb,
                  cos2=cos2, sin2=sin2)
    return shared


def kernel(**inputs):
    x = np.asarray(inputs["x"], np.float32)
    B, S, _ = x.shape
    shared = _host_prep(inputs, S)
    nc, _ = build(S)
    in_maps = []
    for c in range(8):
        m = dict(shared)
        m["xT"] = np.ascontiguousarray(x[c % B].T)
        in_maps.append(m)
    res = run_bass_kernel_spmd(nc, in_maps, core_ids=list(range(8)))
    out = np.stack([np.ascontiguousarray(res.results[b]["outT"].T) for b in range(B)], axis=0)
    return out.astype(np.float32)
```

```python
import math
from contextlib import ExitStack

import numpy as np
import concourse.bass as bass
import concourse.mybir as mybir
from concourse.bass_utils import run_bass_kernel_spmd

F32 = mybir.dt.float32
BF16 = mybir.dt.bfloat16
F16 = mybir.dt.float16
AF = mybir.ActivationFunctionType
ALU = mybir.AluOpType
AX = mybir.AxisListType

D = 1024
HD = 64
DFF = 2816
NCH = D // 128
NFF = DFF // 128
DEPTH = 2
ALPHA = (2 * DEPTH) ** 0.25
LN_EPS = 1e-5
RMS_EPS = 1e-6
NEG = -30000.0
SAFE_SAME_ENGINE = True
ENG = ['pe', 'act', 'dve', 'pool', 'sp']
NDS = 8


class Buf:
    __slots__ = ("t", "wr", "rd")

    def __init__(self, t):
        self.t = t
        self.wr = None
        self.rd = {}


class Prog:
    def __init__(self, nc, stack):
        self.nc = nc
        self.stack = stack
        self.ops = {e: [] for e in ENG}
        self.cur = {}
        self.cnt = {}
        self.seen = {e: {} for e in ENG}
        self.pending = {e: [] for e in ENG}
        self.nsem = 0
        self.dsem = {}
        self.dcnt = {}
        self.dn = {}
        for e in ENG:
            self._newsem(e)
        for e in ('sp', 'pool', 'act'):
            self.dsem[e] = [self._mk(f"d_{e}{i}") for i in range(NDS)]
            self.dcnt[e] = [0] * NDS
            self.dn[e] = 0
        self.nops = 0

    def _mk(self, name):
        self.nsem += 1
        return self.stack.enter_context(self.nc.semaphore(name))

    def _newsem(self, e):
        self.cur[e] = self._mk(f"s_{e}_{self.nsem}")
        self.cnt[e] = 0

    def op(self, e, fn, rd=(), wr=(), waits=(), dma=False):
        evs = [w for w in waits if w is not None]
        for b in rd:
            if b.wr is not None:
                evs.append(b.wr)
        for b in wr:
            if b.wr is not None:
                evs.append(b.wr)
            evs.extend(b.rd.values())
        if self.pending[e]:
            evs.extend(self.pending[e])
            self.pending[e] = []
        if dma:
            r = self.dn[e] % NDS
            self.dn[e] += 1
            sem = self.dsem[e][r]
            if self.dcnt[e][r] > 0:
                evs.append((sem, self.dcnt[e][r], e, True))
            self.dcnt[e][r] += 16
            val = self.dcnt[e][r]
            inc = 16
        else:
            if self.cnt[e] > 30000:
                self._newsem(e)
            sem = self.cur[e]
            self.cnt[e] += 1
            val = self.cnt[e]
            inc = 1
        wl = []
        seen = self.seen[e]
        for (ws, wv, we, wd) in evs:
            if we == e and not wd and (e == 'pe' or not SAFE_SAME_ENGINE):
                continue
            k = id(ws)
            if seen.get(k, -1) >= wv:
                continue
            seen[k] = wv
            wl.append((ws, wv))

        def run(eng, fn=fn, wl=wl, sem=sem, inc=inc, fuse=(not dma)):
            if fuse and wl:
                for ws, wv in wl[:-1]:
                    eng.wait_ge(ws, wv)
                fn(eng)._wait_ge(wl[-1][0], wl[-1][1]).then_inc(sem, inc)
            else:
                for ws, wv in wl:
                    eng.wait_ge(ws, wv)
                fn(eng).then_inc(sem, inc)
        self.ops[e].append(run)
        self.nops += 1
        ev = (sem, val, e, dma)
        for b in rd:
            k = id(sem)
            o = b.rd.get(k)
            if o is None or o[1] < val:
                b.rd[k] = ev
        for b in wr:
            b.wr = ev
            b.rd = {}
        return ev

    def last_events(self):
        evs = []
        for e in ENG:
            if self.cnt[e] > 0:
                evs.append((self.cur[e], self.cnt[e], e, False))
        for e in self.dsem:
            for r in range(NDS):
                if self.dcnt[e][r] > 0:
                    evs.append((self.dsem[e][r], self.dcnt[e][r], e, True))
        return evs

    def barrier(self):
        evs = self.last_events()
        for e in ENG:
            self.pending[e] = list(evs)

    def emit(self):
        nc = self.nc
        final = self.last_events()
        with nc.Block() as block:
            @block.tensor
            def _(eng):
                for f in self.ops['pe']:
                    f(eng)

            @block.scalar
            def _(eng):
                for f in self.ops['act']:
                    f(eng)

            @block.vector
            def _(eng):
                for f in self.ops['dve']:
                    f(eng)

            @block.gpsimd
            def _(eng):
                for f in self.ops['pool']:
                    f(eng)

            @block.sync
            def _(eng):
                for f in self.ops['sp']:
                    f(eng)
                for (ws, wv, we, wd) in final:
                    eng.wait_ge(ws, wv)


def build(S, depth=DEPTH, nlayers_run=None, debug=False, skip=()):
    NT = S // 512
    NKB = S // 128
    VCH = min(16, NKB)
    nc = bass.Bass("TRN2", target_bir_lowering=False)
    dt_in = lambda name, shape, dt=F32: nc.dram_tensor(name, shape, dt, kind="ExternalInput").ap()
    xT = dt_in("xT", [D, S])
    w_in = dt_in("w_in", [depth, D, 4096])
    w_out = dt_in("w_out", [depth, D, D])
    w_up = dt_in("w_up", [depth, D, 2 * DFF])
    w_down = dt_in("w_down", [depth, DFF, D])
    lamv = dt_in("lamv", [depth, 128, 4 * HD])
    dgn = dt_in("dgn", [128, depth])
    sgn = dt_in("sgn", [128, depth])
    lnp = dt_in("lnp", [128, depth * 4 * NCH])
    cvw = dt_in("cvw", [128, depth * 3 * 2 * NFF])
    cvb = dt_in("cvb", [128, depth * 2 * NFF])
    cos2 = dt_in("cos2", [128, S])
    sin2 = dt_in("sin2", [128, S])
    outT = nc.dram_tensor("outT", [D, S], F32, kind="ExternalOutput").ap()
    sk = "ExternalOutput" if debug else "Internal"
    qk_s = nc.dram_tensor("qk_s", [2048, S], BF16, kind=sk).ap()
    v_s = nc.dram_tensor("v_s", [S, 1024], BF16, kind=sk).ap()
    mix_s = nc.dram_tensor("mix_s", [D, S], BF16, kind=sk).ap()
    x1_s = nc.dram_tensor("x1_s", [D, S], F32, kind=sk).ap()
    x_s = nc.dram_tensor("x_s", [D, S], F32, kind=sk).ap()

    with ExitStack() as st:
        P = Prog(nc, st)

        uid = [0]

        def sbt(stack, name, shape, dt):
            uid[0] += 1
            return Buf(stack.enter_context(nc.sbuf_tensor(f"{name}_{uid[0]}", shape, dt)))

        banks = [Buf(st.enter_context(nc.psum_tensor(f"bank{i}", [128, 512], F32))) for i in range(8)]
        ident = sbt(st, "ident", [128, 128], BF16)
        maskD = sbt(st, "maskD", [128, 128], BF16)
        maskS = sbt(st, "maskS", [128, 128], BF16)
        uinc = sbt(st, "uinc", [128, 128], F16)
        lstr = sbt(st, "lstr", [128, 128], F16)
        ones_b = sbt(st, "ones_b", [128, 128], BF16)
        om128 = sbt(st, "om128", [128, 128], F32)
        om64 = sbt(st, "om64", [128, 128], F32)
        om1024 = sbt(st, "om1024", [128, 128], F32)
        cst = sbt(st, "cst", [128, 128], F32)
        lam_t = sbt(st, "lam_t", [128, depth * 4 * HD], F32)
        lam_w = sbt(st, "lam_w", [128, 8], F32)
        nlam = sbt(st, "nlam", [128, depth], F32)
        dgn_t = sbt(st, "dgn_t", [128, depth], F32)
        sgn_t = sbt(st, "sgn_t", [128, depth], F32)
        lnp_t = sbt(st, "lnp_t", [128, depth * 4 * NCH], F32)
        cvw_t = sbt(st, "cvw_t", [128, depth * 6 * NFF], F32)
        cvb_t = sbt(st, "cvb_t", [128, depth * 2 * NFF], F32)

        def aff(buf, val, pattern, cmp, fill, base, cm):
            P.op('pool', lambda e: e.memset(buf.t[:], val), wr=[buf])
            P.op('pool', lambda e: e.affine_select(out=buf.t[:], in_=buf.t[:], pattern=pattern, compare_op=cmp,
                                                    fill=fill, base=base, channel_multiplier=cm), wr=[buf])

        def cvt(dst, src):
            P.op('dve', lambda e: e.tensor_copy(out=dst.t[:], in_=src.t[:]), rd=[src], wr=[dst])

        aff(cst, 1.0, [[-1, 128]], ALU.is_equal, 0.0, 0, 1)
        cvt(ident, cst)
        aff(cst, 0.0, [[1, 128]], ALU.is_ge, NEG, 0, -1)
        cvt(maskD, cst)
        aff(cst, 0.0, [[1, 128]], ALU.is_ge, NEG, -1, -1)
        cvt(maskS, cst)
        aff(cst, 1.0, [[-1, 128]], ALU.is_ge, 0.0, 0, 1)
        cvt(uinc, cst)
        aff(cst, 1.0, [[1, 128]], ALU.is_ge, 0.0, -1, -1)
        cvt(lstr, cst)
        P.op('pool', lambda e: e.memset(cst.t[:], 1.0), wr=[cst])
        cvt(ones_b, cst)
        ones_f = sbt(st, "ones_f", [128, 128], F32)
        P.op('pool', lambda e: e.memset(ones_f.t[:], 1.0), wr=[ones_f])
        P.op('pool', lambda e: e.memset(om128.t[:], 1.0 / 128), wr=[om128])
        P.op('pool', lambda e: e.memset(om1024.t[:], 1.0 / 1024), wr=[om1024])
        P.op('pool', lambda e: e.memset(om64.t[:], 0.0), wr=[om64])
        P.op('pool', lambda e: e.memset(om64.t[0:64, 0:64], 1.0 / 64), wr=[om64])
        P.op('pool', lambda e: e.memset(om64.t[64:128, 64:128], 1.0 / 64), wr=[om64])

        for l in range(depth):
            P.op('sp', lambda e, l=l: e.dma_start(out=lam_t.t[:, l * 256:(l + 1) * 256], in_=lamv[l]), wr=[lam_t], dma=True)
        P.op('sp', lambda e: e.dma_start(out=dgn_t.t[:], in_=dgn), wr=[dgn_t], dma=True)
        P.op('sp', lambda e: e.dma_start(out=sgn_t.t[:], in_=sgn), wr=[sgn_t], dma=True)
        P.op('sp', lambda e: e.dma_start(out=lnp_t.t[:], in_=lnp), wr=[lnp_t], dma=True)
        P.op('sp', lambda e: e.dma_start(out=cvw_t.t[:], in_=cvw), wr=[cvw_t], dma=True)
        P.op('sp', lambda e: e.dma_start(out=cvb_t.t[:], in_=cvb), wr=[cvb_t], dma=True)
        for l in range(depth):
            lam_init = 0.8 - 0.6 * math.exp(-0.3 * l)
            b0 = l * 256
            for i in range(2):
                q = lam_t.t[:, b0 + i * 128: b0 + i * 128 + 64]
                k = lam_t.t[:, b0 + i * 128 + 64: b0 + i * 128 + 128]
                P.op('dve', lambda e, q=q, k=k: e.tensor_tensor(out=q, in0=q, in1=k, op=ALU.mult), wr=[lam_t])
                P.op('dve', lambda e, q=q, i=i: e.reduce_sum(out=lam_w.t[:, i:i + 1], in_=q, axis=AX.X), rd=[lam_t], wr=[lam_w])
            P.op('act', lambda e: e.activation(out=lam_w.t[:, 2:4], in_=lam_w.t[:, 0:2], func=AF.Exp), wr=[lam_w])
            P.op('dve', lambda e: e.tensor_tensor(out=lam_w.t[:, 4:5], in0=lam_w.t[:, 3:4], in1=lam_w.t[:, 2:3], op=ALU.subtract), wr=[lam_w])
            P.op('dve', lambda e, l=l, li=lam_init: e.tensor_scalar(out=nlam.t[:, l:l + 1], in0=lam_w.t[:, 4:5], scalar1=-li, scalar2=None, op0=ALU.add),
                 rd=[lam_w], wr=[nlam])
            P.op('dve', lambda e, l=l, li=lam_init: e.tensor_scalar(out=dgn_t.t[:, l:l + 1], in0=dgn_t.t[:, l:l + 1], scalar1=1.0 - li, scalar2=None, op0=ALU.mult),
                 wr=[dgn_t])

        bank_rr = [0]

        def next_bank():
            b = banks[bank_rr[0] % 8]
            bank_rr[0] += 1
            return b

        def load_weight(ph, dst, src_view, nk, ncols, stg):
            CH = 2048
            i = 0
            for kc in range(nk):
                for c0 in range(0, ncols, CH):
                    cw = min(CH, ncols - c0)
                    sg = stg[i % len(stg)]
                    P.op('sp', lambda e, sg=sg, kc=kc, c0=c0, cw=cw: e.dma_start(out=sg.t[:, 0:cw], in_=src_view[:, kc, c0:c0 + cw]),
                         wr=[sg], dma=True)
                    eng = ('pool', 'dve')[i % 2]
                    P.op(eng, lambda e, sg=sg, kc=kc, c0=c0, cw=cw: e.tensor_copy(out=dst.t[:, kc, c0:c0 + cw], in_=sg.t[:, 0:cw]),
                         rd=[sg], wr=[dst])
                    i += 1

        def layer_norm(ph, y, N, gcol, bcol, sqr, msb, rsb, tmp, off=0):
            b1 = next_bank()
            b2 = next_bank()
            for oc in range(NCH):
                sq = sqr[oc % len(sqr)]
                P.op('act', lambda e, sq=sq, oc=oc: e.activation(out=sq.t[:, 0:N], in_=y.t[:, oc, off:off + N], func=AF.Square), rd=[y], wr=[sq])
                P.op('pe', lambda e, oc=oc: e.matmul(b1.t[:, 0:N], lhsT=om1024.t[:], rhs=y.t[:, oc, off:off + N], start=(oc == 0), stop=(oc == NCH - 1)),
                     rd=[y, om1024], wr=[b1])
                P.op('pe', lambda e, sq=sq, oc=oc: e.matmul(b2.t[:, 0:N], lhsT=om1024.t[:], rhs=sq.t[:, 0:N], start=(oc == 0), stop=(oc == NCH - 1)),
                     rd=[sq, om1024], wr=[b2])
            P.op('act', lambda e: e.activation(out=msb.t[:, 0:N], in_=b1.t[:, 0:N], func=AF.Copy), rd=[b1], wr=[msb])
            P.op('act', lambda e: e.activation(out=rsb.t[:, 0:N], in_=b1.t[:, 0:N], func=AF.Square), rd=[b1], wr=[rsb])
            P.op('dve', lambda e: e.tensor_tensor(out=rsb.t[:, 0:N], in0=b2.t[:, 0:N], in1=rsb.t[:, 0:N], op=ALU.subtract), rd=[b2], wr=[rsb])
            P.op('act', lambda e: e.activation(out=rsb.t[:, 0:N], in_=rsb.t[:, 0:N], func=AF.Ln, bias=LN_EPS, scale=1.0), wr=[rsb])
            P.op('act', lambda e: e.activation(out=rsb.t[:, 0:N], in_=rsb.t[:, 0:N], func=AF.Exp, scale=-0.5), wr=[rsb])
            for oc in range(NCH):
                tm = tmp[oc % len(tmp)]
                P.op('pool', lambda e, tm=tm, oc=oc: e.tensor_tensor(out=tm.t[:, 0:N], in0=y.t[:, oc, off:off + N], in1=msb.t[:, 0:N], op=ALU.subtract),
                     rd=[y, msb], wr=[tm])
                P.op('pool', lambda e, tm=tm: e.tensor_tensor(out=tm.t[:, 0:N], in0=tm.t[:, 0:N], in1=rsb.t[:, 0:N], op=ALU.mult), rd=[rsb], wr=[tm])
                P.op('dve', lambda e, tm=tm, oc=oc: e.tensor_scalar(out=y.t[:, oc, off:off + N], in0=tm.t[:, 0:N], scalar1=gcol(oc), scalar2=bcol(oc),
                                                                 op0=ALU.mult, op1=ALU.add), rd=[tm, lnp_t], wr=[y])

        xview = lambda ap: ap.rearrange("(c p) s -> p c s", p=128)

        nl = depth if nlayers_run is None else nlayers_run
        def run_layer(l, x_in, x_out):

            P.barrier()
            with ExitStack() as ph:
              if 'A' not in skip:
                wA = sbt(ph, "wA", [128, NCH, 4096], BF16)
                stg = [sbt(ph, f"stgA{i}", [128, 2048], F32) for i in range(2)]
                xf = [sbt(ph, f"xfA{i}", [128, NCH, 512], F32) for i in range(1)]
                xb = [sbt(ph, f"xbA{i}", [128, NCH, 512], BF16) for i in range(2)]
                cs = [sbt(ph, f"csA{i}", [128, 512], F32) for i in range(2)]
                sn = [sbt(ph, f"snA{i}", [128, 512], F32) for i in range(2)]
                t1 = [sbt(ph, f"t1A{i}", [128, 512], F32) for i in range(3)]
                t2 = [sbt(ph, f"t2A{i}", [128, 512], F32) for i in range(3)]
                ob = [sbt(ph, f"obA{i}", [128, 512], BF16) for i in range(4)]
                load_weight(ph, wA, xview(w_in[l]), NCH, 4096, stg)
                oi = [0]
                for j in range(NT):
                    t0 = j * 512
                    xfj = xf[0]
                    xbj = xb[j % 2]
                    csj = cs[j % 2]
                    snj = sn[j % 2]
                    P.op('sp', lambda e, xfj=xfj, t0=t0: e.dma_start(out=xfj.t[:, :, :], in_=xview(x_in)[:, :, t0:t0 + 512]), wr=[xfj], dma=True)
                    P.op('sp', lambda e, csj=csj, t0=t0: e.dma_start(out=csj.t[:], in_=cos2[:, t0:t0 + 512]), wr=[csj], dma=True)
                    P.op('sp', lambda e, snj=snj, t0=t0: e.dma_start(out=snj.t[:], in_=sin2[:, t0:t0 + 512]), wr=[snj], dma=True)
                    for kc in range(NCH):
                        eng = ('pool', 'dve')[kc % 2]
                        P.op(eng, lambda e, kc=kc, xfj=xfj, xbj=xbj: e.tensor_copy(out=xbj.t[:, kc, :], in_=xfj.t[:, kc, :]), rd=[xfj], wr=[xbj])
                    for c in range(8):
                        b1 = next_bank()
                        b2 = next_bank()
                        for kc in range(NCH):
                            P.op('pe', lambda e, kc=kc, c=c, b1=b1, xbj=xbj: e.matmul(b1.t[:, :], lhsT=wA.t[:, kc, 128 * c:128 * c + 128], rhs=xbj.t[:, kc, :],
                                                                                start=(kc == 0), stop=(kc == NCH - 1)), rd=[wA, xbj], wr=[b1])
                        for kc in range(NCH):
                            P.op('pe', lambda e, kc=kc, c=c, b2=b2, xbj=xbj: e.matmul(b2.t[:, :], lhsT=wA.t[:, kc, 3072 + 128 * c:3072 + 128 * c + 128], rhs=xbj.t[:, kc, :],
                                                                                start=(kc == 0), stop=(kc == NCH - 1)), rd=[wA, xbj], wr=[b2])
                        a1 = t1[oi[0] % 3]
                        a2 = t2[oi[0] % 3]
                        o = ob[oi[0] % 4]
                        oi[0] += 1
                        P.op('dve', lambda e, a1=a1, b1=b1, csj=csj: e.tensor_tensor(out=a1.t[:], in0=b1.t[:, :], in1=csj.t[:], op=ALU.mult), rd=[b1, csj], wr=[a1])
                        P.op('dve', lambda e, a2=a2, b2=b2, snj=snj: e.tensor_tensor(out=a2.t[:], in0=b2.t[:, :], in1=snj.t[:], op=ALU.mult), rd=[b2, snj], wr=[a2])
                        P.op('pool', lambda e, a1=a1, a2=a2, o=o: e.tensor_tensor(out=o.t[:], in0=a1.t[:], in1=a2.t[:], op=ALU.add), rd=[a1, a2], wr=[o])
                        P.op('pool', lambda e, o=o, c=c, t0=t0: e.dma_start(out=qk_s[128 * c:128 * c + 128, t0:t0 + 512], in_=o.t[:]), rd=[o], dma=True)
                    for c in range(8):
                        b1 = next_bank()
                        for kc in range(NCH):
                            P.op('pe', lambda e, kc=kc, c=c, b1=b1, xbj=xbj: e.matmul(b1.t[:, :], lhsT=wA.t[:, kc, 1536 + 128 * c:1536 + 128 * c + 128], rhs=xbj.t[:, kc, :],
                                                                                start=(kc == 0), stop=(kc == NCH - 1)), rd=[wA, xbj], wr=[b1])
                        o = ob[oi[0] % 4]
                        oi[0] += 1
                        P.op('act', lambda e, o=o, b1=b1: e.activation(out=o.t[:], in_=b1.t[:, :], func=AF.Copy), rd=[b1], wr=[o])
                        P.op('pool', lambda e, o=o, c=c, t0=t0: e.dma_start(out=qk_s[1024 + 128 * c:1024 + 128 * c + 128, t0:t0 + 512], in_=o.t[:]), rd=[o], dma=True)
                    for tb in range(4):
                        for half, wc0 in ((0, 1024), (1, 2560)):
                            b1 = next_bank()
                            for kc in range(NCH):
                                P.op('pe', lambda e, kc=kc, b1=b1, xbj=xbj, tb=tb, wc0=wc0: e.matmul(b1.t[:, :], lhsT=xbj.t[:, kc, 128 * tb:128 * tb + 128], rhs=wA.t[:, kc, wc0:wc0 + 512],
                                                                                            start=(kc == 0), stop=(kc == NCH - 1)), rd=[wA, xbj], wr=[b1])
                            o = ob[oi[0] % 4]
                            oi[0] += 1
                            P.op('act', lambda e, o=o, b1=b1: e.activation(out=o.t[:], in_=b1.t[:, :], func=AF.Copy), rd=[b1], wr=[o])
                            P.op('pool', lambda e, o=o, tb=tb, half=half, t0=t0: e.dma_start(out=v_s[t0 + 128 * tb:t0 + 128 * tb + 128, 512 * half:512 * half + 512], in_=o.t[:]),
                                 rd=[o], dma=True)

            P.barrier()
            with ExitStack() as ph:
                vview = v_s.rearrange("(kb p) f -> p kb f", p=128)
                sets = []
                for si in range(2):
                    kp = [sbt(ph, "KTa", [128, S], BF16), sbt(ph, "KTb", [128, S], BF16)]
                    P.op('pool', lambda e, kp=kp: e.memset(kp[0].t[64:128, :], 0.0), wr=[kp[0]])
                    P.op('pool', lambda e, kp=kp: e.memset(kp[1].t[0:64, :], 0.0), wr=[kp[1]])
                    sets.append((kp, sbt(ph, "QT", [128, S], BF16), sbt(ph, "VV", [128, NKB, 128], BF16)))
                units = ([('d', h) for h in range(0 if 'Bd' in skip else 4)] + [('s', pp) for pp in range(0 if 'Bs' in skip else 4)])

                def load_unit(ui):
                    kind, idx = units[ui]
                    kp, qt, vv = sets[ui % 2]
                    if kind == 'd':
                        kr, qr, vc = 512 + 128 * idx, 128 * idx, 128 * idx
                    else:
                        kr, qr, vc = 1536 + 128 * idx, 1024 + 128 * idx, 512 + 128 * idx
                    P.op('sp', lambda e: e.dma_start(out=kp[0].t[0:64, :], in_=qk_s[kr:kr + 64, :]), wr=[kp[0]], dma=True)
                    P.op('sp', lambda e: e.dma_start(out=kp[1].t[64:128, :], in_=qk_s[kr + 64:kr + 128, :]), wr=[kp[1]], dma=True)
                    P.op('sp', lambda e: e.dma_start(out=qt.t[:], in_=qk_s[qr:qr + 128, :]), wr=[qt], dma=True)
                    for q4 in range(0, NKB, VCH):
                        P.op('sp', lambda e, q4=q4: e.dma_start(out=vv.t[:, q4:q4 + VCH, :], in_=vview[:, q4:q4 + VCH, vc:vc + 128]), wr=[vv], dma=True)
                Er = [sbt(ph, f"Er{i}", [128, 512], F32) for i in range(4)]
                SPr = [sbt(ph, f"SPr{i}", [128, 512], F16) for i in range(4)]
                Fr = [sbt(ph, f"Fr{i}", [128, 512], F32) for i in range(3)]
                Ar = [sbt(ph, f"Ar{i}", [128, 512], BF16) for i in range(4)]
                fa = [sbt(ph, f"fa{i}", [128, 512], F32) for i in range(4)]
                fo = [sbt(ph, f"fo{i}", [128, 512], BF16) for i in range(2)]
                fcnt = [0]

                def rms_finish(o_sb, omat, gcolap, eps, row0, t0):
                    sq = fa[3]
                    rb = banks[7]
                    ofin = fo[fcnt[0] % 2]
                    fcnt[0] += 1
                    P.op('act', lambda e: e.activation(out=sq.t[:], in_=o_sb.t[:], func=AF.Square), rd=[o_sb], wr=[sq])
                    P.op('pe', lambda e: e.matmul(rb.t[:, :], lhsT=omat.t[:], rhs=sq.t[:], start=True, stop=True), rd=[sq, omat], wr=[rb])
                    P.op('act', lambda e: e.activation(out=sq.t[:], in_=rb.t[:, :], func=AF.Ln, bias=eps, scale=1.0), rd=[rb], wr=[sq])
                    P.op('act', lambda e: e.activation(out=sq.t[:], in_=sq.t[:], func=AF.Exp, scale=-0.5), wr=[sq])
                    P.op('dve', lambda e: e.tensor_tensor(out=o_sb.t[:], in0=o_sb.t[:], in1=sq.t[:], op=ALU.mult), rd=[sq], wr=[o_sb])
                    P.op('dve', lambda e: e.tensor_scalar(out=ofin.t[:], in0=o_sb.t[:], scalar1=gcolap, scalar2=None, op0=ALU.mult), rd=[o_sb, dgn_t, sgn_t], wr=[ofin])
                    P.op('pool', lambda e: e.dma_start(out=mix_s[row0:row0 + 128, t0:t0 + 512], in_=ofin.t[:]), rd=[ofin], dma=True)

                if units:
                    load_unit(0)
                for ui, (kind, h) in enumerate(units):
                    if kind != 'd':
                        continue
                    if ui + 1 < len(units):
                        load_unit(ui + 1)
                    KTp, QT, VV = sets[ui % 2]
                    steps = [(j, c, kb) for j in range(NT) for c in range(2) for kb in range(4 * j + 4)]
                    pz_of = {}

                    def d_qk(i):
                        j, c, kb = steps[i]
                        pz = banks[6 + i % 2]
                        pz_of[i] = pz
                        KT = KTp[c]
                        QTl = QT
                        kT = KT.t[:, kb * 128:(kb + 1) * 128]
                        q0 = j * 512
                        if kb >= 4 * j:
                            c0 = 128 * (kb - 4 * j)
                            P.op('pe', lambda e: e.matmul(pz.t[:, c0:512], lhsT=kT, rhs=QTl.t[:, q0 + c0:q0 + 512], start=True, stop=False), rd=[KT, QT], wr=[pz])
                            P.op('pe', lambda e: e.matmul(pz.t[:, c0:c0 + 128], lhsT=ident.t[:], rhs=maskD.t[:], start=False, stop=True), rd=[ident, maskD], wr=[pz])
                        else:
                            P.op('pe', lambda e: e.matmul(pz.t[:, :], lhsT=kT, rhs=QTl.t[:, q0:q0 + 512], start=True, stop=True), rd=[KT, QT], wr=[pz])

                    def d_exp(i):
                        j, c, kb = steps[i]
                        pz = pz_of[i]
                        c0 = max(0, 128 * (kb - 4 * j))
                        A = Ar[i % 4]
                        P.op('act', lambda e: e.activation(out=A.t[:, c0:512], in_=pz.t[:, c0:512], func=AF.Exp, scale=0.125), rd=[pz], wr=[A])

                    def d_pv(i):
                        j, c, kb = steps[i]
                        c0 = max(0, 128 * (kb - 4 * j))
                        A = Ar[i % 4]
                        pr = (2 * j + c) % 3
                        po = banks[2 * pr]
                        pl = banks[2 * pr + 1]
                        stt = (kb == 0)
                        VVl = VV
                        P.op('pe', lambda e: e.matmul(po.t[:, c0:512], lhsT=VVl.t[:, kb, :], rhs=A.t[:, c0:512], start=stt, stop=True, skip_group_check=True), rd=[VV, A], wr=[po])
                        P.op('pe', lambda e: e.matmul(pl.t[:, c0:512], lhsT=ones_b.t[:], rhs=A.t[:, c0:512], start=stt, stop=True, skip_group_check=True), rd=[ones_b, A], wr=[pl])
                        if c == 1 and kb == 4 * j + 3:
                            p0 = (2 * j) % 3
                            po0, pl0 = banks[2 * p0], banks[2 * p0 + 1]
                            r0, o0, r1 = fa[0], fa[1], fa[2]
                            P.op('dve', lambda e: e.reciprocal(out=r0.t[:], in_=pl0.t[:, :]), rd=[pl0], wr=[r0])
                            P.op('dve', lambda e: e.tensor_tensor(out=o0.t[:], in0=po0.t[:, :], in1=r0.t[:], op=ALU.mult), rd=[po0, r0], wr=[o0])
                            P.op('dve', lambda e: e.reciprocal(out=r1.t[:], in_=pl.t[:, :]), rd=[pl], wr=[r1])
                            P.op('dve', lambda e: e.tensor_tensor(out=r1.t[:], in0=po.t[:, :], in1=r1.t[:], op=ALU.mult), rd=[po], wr=[r1])
                            P.op('dve', lambda e: e.scalar_tensor_tensor(out=o0.t[:], in0=r1.t[:], scalar=nlam.t[:, l:l + 1], in1=o0.t[:], op0=ALU.mult, op1=ALU.add),
                                 rd=[r1, nlam], wr=[o0])
                            rms_finish(o0, om128, dgn_t.t[:, l:l + 1], RMS_EPS, 128 * h, j * 512)

                    n = len(steps)
                    for t in range(n + 2):
                        if t < n:
                            d_qk(t)
                            d_exp(t)
                        if t >= 2:
                            d_pv(t - 2)

                for ui, (kind, pp) in enumerate(units):
                    if kind != 's':
                        continue
                    if ui + 1 < len(units):
                        load_unit(ui + 1)
                    KTp, QT, VV = sets[ui % 2]
                    steps = [(j, kb, hh) for j in range(NT) for kb in range(4 * j + 3, -1, -1) for hh in range(2)]
                    pz_of = {}

                    def geom(i):
                        j, kb, hh = steps[i]
                        diag = kb >= 4 * j
                        c0 = 128 * (kb - 4 * j) if diag else 0
                        return j, kb, hh, diag, c0

                    def s_qk(i):
                        j, kb, hh, diag, c0 = geom(i)
                        pz = banks[4 + i % 3]
                        pz_of[i] = pz
                        KT = KTp[hh]
                        QTl = QT
                        kT = KT.t[:, kb * 128:(kb + 1) * 128]
                        q0 = j * 512
                        if diag:
                            P.op('pe', lambda e: e.matmul(pz.t[:, c0:512], lhsT=kT, rhs=QTl.t[:, q0 + c0:q0 + 512], start=True, stop=False), rd=[KT, QT], wr=[pz])
                            P.op('pe', lambda e: e.matmul(pz.t[:, c0:c0 + 128], lhsT=ident.t[:], rhs=maskS.t[:], start=False, stop=True), rd=[ident, maskS], wr=[pz])
                        else:
                            P.op('pe', lambda e: e.matmul(pz.t[:, :], lhsT=kT, rhs=QTl.t[:, q0:q0 + 512], start=True, stop=True), rd=[KT, QT], wr=[pz])

                    def s_act1(i):
                        j, kb, hh, diag, c0 = geom(i)
                        pz = pz_of[i]
                        E = Er[i % 4]
                        SP = SPr[i % 4]
                        P.op('act', lambda e: e.activation(out=E.t[:, c0:512], in_=pz.t[:, c0:512], func=AF.Exp, scale=0.125), rd=[pz], wr=[E])

                    def s_act1b(i):
                        j, kb, hh, diag, c0 = geom(i)
                        E = Er[i % 4]
                        SP = SPr[i % 4]
                        P.op('act', lambda e: e.activation(out=SP.t[:, c0:512], in_=E.t[:, c0:512], func=AF.Ln, bias=1.0, scale=1.0), rd=[E], wr=[SP])

                    def s_u(i):
                        j, kb, hh, diag, c0 = geom(i)
                        SP = SPr[i % 4]
                        X = banks[hh]
                        if kb == 4 * j + 3:
                            P.op('dve', lambda e: e.memset(X.t[:, :], 0.0), wr=[X])
                        P.op('pe', lambda e: e.matmul(X.t[:, c0:512], lhsT=uinc.t[:], rhs=SP.t[:, c0:512], start=False, stop=True, skip_group_check=True), rd=[uinc, SP], wr=[X])

                    def s_act2(i):
                        j, kb, hh, diag, c0 = geom(i)
                        X = banks[hh]
                        Fb = Fr[i % 3]
                        P.op('act', lambda e: e.activation(out=Fb.t[:, c0:512], in_=X.t[:, c0:512], func=AF.Exp, scale=-1.0), rd=[X], wr=[Fb])

                    def s_mul(i):
                        j, kb, hh, diag, c0 = geom(i)
                        E = Er[i % 4]
                        Fb = Fr[i % 3]
                        A = Ar[i % 4]
                        P.op('dve', lambda e: e.tensor_tensor(out=A.t[:, c0:512], in0=E.t[:, c0:512], in1=Fb.t[:, c0:512], op=ALU.mult), rd=[E, Fb], wr=[A])

                    def s_lpv(i):
                        j, kb, hh, diag, c0 = geom(i)
                        SP = SPr[i % 4]
                        A = Ar[i % 4]
                        X = banks[hh]
                        po = banks[2 + hh]
                        VVl = VV
                        if kb > 0:
                            P.op('pe', lambda e: e.matmul(X.t[:, c0:512], lhsT=lstr.t[:], rhs=SP.t[:, c0:512], start=False, stop=True, skip_group_check=True), rd=[lstr, SP], wr=[X])
                        if kb == 4 * j + 3:
                            P.op('dve', lambda e: e.memset(po.t[:, :], 0.0), wr=[po])
                        P.op('pe', lambda e: e.matmul(po.t[:, c0:512], lhsT=VVl.t[:, kb, :], rhs=A.t[:, c0:512], start=False, stop=True, skip_group_check=True), rd=[VV, A], wr=[po])
                        if kb == 0 and hh == 1:
                            o0 = fa[0]
                            pa, pb = banks[2], banks[3]
                            P.op('act', lambda e: e.activation(out=o0.t[0:64, :], in_=pa.t[0:64, :], func=AF.Copy), rd=[pa], wr=[o0])
                            P.op('dve', lambda e: e.tensor_copy(out=o0.t[64:128, :], in_=pb.t[64:128, :]), rd=[pb], wr=[o0])
                            rms_finish(o0, om64, sgn_t.t[:, l:l + 1], RMS_EPS, 512 + 128 * pp, j * 512)

                    n = len(steps)
                    for t in range(n + 2):
                        if t < n:
                            s_qk(t)
                            s_act1(t)
                        if 1 <= t <= n:
                            s_u(t - 1)
                            s_act2(t - 1)
                        if t < n:
                            s_act1b(t)
                        if 1 <= t <= n:
                            s_mul(t - 1)
                        if t >= 2:
                            s_lpv(t - 2)

            P.barrier()
            with ExitStack() as ph:
                wO = sbt(ph, "wO", [128, NCH, D], BF16)
                stg = [sbt(ph, f"stgC{i}", [128, 2048], F32) for i in range(2)]
                mx = [sbt(ph, f"mxC{i}", [128, NCH, 512], BF16) for i in range(2)]
                yy = [sbt(ph, f"yyC{i}", [128, NCH, 512], F32) for i in range(2)]
                sqr = [sbt(ph, f"sqC{i}", [128, 512], F32) for i in range(2)]
                tmp = [sbt(ph, f"tmC{i}", [128, 512], F32) for i in range(2)]
                msb = sbt(ph, "msbC", [128, 512], F32)
                rsb = sbt(ph, "rsbC", [128, 512], F32)
                load_weight(ph, wO, xview(w_out[l]), NCH, D, stg)
                g0 = l * 4 * NCH
                for j in range(0 if 'C1' in skip else NT):
                    t0 = j * 512
                    m = mx[j % 2]
                    y = yy[j % 2]
                    P.op('sp', lambda e, m=m, t0=t0: e.dma_start(out=m.t[:, :, :], in_=xview(mix_s)[:, :, t0:t0 + 512]), wr=[m], dma=True)
                    P.op('sp', lambda e, y=y, t0=t0: e.dma_start(out=y.t[:, :, :], in_=xview(x_in)[:, :, t0:t0 + 512]), wr=[y], dma=True)
                    for oc in range(NCH):
                        b1 = next_bank()
                        for kc in range(NCH):
                            P.op('pe', lambda e, kc=kc, oc=oc, b1=b1, m=m: e.matmul(b1.t[:, :], lhsT=wO.t[:, kc, 128 * oc:128 * oc + 128], rhs=m.t[:, kc, :],
                                                                               start=(kc == 0), stop=(kc == NCH - 1)), rd=[wO, m], wr=[b1])
                        P.op('dve', lambda e, oc=oc, b1=b1, y=y: e.scalar_tensor_tensor(out=y.t[:, oc, :], in0=y.t[:, oc, :], scalar=ALPHA, in1=b1.t[:, :],
                                                                                   op0=ALU.mult, op1=ALU.add), rd=[b1], wr=[y])
                    layer_norm(ph, y, 512, lambda oc, g0=g0: lnp_t.t[:, g0 + oc:g0 + oc + 1], lambda oc, g0=g0: lnp_t.t[:, g0 + NCH + oc:g0 + NCH + oc + 1],
                               sqr, msb, rsb, tmp)
                    P.op('pool', lambda e, y=y, t0=t0: e.dma_start(out=xview(x1_s)[:, :, t0:t0 + 512], in_=y.t[:, :, :]), rd=[y], dma=True)

            P.barrier()
            with ExitStack() as ph:
                NF = 256
                wU = sbt(ph, "wU", [128, NCH, 2 * DFF], BF16)
                wD = sbt(ph, "wD", [128, NFF, D], BF16)
                stg = [sbt(ph, f"stgF{i}", [128, 2048], F32) for i in range(2)]
                xh = [sbt(ph, f"xhF{i}", [128, NCH, NF + 2], F32) for i in range(2)]
                xbh = [sbt(ph, f"xbF{i}", [128, NCH, NF + 2], BF16) for i in range(2)]
                gT = sbt(ph, "gT", [128, NFF, NF], BF16)
                ua = [sbt(ph, f"uaF{i}", [128, NF], F32) for i in range(2)]
                ub = [sbt(ph, f"ubF{i}", [128, NF], F32) for i in range(2)]
                sa = [sbt(ph, f"saF{i}", [128, NF], F32) for i in range(2)]
                sqr = [sbt(ph, f"sqF{i}", [128, NF], F32) for i in range(2)]
                tmp = [sbt(ph, f"tmF{i}", [128, NF], F32) for i in range(2)]
                msb = sbt(ph, "msbF", [128, NF], F32)
                rsb = sbt(ph, "rsbF", [128, NF], F32)
                load_weight(ph, wU, xview(w_up[l]), NCH, 2 * DFF, stg)
                load_weight(ph, wD, xview(w_down[l]), NFF, D, stg)
                g0 = l * 4 * NCH + 2 * NCH
                cw0 = l * 6 * NFF
                cb0 = l * 2 * NFF
                ui = [0]
                for j in range(0 if 'C2' in skip else S // NF):
                    t0 = j * NF
                    x1 = xh[j % 2]
                    x1b = xbh[j % 2]
                    if j == 0:
                        P.op('pool', lambda e, x1=x1: e.memset(x1.t[:, :, 0:2], 0.0), wr=[x1])
                        P.op('sp', lambda e, x1=x1: e.dma_start(out=x1.t[:, :, 2:NF + 2], in_=xview(x1_s)[:, :, 0:NF]), wr=[x1], dma=True)
                    else:
                        P.op('sp', lambda e, x1=x1, t0=t0: e.dma_start(out=x1.t[:, :, :], in_=xview(x1_s)[:, :, t0 - 2:t0 + NF]), wr=[x1], dma=True)
                    for kc in range(NCH):
                        eng = ('pool', 'dve')[kc % 2]
                        P.op(eng, lambda e, kc=kc, x1=x1, x1b=x1b: e.tensor_copy(out=x1b.t[:, kc, :], in_=x1.t[:, kc, :]), rd=[x1], wr=[x1b])
                    for i in range(NFF):
                        res = []
                        for half in range(2):
                            ch = half * NFF + i
                            b1 = next_bank()
                            for kc in range(NCH):
                                P.op('pe', lambda e, kc=kc, ch=ch, b1=b1, x1b=x1b: e.matmul(b1.t[:, 0:NF + 2], lhsT=wU.t[:, kc, 128 * ch:128 * ch + 128], rhs=x1b.t[:, kc, :],
                                                                                       start=(kc == 0), stop=(kc == NCH - 1)), rd=[wU, x1b], wr=[b1])
                            u = (ua, ub)[half][ui[0] % 2]
                            w = lambda tap, ch=ch: cvw_t.t[:, cw0 + tap * 2 * NFF + ch:cw0 + tap * 2 * NFF + ch + 1]
                            bcol = cvb_t.t[:, cb0 + ch:cb0 + ch + 1]
                            P.op('act', lambda e, u=u, b1=b1, w=w, bcol=bcol: e.activation(out=u.t[:], in_=b1.t[:, 2:NF + 2], func=AF.Identity, bias=bcol, scale=w(2)),
                                 rd=[b1, cvw_t, cvb_t], wr=[u])
                            P.op('dve', lambda e, u=u, b1=b1, w=w: e.scalar_tensor_tensor(out=u.t[:], in0=b1.t[:, 1:NF + 1], scalar=w(1), in1=u.t[:], op0=ALU.mult, op1=ALU.add),
                                 rd=[b1, cvw_t], wr=[u])
                            P.op('dve', lambda e, u=u, b1=b1, w=w: e.scalar_tensor_tensor(out=u.t[:], in0=b1.t[:, 0:NF], scalar=w(0), in1=u.t[:], op0=ALU.mult, op1=ALU.add),
                                 rd=[b1, cvw_t], wr=[u])
                            res.append(u)
                        s = sa[ui[0] % 2]
                        ui[0] += 1
                        P.op('act', lambda e, s=s, u=res[0]: e.activation(out=s.t[:], in_=u.t[:], func=AF.Silu), rd=[res[0]], wr=[s])
                        P.op('pool', lambda e, s=s, u=res[1], i=i: e.tensor_tensor(out=gT.t[:, i, :], in0=s.t[:], in1=u.t[:], op=ALU.mult), rd=[s, res[1]], wr=[gT])
                    for oc in range(NCH):
                        b1 = next_bank()
                        for i in range(NFF):
                            P.op('pe', lambda e, i=i, oc=oc, b1=b1: e.matmul(b1.t[:, 0:NF], lhsT=wD.t[:, i, 128 * oc:128 * oc + 128], rhs=gT.t[:, i, :],
                                                                        start=(i == 0), stop=(i == NFF - 1)), rd=[wD, gT], wr=[b1])
                        P.op('dve', lambda e, oc=oc, b1=b1, x1=x1: e.scalar_tensor_tensor(out=x1.t[:, oc, 2:NF + 2], in0=x1.t[:, oc, 2:NF + 2], scalar=ALPHA, in1=b1.t[:, 0:NF],
                                                                                     op0=ALU.mult, op1=ALU.add), rd=[b1], wr=[x1])
                    layer_norm(ph, x1, NF, lambda oc, g0=g0: lnp_t.t[:, g0 + oc:g0 + oc + 1], lambda oc, g0=g0: lnp_t.t[:, g0 + NCH + oc:g0 + NCH + oc + 1],
                               sqr, msb, rsb, tmp, off=2)
                    P.op('pool', lambda e, x1=x1, t0=t0: e.dma_start(out=xview(x_out)[:, :, t0:t0 + NF], in_=x1.t[:, :, 2:NF + 2]), rd=[x1], dma=True)

        for l in range(nl):
            run_layer(l, xT if l == 0 else x_s, outT if l == nl - 1 else x_s)
        P.emit()
        nops = P.nops
    return nc, nops


def _host_prep(inp, S):
    depth = DEPTH
    f = np.float32
    w_in = np.asarray(inp["w_in"], f)
    idx = np.arange(1024)
    base = (idx // 64) * 64
    off = idx % 64
    perm = base + (off + 32) % 64
    w_ext = np.concatenate([w_in, w_in[:, :, perm]], axis=2)
    lamv = np.concatenate([inp["lam_q1"], inp["lam_k1"], inp["lam_q2"], inp["lam_k2"]], axis=1).astype(f)
    lamv = np.ascontiguousarray(np.broadcast_to(lamv[:, None, :], (depth, 128, 256)))
    dgn = np.ascontiguousarray(np.asarray(inp["diff_norm_g"], f).T)
    sgn = np.ascontiguousarray(np.tile(np.asarray(inp["sb_norm_g"], f), (1, 2)).T)
    cols = []
    for l in range(depth):
        for nm in ("ln1_g", "ln1_b", "ln2_g", "ln2_b"):
            cols.append(np.asarray(inp[nm][l], f).reshape(NCH, 128).T)
    lnp = np.ascontiguousarray(np.concatenate(cols, axis=1))
    cw = []
    cb = []
    for l in range(depth):
        for tap in range(3):
            cw.append(np.asarray(inp["conv_w"][l, tap], f).reshape(2 * NFF, 128).T)
        cb.append(np.asarray(inp["conv_b"][l], f).reshape(2 * NFF, 128).T)
    cvw = np.ascontiguousarray(np.concatenate(cw, axis=1))
    cvb = np.ascontiguousarray(np.concatenate(cb, axis=1))
    inv = (1.0 / (np.float32(10000.0) ** (np.arange(0, HD, 2, dtype=np.float32) / np.float32(HD)))).astype(f)
    ang = (np.arange(S, dtype=f)[None, :] * inv[:, None]).astype(f)
    c = np.cos(ang).astype(f)
    s = np.sin(ang).astype(f)
    cos2 = np.ascontiguousarray(np.tile(c, (4, 1)))
    sin2 = np.ascontiguousarray(np.concatenate([-s, s, -s, s], axis=0))
    shared = dict(w_in=np.ascontiguousarray(w_ext), w_out=np.asarray(inp["w_out"], f), w_up=np.asarray(inp["w_up"], f),
                  w_down=np.asarray(inp["w_down"], f), lamv=lamv, dgn=dgn, sgn=sgn, lnp=lnp, cvw=cvw, cvb=cvb,
                  cos2=cos2, sin2=sin2)
    return shared


def kernel(**inputs):
    x = np.asarray(inputs["x"], np.float32)
    B, S, _ = x.shape
    shared = _host_prep(inputs, S)
    nc, _ = build(S)
    in_maps = []
    for c in range(8):
        m = dict(shared)
        m["xT"] = np.ascontiguousarray(x[c % B].T)
        in_maps.append(m)
    res = run_bass_kernel_spmd(nc, in_maps, core_ids=list(range(8)))
    out = np.stack([np.ascontiguousarray(res.results[b]["outT"].T) for b in range(B)], axis=0)
    return out.astype(np.float32)
```

```python
import math
from contextlib import ExitStack

import numpy as np
import concourse.bass as bass
import concourse.mybir as mybir
from concourse.bass_utils import run_bass_kernel_spmd

F32 = mybir.dt.float32
BF16 = mybir.dt.bfloat16
F16 = mybir.dt.float16
AF = mybir.ActivationFunctionType
ALU = mybir.AluOpType
AX = mybir.AxisListType

D = 1024
HD = 64
DFF = 2816
NCH = D // 128
NFF = DFF // 128
DEPTH = 2
ALPHA = (2 * DEPTH) ** 0.25
LN_EPS = 1e-5
RMS_EPS = 1e-6
NEG = -30000.0
SAFE_SAME_ENGINE = True
ENG = ['pe', 'act', 'dve', 'pool', 'sp']
NDS = 8


class Buf:
    __slots__ = ("t", "wr", "rd")

    def __init__(self, t):
        self.t = t
        self.wr = None
        self.rd = {}


class Prog:
    def __init__(self, nc, stack):
        self.nc = nc
        self.stack = stack
        self.ops = {e: [] for e in ENG}
        self.cur = {}
        self.cnt = {}
        self.seen = {e: {} for e in ENG}
        self.know = {e: {} for e in ENG}
        self.pending = {e: [] for e in ENG}
        self.nsem = 0
        self.dsem = {}
        self.dcnt = {}
        self.dn = {}
        for e in ENG:
            self._newsem(e)
        for e in ('sp', 'pool', 'act'):
            self.dsem[e] = [self._mk(f"d_{e}{i}") for i in range(NDS)]
            self.dcnt[e] = [0] * NDS
            self.dn[e] = 0
        self.nops = 0

    def _mk(self, name):
        self.nsem += 1
        return self.stack.enter_context(self.nc.semaphore(name))

    def _newsem(self, e):
        self.cur[e] = self._mk(f"s_{e}_{self.nsem}")
        self.cnt[e] = 0

    def op(self, e, fn, rd=(), wr=(), waits=(), dma=False):
        evs = [w for w in waits if w is not None]
        for b in rd:
            if b.wr is not None:
                evs.append(b.wr)
        for b in wr:
            if b.wr is not None:
                evs.append(b.wr)
            evs.extend(b.rd.values())
        if self.pending[e]:
            evs.extend(self.pending[e])
            self.pending[e] = []
        if dma:
            r = self.dn[e] % NDS
            self.dn[e] += 1
            sem = self.dsem[e][r]
            if self.dcnt[e][r] > 0:
                evs.append((sem, self.dcnt[e][r], e, True, {id(sem): self.dcnt[e][r]}))
            self.dcnt[e][r] += 16
            val = self.dcnt[e][r]
            inc = 16
        else:
            if self.cnt[e] > 30000:
                self._newsem(e)
            sem = self.cur[e]
            self.cnt[e] += 1
            val = self.cnt[e]
            inc = 1
        wl = []
        know = self.know[e]
        evs.sort(key=lambda t: -t[1])
        for ev_ in evs:
            ws, wv, we, wd = ev_[0], ev_[1], ev_[2], ev_[3]
            if we == e and not wd and (e == 'pe' or not SAFE_SAME_ENGINE):
                continue
            k = id(ws)
            if know.get(k, -1) >= wv:
                continue
            wl.append((ws, wv))
            vec = ev_[4]
            for kk, vv in vec.items():
                if know.get(kk, -1) < vv:
                    know[kk] = vv

        def run(eng, fn=fn, wl=wl, sem=sem, inc=inc, fuse=(not dma)):
            if fuse and wl:
                for ws, wv in wl[:-1]:
                    eng.wait_ge(ws, wv)
                fn(eng)._wait_ge(wl[-1][0], wl[-1][1]).then_inc(sem, inc)
            else:
                for ws, wv in wl:
                    eng.wait_ge(ws, wv)
                fn(eng).then_inc(sem, inc)
        self.ops[e].append(run)
        self.nops += 1
        vec = dict(know)
        vec[id(sem)] = val
        ev = (sem, val, e, dma, vec)
        for b in rd:
            k = id(sem)
            o = b.rd.get(k)
            if o is None or o[1] < val:
                b.rd[k] = ev
        for b in wr:
            b.wr = ev
            b.rd = {}
        return ev

    def last_events(self):
        evs = []
        for e in ENG:
            if self.cnt[e] > 0:
                evs.append((self.cur[e], self.cnt[e], e, False, {id(self.cur[e]): self.cnt[e]}))
        for e in self.dsem:
            for r in range(NDS):
                if self.dcnt[e][r] > 0:
                    evs.append((self.dsem[e][r], self.dcnt[e][r], e, True, {id(self.dsem[e][r]): self.dcnt[e][r]}))
        return evs

    def barrier(self):
        evs = self.last_events()
        for e in ENG:
            self.pending[e] = list(evs)

    def emit(self):
        nc = self.nc
        final = self.last_events()
        with nc.Block() as block:
            @block.tensor
            def _(eng):
                for f in self.ops['pe']:
                    f(eng)

            @block.scalar
            def _(eng):
                for f in self.ops['act']:
                    f(eng)

            @block.vector
            def _(eng):
                for f in self.ops['dve']:
                    f(eng)

            @block.gpsimd
            def _(eng):
                for f in self.ops['pool']:
                    f(eng)

            @block.sync
            def _(eng):
                for f in self.ops['sp']:
                    f(eng)
                for (ws, wv, we, wd, _v) in final:
                    eng.wait_ge(ws, wv)


def build(S, depth=DEPTH, nlayers_run=None, debug=False, skip=()):
    NT = S // 512
    NKB = S // 128
    VCH = min(16, NKB)
    nc = bass.Bass("TRN2", target_bir_lowering=False)
    dt_in = lambda name, shape, dt=F32: nc.dram_tensor(name, shape, dt, kind="ExternalInput").ap()
    xT = dt_in("xT", [D, S])
    w_in = dt_in("w_in", [depth, D, 4096])
    w_out = dt_in("w_out", [depth, D, D])
    w_up = dt_in("w_up", [depth, D, 2 * DFF])
    w_down = dt_in("w_down", [depth, DFF, D])
    lamv = dt_in("lamv", [depth, 128, 4 * HD])
    dgn = dt_in("dgn", [128, depth])
    sgn = dt_in("sgn", [128, depth])
    lnp = dt_in("lnp", [128, depth * 4 * NCH])
    cvw = dt_in("cvw", [128, depth * 3 * 2 * NFF])
    cvb = dt_in("cvb", [128, depth * 2 * NFF])
    cos2 = dt_in("cos2", [128, S])
    sin2 = dt_in("sin2", [128, S])
    outT = nc.dram_tensor("outT", [D, S], F32, kind="ExternalOutput").ap()
    sk = "ExternalOutput" if debug else "Internal"
    qk_s = nc.dram_tensor("qk_s", [2048, S], BF16, kind=sk).ap()
    v_s = nc.dram_tensor("v_s", [S, 1024], BF16, kind=sk).ap()
    mix_s = nc.dram_tensor("mix_s", [D, S], BF16, kind=sk).ap()
    x1_s = nc.dram_tensor("x1_s", [D, S], F32, kind=sk).ap()
    x_s = nc.dram_tensor("x_s", [D, S], F32, kind=sk).ap()

    with ExitStack() as st:
        P = Prog(nc, st)

        uid = [0]

        def sbt(stack, name, shape, dt):
            uid[0] += 1
            return Buf(stack.enter_context(nc.sbuf_tensor(f"{name}_{uid[0]}", shape, dt)))

        banks = [Buf(st.enter_context(nc.psum_tensor(f"bank{i}", [128, 512], F32))) for i in range(8)]
        ident = sbt(st, "ident", [128, 128], BF16)
        maskD = sbt(st, "maskD", [128, 128], BF16)
        maskS = sbt(st, "maskS", [128, 128], BF16)
        uinc = sbt(st, "uinc", [128, 128], F16)
        lstr = sbt(st, "lstr", [128, 128], F16)
        ones_b = sbt(st, "ones_b", [128, 128], BF16)
        om128 = sbt(st, "om128", [128, 128], F32)
        om64 = sbt(st, "om64", [128, 128], F32)
        om1024 = sbt(st, "om1024", [128, 128], F32)
        cst = sbt(st, "cst", [128, 128], F32)
        lam_t = sbt(st, "lam_t", [128, depth * 4 * HD], F32)
        lam_w = sbt(st, "lam_w", [128, 8], F32)
        nlam = sbt(st, "nlam", [128, depth], F32)
        dgn_t = sbt(st, "dgn_t", [128, depth], F32)
        sgn_t = sbt(st, "sgn_t", [128, depth], F32)
        lnp_t = sbt(st, "lnp_t", [128, depth * 4 * NCH], F32)
        cvw_t = sbt(st, "cvw_t", [128, depth * 6 * NFF], F32)
        cvb_t = sbt(st, "cvb_t", [128, depth * 2 * NFF], F32)

        def aff(buf, val, pattern, cmp, fill, base, cm):
            P.op('pool', lambda e: e.memset(buf.t[:], val), wr=[buf])
            P.op('pool', lambda e: e.affine_select(out=buf.t[:], in_=buf.t[:], pattern=pattern, compare_op=cmp,
                                                    fill=fill, base=base, channel_multiplier=cm), wr=[buf])

        def cvt(dst, src):
            P.op('dve', lambda e: e.tensor_copy(out=dst.t[:], in_=src.t[:]), rd=[src], wr=[dst])

        aff(cst, 1.0, [[-1, 128]], ALU.is_equal, 0.0, 0, 1)
        cvt(ident, cst)
        aff(cst, 0.0, [[1, 128]], ALU.is_ge, NEG, 0, -1)
        cvt(maskD, cst)
        aff(cst, 0.0, [[1, 128]], ALU.is_ge, NEG, -1, -1)
        cvt(maskS, cst)
        aff(cst, 1.0, [[-1, 128]], ALU.is_ge, 0.0, 0, 1)
        cvt(uinc, cst)
        aff(cst, 1.0, [[1, 128]], ALU.is_ge, 0.0, -1, -1)
        cvt(lstr, cst)
        P.op('pool', lambda e: e.memset(cst.t[:], 1.0), wr=[cst])
        cvt(ones_b, cst)
        ones_f = sbt(st, "ones_f", [128, 128], F32)
        P.op('pool', lambda e: e.memset(ones_f.t[:], 1.0), wr=[ones_f])
        P.op('pool', lambda e: e.memset(om128.t[:], 1.0 / 128), wr=[om128])
        P.op('pool', lambda e: e.memset(om1024.t[:], 1.0 / 1024), wr=[om1024])
        P.op('pool', lambda e: e.memset(om64.t[:], 0.0), wr=[om64])
        P.op('pool', lambda e: e.memset(om64.t[0:64, 0:64], 1.0 / 64), wr=[om64])
        P.op('pool', lambda e: e.memset(om64.t[64:128, 64:128], 1.0 / 64), wr=[om64])

        for l in range(depth):
            P.op('sp', lambda e, l=l: e.dma_start(out=lam_t.t[:, l * 256:(l + 1) * 256], in_=lamv[l]), wr=[lam_t], dma=True)
        P.op('sp', lambda e: e.dma_start(out=dgn_t.t[:], in_=dgn), wr=[dgn_t], dma=True)
        P.op('sp', lambda e: e.dma_start(out=sgn_t.t[:], in_=sgn), wr=[sgn_t], dma=True)
        P.op('sp', lambda e: e.dma_start(out=lnp_t.t[:], in_=lnp), wr=[lnp_t], dma=True)
        P.op('sp', lambda e: e.dma_start(out=cvw_t.t[:], in_=cvw), wr=[cvw_t], dma=True)
        P.op('sp', lambda e: e.dma_start(out=cvb_t.t[:], in_=cvb), wr=[cvb_t], dma=True)
        for l in range(depth):
            lam_init = 0.8 - 0.6 * math.exp(-0.3 * l)
            b0 = l * 256
            for i in range(2):
                q = lam_t.t[:, b0 + i * 128: b0 + i * 128 + 64]
                k = lam_t.t[:, b0 + i * 128 + 64: b0 + i * 128 + 128]
                P.op('dve', lambda e, q=q, k=k: e.tensor_tensor(out=q, in0=q, in1=k, op=ALU.mult), wr=[lam_t])
                P.op('dve', lambda e, q=q, i=i: e.reduce_sum(out=lam_w.t[:, i:i + 1], in_=q, axis=AX.X), rd=[lam_t], wr=[lam_w])
            P.op('act', lambda e: e.activation(out=lam_w.t[:, 2:4], in_=lam_w.t[:, 0:2], func=AF.Exp), wr=[lam_w])
            P.op('dve', lambda e: e.tensor_tensor(out=lam_w.t[:, 4:5], in0=lam_w.t[:, 3:4], in1=lam_w.t[:, 2:3], op=ALU.subtract), wr=[lam_w])
            P.op('dve', lambda e, l=l, li=lam_init: e.tensor_scalar(out=nlam.t[:, l:l + 1], in0=lam_w.t[:, 4:5], scalar1=-li, scalar2=None, op0=ALU.add),
                 rd=[lam_w], wr=[nlam])
            P.op('dve', lambda e, l=l, li=lam_init: e.tensor_scalar(out=dgn_t.t[:, l:l + 1], in0=dgn_t.t[:, l:l + 1], scalar1=1.0 - li, scalar2=None, op0=ALU.mult),
                 wr=[dgn_t])

        bank_rr = [0]

        def next_bank():
            b = banks[bank_rr[0] % 8]
            bank_rr[0] += 1
            return b

        def load_weight(ph, dst, src_view, nk, ncols, stg):
            CH = 2048
            i = 0
            for kc in range(nk):
                for c0 in range(0, ncols, CH):
                    cw = min(CH, ncols - c0)
                    sg = stg[i % len(stg)]
                    P.op('sp', lambda e, sg=sg, kc=kc, c0=c0, cw=cw: e.dma_start(out=sg.t[:, 0:cw], in_=src_view[:, kc, c0:c0 + cw]),
                         wr=[sg], dma=True)
                    eng = ('pool', 'dve')[i % 2]
                    P.op(eng, lambda e, sg=sg, kc=kc, c0=c0, cw=cw: e.tensor_copy(out=dst.t[:, kc, c0:c0 + cw], in_=sg.t[:, 0:cw]),
                         rd=[sg], wr=[dst])
                    i += 1

        def layer_norm(ph, y, N, gcol, bcol, sqr, msb, rsb, tmp, off=0):
            b1 = next_bank()
            b2 = next_bank()
            for oc in range(NCH):
                sq = sqr[oc % len(sqr)]
                P.op('act', lambda e, sq=sq, oc=oc: e.activation(out=sq.t[:, 0:N], in_=y.t[:, oc, off:off + N], func=AF.Square), rd=[y], wr=[sq])
                P.op('pe', lambda e, oc=oc: e.matmul(b1.t[:, 0:N], lhsT=om1024.t[:], rhs=y.t[:, oc, off:off + N], start=(oc == 0), stop=(oc == NCH - 1)),
                     rd=[y, om1024], wr=[b1])
                P.op('pe', lambda e, sq=sq, oc=oc: e.matmul(b2.t[:, 0:N], lhsT=om1024.t[:], rhs=sq.t[:, 0:N], start=(oc == 0), stop=(oc == NCH - 1)),
                     rd=[sq, om1024], wr=[b2])
            P.op('act', lambda e: e.activation(out=msb.t[:, 0:N], in_=b1.t[:, 0:N], func=AF.Copy), rd=[b1], wr=[msb])
            P.op('act', lambda e: e.activation(out=rsb.t[:, 0:N], in_=b1.t[:, 0:N], func=AF.Square), rd=[b1], wr=[rsb])
            P.op('dve', lambda e: e.tensor_tensor(out=rsb.t[:, 0:N], in0=b2.t[:, 0:N], in1=rsb.t[:, 0:N], op=ALU.subtract), rd=[b2], wr=[rsb])
            P.op('act', lambda e: e.activation(out=rsb.t[:, 0:N], in_=rsb.t[:, 0:N], func=AF.Ln, bias=LN_EPS, scale=1.0), wr=[rsb])
            P.op('act', lambda e: e.activation(out=rsb.t[:, 0:N], in_=rsb.t[:, 0:N], func=AF.Exp, scale=-0.5), wr=[rsb])
            for oc in range(NCH):
                tm = tmp[oc % len(tmp)]
                P.op('pool', lambda e, tm=tm, oc=oc: e.tensor_tensor(out=tm.t[:, 0:N], in0=y.t[:, oc, off:off + N], in1=msb.t[:, 0:N], op=ALU.subtract),
                     rd=[y, msb], wr=[tm])
                P.op('pool', lambda e, tm=tm: e.tensor_tensor(out=tm.t[:, 0:N], in0=tm.t[:, 0:N], in1=rsb.t[:, 0:N], op=ALU.mult), rd=[rsb], wr=[tm])
                P.op('dve', lambda e, tm=tm, oc=oc: e.tensor_scalar(out=y.t[:, oc, off:off + N], in0=tm.t[:, 0:N], scalar1=gcol(oc), scalar2=bcol(oc),
                                                                 op0=ALU.mult, op1=ALU.add), rd=[tm, lnp_t], wr=[y])

        xview = lambda ap: ap.rearrange("(c p) s -> p c s", p=128)

        nl = depth if nlayers_run is None else nlayers_run
        def run_layer(l, x_in, x_out):

            P.barrier()
            with ExitStack() as ph:
              if 'A' not in skip:
                wA = sbt(ph, "wA", [128, NCH, 4096], BF16)
                stg = [sbt(ph, f"stgA{i}", [128, 2048], F32) for i in range(2)]
                xf = [sbt(ph, f"xfA{i}", [128, NCH, 512], F32) for i in range(1)]
                xb = [sbt(ph, f"xbA{i}", [128, NCH, 512], BF16) for i in range(2)]
                cs = [sbt(ph, f"csA{i}", [128, 512], F32) for i in range(2)]
                sn = [sbt(ph, f"snA{i}", [128, 512], F32) for i in range(2)]
                t1 = [sbt(ph, f"t1A{i}", [128, 512], F32) for i in range(3)]
                t2 = [sbt(ph, f"t2A{i}", [128, 512], F32) for i in range(3)]
                ob = [sbt(ph, f"obA{i}", [128, 512], BF16) for i in range(4)]
                load_weight(ph, wA, xview(w_in[l]), NCH, 4096, stg)
                oi = [0]
                for j in range(NT):
                    t0 = j * 512
                    xfj = xf[0]
                    xbj = xb[j % 2]
                    csj = cs[j % 2]
                    snj = sn[j % 2]
                    P.op('sp', lambda e, xfj=xfj, t0=t0: e.dma_start(out=xfj.t[:, :, :], in_=xview(x_in)[:, :, t0:t0 + 512]), wr=[xfj], dma=True)
                    P.op('sp', lambda e, csj=csj, t0=t0: e.dma_start(out=csj.t[:], in_=cos2[:, t0:t0 + 512]), wr=[csj], dma=True)
                    P.op('sp', lambda e, snj=snj, t0=t0: e.dma_start(out=snj.t[:], in_=sin2[:, t0:t0 + 512]), wr=[snj], dma=True)
                    for kc in range(NCH):
                        eng = ('pool', 'dve')[kc % 2]
                        P.op(eng, lambda e, kc=kc, xfj=xfj, xbj=xbj: e.tensor_copy(out=xbj.t[:, kc, :], in_=xfj.t[:, kc, :]), rd=[xfj], wr=[xbj])
                    for c in range(8):
                        b1 = next_bank()
                        b2 = next_bank()
                        for kc in range(NCH):
                            P.op('pe', lambda e, kc=kc, c=c, b1=b1, xbj=xbj: e.matmul(b1.t[:, :], lhsT=wA.t[:, kc, 128 * c:128 * c + 128], rhs=xbj.t[:, kc, :],
                                                                                start=(kc == 0), stop=(kc == NCH - 1)), rd=[wA, xbj], wr=[b1])
                        for kc in range(NCH):
                            P.op('pe', lambda e, kc=kc, c=c, b2=b2, xbj=xbj: e.matmul(b2.t[:, :], lhsT=wA.t[:, kc, 3072 + 128 * c:3072 + 128 * c + 128], rhs=xbj.t[:, kc, :],
                                                                                start=(kc == 0), stop=(kc == NCH - 1)), rd=[wA, xbj], wr=[b2])
                        a1 = t1[oi[0] % 3]
                        a2 = t2[oi[0] % 3]
                        o = ob[oi[0] % 4]
                        oi[0] += 1
                        P.op('dve', lambda e, a1=a1, b1=b1, csj=csj: e.tensor_tensor(out=a1.t[:], in0=b1.t[:, :], in1=csj.t[:], op=ALU.mult), rd=[b1, csj], wr=[a1])
                        P.op('dve', lambda e, a2=a2, b2=b2, snj=snj: e.tensor_tensor(out=a2.t[:], in0=b2.t[:, :], in1=snj.t[:], op=ALU.mult), rd=[b2, snj], wr=[a2])
                        P.op('pool', lambda e, a1=a1, a2=a2, o=o: e.tensor_tensor(out=o.t[:], in0=a1.t[:], in1=a2.t[:], op=ALU.add), rd=[a1, a2], wr=[o])
                        P.op('pool', lambda e, o=o, c=c, t0=t0: e.dma_start(out=qk_s[128 * c:128 * c + 128, t0:t0 + 512], in_=o.t[:]), rd=[o], dma=True)
                    for c in range(8):
                        b1 = next_bank()
                        for kc in range(NCH):
                            P.op('pe', lambda e, kc=kc, c=c, b1=b1, xbj=xbj: e.matmul(b1.t[:, :], lhsT=wA.t[:, kc, 1536 + 128 * c:1536 + 128 * c + 128], rhs=xbj.t[:, kc, :],
                                                                                start=(kc == 0), stop=(kc == NCH - 1)), rd=[wA, xbj], wr=[b1])
                        o = ob[oi[0] % 4]
                        oi[0] += 1
                        P.op('act', lambda e, o=o, b1=b1: e.activation(out=o.t[:], in_=b1.t[:, :], func=AF.Copy), rd=[b1], wr=[o])
                        P.op('pool', lambda e, o=o, c=c, t0=t0: e.dma_start(out=qk_s[1024 + 128 * c:1024 + 128 * c + 128, t0:t0 + 512], in_=o.t[:]), rd=[o], dma=True)
                    for tb in range(4):
                        for half, wc0 in ((0, 1024), (1, 2560)):
                            b1 = next_bank()
                            for kc in range(NCH):
                                P.op('pe', lambda e, kc=kc, b1=b1, xbj=xbj, tb=tb, wc0=wc0: e.matmul(b1.t[:, :], lhsT=xbj.t[:, kc, 128 * tb:128 * tb + 128], rhs=wA.t[:, kc, wc0:wc0 + 512],
                                                                                            start=(kc == 0), stop=(kc == NCH - 1)), rd=[wA, xbj], wr=[b1])
                            o = ob[oi[0] % 4]
                            oi[0] += 1
                            P.op('act', lambda e, o=o, b1=b1: e.activation(out=o.t[:], in_=b1.t[:, :], func=AF.Copy), rd=[b1], wr=[o])
                            P.op('pool', lambda e, o=o, tb=tb, half=half, t0=t0: e.dma_start(out=v_s[t0 + 128 * tb:t0 + 128 * tb + 128, 512 * half:512 * half + 512], in_=o.t[:]),
                                 rd=[o], dma=True)

            P.barrier()
            with ExitStack() as ph:
                vview = v_s.rearrange("(kb p) f -> p kb f", p=128)
                sets = []
                for si in range(2):
                    kp = [sbt(ph, "KTa", [128, S], BF16), sbt(ph, "KTb", [128, S], BF16)]
                    P.op('pool', lambda e, kp=kp: e.memset(kp[0].t[64:128, :], 0.0), wr=[kp[0]])
                    P.op('pool', lambda e, kp=kp: e.memset(kp[1].t[0:64, :], 0.0), wr=[kp[1]])
                    sets.append((kp, sbt(ph, "QT", [128, S], BF16), sbt(ph, "VV", [128, NKB, 128], BF16)))
                units = ([('d', h) for h in range(0 if 'Bd' in skip else 4)] + [('s', pp) for pp in range(0 if 'Bs' in skip else 4)])

                def load_unit(ui):
                    kind, idx = units[ui]
                    kp, qt, vv = sets[ui % 2]
                    if kind == 'd':
                        kr, qr, vc = 512 + 128 * idx, 128 * idx, 128 * idx
                    else:
                        kr, qr, vc = 1536 + 128 * idx, 1024 + 128 * idx, 512 + 128 * idx
                    P.op('sp', lambda e: e.dma_start(out=kp[0].t[0:64, :], in_=qk_s[kr:kr + 64, :]), wr=[kp[0]], dma=True)
                    P.op('sp', lambda e: e.dma_start(out=kp[1].t[64:128, :], in_=qk_s[kr + 64:kr + 128, :]), wr=[kp[1]], dma=True)
                    P.op('sp', lambda e: e.dma_start(out=qt.t[:], in_=qk_s[qr:qr + 128, :]), wr=[qt], dma=True)
                    for q4 in range(0, NKB, VCH):
                        P.op('sp', lambda e, q4=q4: e.dma_start(out=vv.t[:, q4:q4 + VCH, :], in_=vview[:, q4:q4 + VCH, vc:vc + 128]), wr=[vv], dma=True)
                Er = [sbt(ph, f"Er{i}", [128, 512], F32) for i in range(4)]
                SPr = [sbt(ph, f"SPr{i}", [128, 512], F16) for i in range(4)]
                Fr = [sbt(ph, f"Fr{i}", [128, 512], F32) for i in range(3)]
                Ar = [sbt(ph, f"Ar{i}", [128, 512], BF16) for i in range(4)]
                fa = [sbt(ph, f"fa{i}", [128, 512], F32) for i in range(4)]
                fo = [sbt(ph, f"fo{i}", [128, 512], BF16) for i in range(2)]
                fcnt = [0]

                def rms_finish(o_sb, omat, gcolap, eps, row0, t0):
                    sq = fa[3]
                    rb = banks[7]
                    ofin = fo[fcnt[0] % 2]
                    fcnt[0] += 1
                    P.op('act', lambda e: e.activation(out=sq.t[:], in_=o_sb.t[:], func=AF.Square), rd=[o_sb], wr=[sq])
                    P.op('pe', lambda e: e.matmul(rb.t[:, :], lhsT=omat.t[:], rhs=sq.t[:], start=True, stop=True), rd=[sq, omat], wr=[rb])
                    P.op('act', lambda e: e.activation(out=sq.t[:], in_=rb.t[:, :], func=AF.Ln, bias=eps, scale=1.0), rd=[rb], wr=[sq])
                    P.op('act', lambda e: e.activation(out=sq.t[:], in_=sq.t[:], func=AF.Exp, scale=-0.5), wr=[sq])
                    P.op('dve', lambda e: e.tensor_tensor(out=o_sb.t[:], in0=o_sb.t[:], in1=sq.t[:], op=ALU.mult), rd=[sq], wr=[o_sb])
                    P.op('dve', lambda e: e.tensor_scalar(out=ofin.t[:], in0=o_sb.t[:], scalar1=gcolap, scalar2=None, op0=ALU.mult), rd=[o_sb, dgn_t, sgn_t], wr=[ofin])
                    P.op('pool', lambda e: e.dma_start(out=mix_s[row0:row0 + 128, t0:t0 + 512], in_=ofin.t[:]), rd=[ofin], dma=True)

                if units:
                    load_unit(0)
                for ui, (kind, h) in enumerate(units):
                    if kind != 'd':
                        continue
                    if ui + 1 < len(units):
                        load_unit(ui + 1)
                    KTp, QT, VV = sets[ui % 2]
                    steps = [(j, c, kb) for j in range(NT) for c in range(2) for kb in range(4 * j + 4)]
                    pz_of = {}

                    def d_qk(i):
                        j, c, kb = steps[i]
                        pz = banks[6 + i % 2]
                        pz_of[i] = pz
                        KT = KTp[c]
                        QTl = QT
                        kT = KT.t[:, kb * 128:(kb + 1) * 128]
                        q0 = j * 512
                        if kb >= 4 * j:
                            c0 = 128 * (kb - 4 * j)
                            P.op('pe', lambda e: e.matmul(pz.t[:, c0:512], lhsT=kT, rhs=QTl.t[:, q0 + c0:q0 + 512], start=True, stop=False), rd=[KT, QT], wr=[pz])
                            P.op('pe', lambda e: e.matmul(pz.t[:, c0:c0 + 128], lhsT=ident.t[:], rhs=maskD.t[:], start=False, stop=True), rd=[ident, maskD], wr=[pz])
                        else:
                            P.op('pe', lambda e: e.matmul(pz.t[:, :], lhsT=kT, rhs=QTl.t[:, q0:q0 + 512], start=True, stop=True), rd=[KT, QT], wr=[pz])

                    def d_exp(i):
                        j, c, kb = steps[i]
                        pz = pz_of[i]
                        c0 = max(0, 128 * (kb - 4 * j))
                        A = Ar[i % 4]
                        P.op('act', lambda e: e.activation(out=A.t[:, c0:512], in_=pz.t[:, c0:512], func=AF.Exp, scale=0.125), rd=[pz], wr=[A])

                    def d_pv(i):
                        j, c, kb = steps[i]
                        c0 = max(0, 128 * (kb - 4 * j))
                        A = Ar[i % 4]
                        pr = (2 * j + c) % 3
                        po = banks[2 * pr]
                        pl = banks[2 * pr + 1]
                        stt = (kb == 0)
                        VVl = VV
                        P.op('pe', lambda e: e.matmul(po.t[:, c0:512], lhsT=VVl.t[:, kb, :], rhs=A.t[:, c0:512], start=stt, stop=True, skip_group_check=True), rd=[VV, A], wr=[po])
                        P.op('pe', lambda e: e.matmul(pl.t[:, c0:512], lhsT=ones_b.t[:], rhs=A.t[:, c0:512], start=stt, stop=True, skip_group_check=True), rd=[ones_b, A], wr=[pl])
                        if c == 1 and kb == 4 * j + 3:
                            p0 = (2 * j) % 3
                            po0, pl0 = banks[2 * p0], banks[2 * p0 + 1]
                            r0, o0, r1 = fa[0], fa[1], fa[2]
                            P.op('dve', lambda e: e.reciprocal(out=r0.t[:], in_=pl0.t[:, :]), rd=[pl0], wr=[r0])
                            P.op('dve', lambda e: e.tensor_tensor(out=o0.t[:], in0=po0.t[:, :], in1=r0.t[:], op=ALU.mult), rd=[po0, r0], wr=[o0])
                            P.op('dve', lambda e: e.reciprocal(out=r1.t[:], in_=pl.t[:, :]), rd=[pl], wr=[r1])
                            P.op('dve', lambda e: e.tensor_tensor(out=r1.t[:], in0=po.t[:, :], in1=r1.t[:], op=ALU.mult), rd=[po], wr=[r1])
                            P.op('dve', lambda e: e.scalar_tensor_tensor(out=o0.t[:], in0=r1.t[:], scalar=nlam.t[:, l:l + 1], in1=o0.t[:], op0=ALU.mult, op1=ALU.add),
                                 rd=[r1, nlam], wr=[o0])
                            rms_finish(o0, om128, dgn_t.t[:, l:l + 1], RMS_EPS, 128 * h, j * 512)

                    n = len(steps)
                    for t in range(n + 2):
                        if t < n:
                            d_qk(t)
                            d_exp(t)
                        if t >= 2:
                            d_pv(t - 2)

                for ui, (kind, pp) in enumerate(units):
                    if kind != 's':
                        continue
                    if ui + 1 < len(units):
                        load_unit(ui + 1)
                    KTp, QT, VV = sets[ui % 2]
                    steps = [(j, kb, hh) for j in range(NT) for kb in range(4 * j + 3, -1, -1) for hh in range(2)]
                    pz_of = {}

                    def geom(i):
                        j, kb, hh = steps[i]
                        diag = kb >= 4 * j
                        c0 = 128 * (kb - 4 * j) if diag else 0
                        return j, kb, hh, diag, c0

                    def s_qk(i):
                        j, kb, hh, diag, c0 = geom(i)
                        pz = banks[4 + i % 3]
                        pz_of[i] = pz
                        KT = KTp[hh]
                        QTl = QT
                        kT = KT.t[:, kb * 128:(kb + 1) * 128]
                        q0 = j * 512
                        if diag:
                            P.op('pe', lambda e: e.matmul(pz.t[:, c0:512], lhsT=kT, rhs=QTl.t[:, q0 + c0:q0 + 512], start=True, stop=False), rd=[KT, QT], wr=[pz])
                            P.op('pe', lambda e: e.matmul(pz.t[:, c0:c0 + 128], lhsT=ident.t[:], rhs=maskS.t[:], start=False, stop=True), rd=[ident, maskS], wr=[pz])
                        else:
                            P.op('pe', lambda e: e.matmul(pz.t[:, :], lhsT=kT, rhs=QTl.t[:, q0:q0 + 512], start=True, stop=True), rd=[KT, QT], wr=[pz])

                    def s_act1(i):
                        j, kb, hh, diag, c0 = geom(i)
                        pz = pz_of[i]
                        E = Er[i % 4]
                        SP = SPr[i % 4]
                        P.op('act', lambda e: e.activation(out=E.t[:, c0:512], in_=pz.t[:, c0:512], func=AF.Exp, scale=0.125), rd=[pz], wr=[E])

                    def s_act1b(i):
                        j, kb, hh, diag, c0 = geom(i)
                        E = Er[i % 4]
                        SP = SPr[i % 4]
                        P.op('act', lambda e: e.activation(out=SP.t[:, c0:512], in_=E.t[:, c0:512], func=AF.Ln, bias=1.0, scale=1.0), rd=[E], wr=[SP])

                    def s_u(i):
                        j, kb, hh, diag, c0 = geom(i)
                        SP = SPr[i % 4]
                        X = banks[hh]
                        if kb == 4 * j + 3:
                            P.op('dve', lambda e: e.memset(X.t[:, :], 0.0), wr=[X])
                        P.op('pe', lambda e: e.matmul(X.t[:, c0:512], lhsT=uinc.t[:], rhs=SP.t[:, c0:512], start=False, stop=True, skip_group_check=True), rd=[uinc, SP], wr=[X])

                    def s_act2(i):
                        j, kb, hh, diag, c0 = geom(i)
                        X = banks[hh]
                        Fb = Fr[i % 3]
                        P.op('act', lambda e: e.activation(out=Fb.t[:, c0:512], in_=X.t[:, c0:512], func=AF.Exp, scale=-1.0), rd=[X], wr=[Fb])

                    def s_mul(i):
                        j, kb, hh, diag, c0 = geom(i)
                        E = Er[i % 4]
                        Fb = Fr[i % 3]
                        A = Ar[i % 4]
                        P.op('dve', lambda e: e.tensor_tensor(out=A.t[:, c0:512], in0=E.t[:, c0:512], in1=Fb.t[:, c0:512], op=ALU.mult), rd=[E, Fb], wr=[A])

                    def s_lpv(i):
                        j, kb, hh, diag, c0 = geom(i)
                        SP = SPr[i % 4]
                        A = Ar[i % 4]
                        X = banks[hh]
                        po = banks[2 + hh]
                        VVl = VV
                        if kb > 0:
                            P.op('pe', lambda e: e.matmul(X.t[:, c0:512], lhsT=lstr.t[:], rhs=SP.t[:, c0:512], start=False, stop=True, skip_group_check=True), rd=[lstr, SP], wr=[X])
                        if kb == 4 * j + 3:
                            P.op('dve', lambda e: e.memset(po.t[:, :], 0.0), wr=[po])
                        P.op('pe', lambda e: e.matmul(po.t[:, c0:512], lhsT=VVl.t[:, kb, :], rhs=A.t[:, c0:512], start=False, stop=True, skip_group_check=True), rd=[VV, A], wr=[po])
                        if kb == 0 and hh == 1:
                            o0 = fa[0]
                            pa, pb = banks[2], banks[3]
                            P.op('act', lambda e: e.activation(out=o0.t[0:64, :], in_=pa.t[0:64, :], func=AF.Copy), rd=[pa], wr=[o0])
                            P.op('dve', lambda e: e.tensor_copy(out=o0.t[64:128, :], in_=pb.t[64:128, :]), rd=[pb], wr=[o0])
                            rms_finish(o0, om64, sgn_t.t[:, l:l + 1], RMS_EPS, 512 + 128 * pp, j * 512)

                    n = len(steps)
                    for t in range(n + 2):
                        if t < n:
                            s_qk(t)
                            s_act1(t)
                        if 1 <= t <= n:
                            s_u(t - 1)
                            s_act2(t - 1)
                        if t < n:
                            s_act1b(t)
                        if 1 <= t <= n:
                            s_mul(t - 1)
                        if t >= 2:
                            s_lpv(t - 2)

            P.barrier()
            with ExitStack() as ph:
                wO = sbt(ph, "wO", [128, NCH, D], BF16)
                stg = [sbt(ph, f"stgC{i}", [128, 2048], F32) for i in range(2)]
                mx = [sbt(ph, f"mxC{i}", [128, NCH, 512], BF16) for i in range(2)]
                yy = [sbt(ph, f"yyC{i}", [128, NCH, 512], F32) for i in range(2)]
                sqr = [sbt(ph, f"sqC{i}", [128, 512], F32) for i in range(2)]
                tmp = [sbt(ph, f"tmC{i}", [128, 512], F32) for i in range(2)]
                msb = sbt(ph, "msbC", [128, 512], F32)
                rsb = sbt(ph, "rsbC", [128, 512], F32)
                load_weight(ph, wO, xview(w_out[l]), NCH, D, stg)
                g0 = l * 4 * NCH
                for j in range(0 if 'C1' in skip else NT):
                    t0 = j * 512
                    m = mx[j % 2]
                    y = yy[j % 2]
                    P.op('sp', lambda e, m=m, t0=t0: e.dma_start(out=m.t[:, :, :], in_=xview(mix_s)[:, :, t0:t0 + 512]), wr=[m], dma=True)
                    P.op('sp', lambda e, y=y, t0=t0: e.dma_start(out=y.t[:, :, :], in_=xview(x_in)[:, :, t0:t0 + 512]), wr=[y], dma=True)
                    for oc in range(NCH):
                        b1 = next_bank()
                        for kc in range(NCH):
                            P.op('pe', lambda e, kc=kc, oc=oc, b1=b1, m=m: e.matmul(b1.t[:, :], lhsT=wO.t[:, kc, 128 * oc:128 * oc + 128], rhs=m.t[:, kc, :],
                                                                               start=(kc == 0), stop=(kc == NCH - 1)), rd=[wO, m], wr=[b1])
                        P.op('dve', lambda e, oc=oc, b1=b1, y=y: e.scalar_tensor_tensor(out=y.t[:, oc, :], in0=y.t[:, oc, :], scalar=ALPHA, in1=b1.t[:, :],
                                                                                   op0=ALU.mult, op1=ALU.add), rd=[b1], wr=[y])
                    layer_norm(ph, y, 512, lambda oc, g0=g0: lnp_t.t[:, g0 + oc:g0 + oc + 1], lambda oc, g0=g0: lnp_t.t[:, g0 + NCH + oc:g0 + NCH + oc + 1],
                               sqr, msb, rsb, tmp)
                    P.op('pool', lambda e, y=y, t0=t0: e.dma_start(out=xview(x1_s)[:, :, t0:t0 + 512], in_=y.t[:, :, :]), rd=[y], dma=True)

            P.barrier()
            with ExitStack() as ph:
                NF = 256
                wU = sbt(ph, "wU", [128, NCH, 2 * DFF], BF16)
                wD = sbt(ph, "wD", [128, NFF, D], BF16)
                stg = [sbt(ph, f"stgF{i}", [128, 2048], F32) for i in range(2)]
                xh = [sbt(ph, f"xhF{i}", [128, NCH, NF + 2], F32) for i in range(2)]
                xbh = [sbt(ph, f"xbF{i}", [128, NCH, NF + 2], BF16) for i in range(2)]
                gT = sbt(ph, "gT", [128, NFF, NF], BF16)
                ua = [sbt(ph, f"uaF{i}", [128, NF], F32) for i in range(2)]
                ub = [sbt(ph, f"ubF{i}", [128, NF], F32) for i in range(2)]
                sa = [sbt(ph, f"saF{i}", [128, NF], F32) for i in range(2)]
                sqr = [sbt(ph, f"sqF{i}", [128, NF], F32) for i in range(2)]
                tmp = [sbt(ph, f"tmF{i}", [128, NF], F32) for i in range(2)]
                msb = sbt(ph, "msbF", [128, NF], F32)
                rsb = sbt(ph, "rsbF", [128, NF], F32)
                load_weight(ph, wU, xview(w_up[l]), NCH, 2 * DFF, stg)
                load_weight(ph, wD, xview(w_down[l]), NFF, D, stg)
                g0 = l * 4 * NCH + 2 * NCH
                cw0 = l * 6 * NFF
                cb0 = l * 2 * NFF
                ui = [0]
                for j in range(0 if 'C2' in skip else S // NF):
                    t0 = j * NF
                    x1 = xh[j % 2]
                    x1b = xbh[j % 2]
                    if j == 0:
                        P.op('pool', lambda e, x1=x1: e.memset(x1.t[:, :, 0:2], 0.0), wr=[x1])
                        P.op('sp', lambda e, x1=x1: e.dma_start(out=x1.t[:, :, 2:NF + 2], in_=xview(x1_s)[:, :, 0:NF]), wr=[x1], dma=True)
                    else:
                        P.op('sp', lambda e, x1=x1, t0=t0: e.dma_start(out=x1.t[:, :, :], in_=xview(x1_s)[:, :, t0 - 2:t0 + NF]), wr=[x1], dma=True)
                    for kc in range(NCH):
                        eng = ('pool', 'dve')[kc % 2]
                        P.op(eng, lambda e, kc=kc, x1=x1, x1b=x1b: e.tensor_copy(out=x1b.t[:, kc, :], in_=x1.t[:, kc, :]), rd=[x1], wr=[x1b])
                    for i in range(NFF):
                        res = []
                        for half in range(2):
                            ch = half * NFF + i
                            b1 = next_bank()
                            for kc in range(NCH):
                                P.op('pe', lambda e, kc=kc, ch=ch, b1=b1, x1b=x1b: e.matmul(b1.t[:, 0:NF + 2], lhsT=wU.t[:, kc, 128 * ch:128 * ch + 128], rhs=x1b.t[:, kc, :],
                                                                                       start=(kc == 0), stop=(kc == NCH - 1)), rd=[wU, x1b], wr=[b1])
                            u = (ua, ub)[half][ui[0] % 2]
                            w = lambda tap, ch=ch: cvw_t.t[:, cw0 + tap * 2 * NFF + ch:cw0 + tap * 2 * NFF + ch + 1]
                            bcol = cvb_t.t[:, cb0 + ch:cb0 + ch + 1]
                            P.op('act', lambda e, u=u, b1=b1, w=w, bcol=bcol: e.activation(out=u.t[:], in_=b1.t[:, 2:NF + 2], func=AF.Identity, bias=bcol, scale=w(2)),
                                 rd=[b1, cvw_t, cvb_t], wr=[u])
                            P.op('dve', lambda e, u=u, b1=b1, w=w: e.scalar_tensor_tensor(out=u.t[:], in0=b1.t[:, 1:NF + 1], scalar=w(1), in1=u.t[:], op0=ALU.mult, op1=ALU.add),
                                 rd=[b1, cvw_t], wr=[u])
                            P.op('dve', lambda e, u=u, b1=b1, w=w: e.scalar_tensor_tensor(out=u.t[:], in0=b1.t[:, 0:NF], scalar=w(0), in1=u.t[:], op0=ALU.mult, op1=ALU.add),
                                 rd=[b1, cvw_t], wr=[u])
                            res.append(u)
                        s = sa[ui[0] % 2]
                        ui[0] += 1
                        P.op('act', lambda e, s=s, u=res[0]: e.activation(out=s.t[:], in_=u.t[:], func=AF.Silu), rd=[res[0]], wr=[s])
                        P.op('pool', lambda e, s=s, u=res[1], i=i: e.tensor_tensor(out=gT.t[:, i, :], in0=s.t[:], in1=u.t[:], op=ALU.mult), rd=[s, res[1]], wr=[gT])
                    for oc in range(NCH):
                        b1 = next_bank()
                        for i in range(NFF):
                            P.op('pe', lambda e, i=i, oc=oc, b1=b1: e.matmul(b1.t[:, 0:NF], lhsT=wD.t[:, i, 128 * oc:128 * oc + 128], rhs=gT.t[:, i, :],
                                                                        start=(i == 0), stop=(i == NFF - 1)), rd=[wD, gT], wr=[b1])
                        P.op('dve', lambda e, oc=oc, b1=b1, x1=x1: e.scalar_tensor_tensor(out=x1.t[:, oc, 2:NF + 2], in0=x1.t[:, oc, 2:NF + 2], scalar=ALPHA, in1=b1.t[:, 0:NF],
                                                                                     op0=ALU.mult, op1=ALU.add), rd=[b1], wr=[x1])
                    layer_norm(ph, x1, NF, lambda oc, g0=g0: lnp_t.t[:, g0 + oc:g0 + oc + 1], lambda oc, g0=g0: lnp_t.t[:, g0 + NCH + oc:g0 + NCH + oc + 1],
                               sqr, msb, rsb, tmp, off=2)
                    P.op('pool', lambda e, x1=x1, t0=t0: e.dma_start(out=xview(x_out)[:, :, t0:t0 + NF], in_=x1.t[:, :, 2:NF + 2]), rd=[x1], dma=True)

        for l in range(nl):
            run_layer(l, xT if l == 0 else x_s, outT if l == nl - 1 else x_s)
        P.emit()
        nops = P.nops
    return nc, nops


def _host_prep(inp, S):
    depth = DEPTH
    f = np.float32
    w_in = np.asarray(inp["w_in"], f)
    idx = np.arange(1024)
    base = (idx // 64) * 64
    off = idx % 64
    perm = base + (off + 32) % 64
    w_ext = np.concatenate([w_in, w_in[:, :, perm]], axis=2)
    lamv = np.concatenate([inp["lam_q1"], inp["lam_k1"], inp["lam_q2"], inp["lam_k2"]], axis=1).astype(f)
    lamv = np.ascontiguousarray(np.broadcast_to(lamv[:, None, :], (depth, 128, 256)))
    dgn = np.ascontiguousarray(np.asarray(inp["diff_norm_g"], f).T)
    sgn = np.ascontiguousarray(np.tile(np.asarray(inp["sb_norm_g"], f), (1, 2)).T)
    cols = []
    for l in range(depth):
        for nm in ("ln1_g", "ln1_b", "ln2_g", "ln2_b"):
            cols.append(np.asarray(inp[nm][l], f).reshape(NCH, 128).T)
    lnp = np.ascontiguousarray(np.concatenate(cols, axis=1))
    cw = []
    cb = []
    for l in range(depth):
        for tap in range(3):
            cw.append(np.asarray(inp["conv_w"][l, tap], f).reshape(2 * NFF, 128).T)
        cb.append(np.asarray(inp["conv_b"][l], f).reshape(2 * NFF, 128).T)
    cvw = np.ascontiguousarray(np.concatenate(cw, axis=1))
    cvb = np.ascontiguousarray(np.concatenate(cb, axis=1))
    inv = (1.0 / (np.float32(10000.0) ** (np.arange(0, HD, 2, dtype=np.float32) / np.float32(HD)))).astype(f)
    ang = (np.arange(S, dtype=f)[None, :] * inv[:, None]).astype(f)
    c = np.cos(ang).astype(f)
    s = np.sin(ang).astype(f)
    cos2 = np.ascontiguousarray(np.tile(c, (4, 1)))
    sin2 = np.ascontiguousarray(np.concatenate([-s, s, -s, s], axis=0))
    shared = dict(w_in=np.ascontiguousarray(w_ext), w_out=np.asarray(inp["w_out"], f), w_up=np.asarray(inp["w_up"], f),
                  w_down=np.asarray(inp["w_down"], f), lamv=lamv, dgn=dgn, sgn=sgn, lnp=lnp, cvw=cvw, cvb=cvb,
                  cos2=cos2, sin2=sin2)
    return shared


def kernel(**inputs):
    x = np.asarray(inputs["x"], np.float32)
    B, S, _ = x.shape
    shared = _host_prep(inputs, S)
    nc, _ = build(S)
    in_maps = []
    for c in range(8):
        m = dict(shared)
        m["xT"] = np.ascontiguousarray(x[c % B].T)
        in_maps.append(m)
    res = run_bass_kernel_spmd(nc, in_maps, core_ids=list(range(8)))
    out = np.stack([np.ascontiguousarray(res.results[b]["outT"].T) for b in range(B)], axis=0)
    return out.astype(np.float32)
```
